# Optimizing a Trainium2 kernel written in Bass

```python
import jax, jax.numpy as jnp
from jax import lax
import numpy as np

D_MODEL = 2048
BATCH = 2
SEQ = 4096
DEPTH = 1

CHUNK = 64
GLA_HEADS = 4
GLA_DK = D_MODEL // 8
GLA_DV = D_MODEL // 4
GLA_GATE_RANK = 16
GLA_TAU = 16.0
GMLP_BLOCK = 128
GMLP_GROUPS = 8
GMLP_WIDTH = D_MODEL
GMLP_DG = GMLP_WIDTH // GMLP_GROUPS
N_BRANCH = 2
D_FF = 4 * D_MODEL
EPS = 1e-6
QK_W = GLA_HEADS * GLA_DK
V_W = GLA_HEADS * GLA_DV
SPLIT_SIZES = (QK_W, QK_W, V_W, V_W, GLA_GATE_RANK, GMLP_WIDTH, GMLP_WIDTH, N_BRANCH * D_MODEL)
D_IN = QK_W + QK_W + V_W + V_W + GLA_GATE_RANK + GMLP_WIDTH + GMLP_WIDTH + N_BRANCH * D_MODEL

kernel_name = "hybrid_gla_gmlp_gated_block"


def rmsnorm(x, w):
    xf = x.astype(jnp.float32)
    y = xf * lax.rsqrt(jnp.mean(xf * xf, axis=-1, keepdims=True) + EPS)
    return (y * w.astype(jnp.float32)).astype(x.dtype)


def gla_branch(q, k, v, gate_lr, r, w_alpha_up, b_alpha, gla_norm_w):
    B, S, _ = q.shape
    N = S // CHUNK
    f32 = jnp.float32

    def heads(t, d):
        return t.reshape(B, N, CHUNK, GLA_HEADS, d).transpose(0, 3, 1, 2, 4)

    log_a = jax.nn.log_sigmoid((gate_lr @ w_alpha_up + b_alpha).astype(f32)) / GLA_TAU
    qh = heads(q.astype(f32), GLA_DK) * (GLA_DK ** -0.5)
    kh = heads(k.astype(f32), GLA_DK)
    vh = heads(v.astype(f32), GLA_DV)
    lcum = jnp.cumsum(heads(log_a, GLA_DK), axis=3)
    l_end = lcum[:, :, :, -1:]
    k_dec = kh * jnp.exp(l_end - lcum)
    scores = jnp.einsum('bhncd,bhnsd->bhncs', qh, k_dec)
    o_intra = jnp.einsum('bhncs,bhnse->bhnce', scores, vh)
    kv = jnp.einsum('bhnsd,bhnse->bhnde', k_dec, vh)
    q_inter = qh * jnp.exp(l_end)
    decay = jnp.exp(l_end[:, :, :, 0])

    def step(state, xs):
        q_c, kv_c, d_c = xs
        o = jnp.einsum('bhcd,bhde->bhce', q_c, state)
        state = d_c[..., None] * state + kv_c
        return state, o

    xs = (jnp.moveaxis(q_inter, 2, 0), jnp.moveaxis(kv, 2, 0), jnp.moveaxis(decay, 2, 0))
    state0 = jnp.zeros((B, GLA_HEADS, GLA_DK, GLA_DV), f32)
    _, o_inter = lax.scan(step, state0, xs)
    o = o_intra + jnp.moveaxis(o_inter, 0, 2)
    o = o * lax.rsqrt(jnp.mean(o * o, axis=-1, keepdims=True) + EPS) * gla_norm_w.astype(f32)
    o = o.transpose(0, 2, 3, 1, 4).reshape(B, S, V_W)
    return (o * jax.nn.silu(r.astype(f32))).astype(q.dtype)


def gmlp_branch(u, v, ln_w, ln_b, w_spatial, b_spatial):
    B, S, _ = u.shape
    f32 = jnp.float32
    nb = S // GMLP_BLOCK
    u = jax.nn.gelu(u.astype(f32)).reshape(B, nb, GMLP_BLOCK, GMLP_GROUPS, GMLP_DG)
    v = jax.nn.gelu(v.astype(f32)).reshape(B, nb, GMLP_BLOCK, GMLP_GROUPS, GMLP_DG)
    mu = jnp.mean(v, axis=-1, keepdims=True)
    vc = v - mu
    v = vc * lax.rsqrt(jnp.mean(vc * vc, axis=-1, keepdims=True) + EPS) * ln_w + ln_b
    pos_chunk = jnp.arange(GMLP_BLOCK) // CHUNK
    mask = pos_chunk[:, None] >= pos_chunk[None, :]
    w = jnp.where(mask[None], w_spatial.astype(f32), 0.0)
    mixed = jnp.einsum('gts,bnsgc->bntgc', w, v) + b_spatial.astype(f32).T[None, None, :, :, None]
    return (u * mixed).reshape(B, S, GMLP_WIDTH).astype(u.dtype)


def setup_inputs(seed: int = 0) -> dict:
    key = jax.random.key(seed)
    ks = jax.random.split(key, 20)
    n = jax.random.normal
    L = DEPTH
    f32 = jnp.float32
    return {
        "x": n(ks[0], (BATCH, SEQ, D_MODEL), f32),
        "norm_mix_w": 1.0 + 0.01 * n(ks[1], (L, D_MODEL), f32),
        "w_in": n(ks[2], (L, D_MODEL, D_IN), f32) * D_MODEL ** -0.5,
        "w_alpha_up": n(ks[3], (L, GLA_GATE_RANK, QK_W), f32) * GLA_GATE_RANK ** -0.5,
        "b_alpha": 0.1 * n(ks[4], (L, QK_W), f32),
        "gla_norm_w": 1.0 + 0.01 * n(ks[5], (L, GLA_DV), f32),
        "gmlp_ln_w": 1.0 + 0.01 * n(ks[6], (L, GMLP_DG), f32),
        "gmlp_ln_b": 0.01 * n(ks[7], (L, GMLP_DG), f32),
        "w_spatial": n(ks[8], (L, GMLP_GROUPS, GMLP_BLOCK, GMLP_BLOCK), f32) * GMLP_BLOCK ** -0.5,
        "b_spatial": 1.0 + 0.01 * n(ks[9], (L, GMLP_GROUPS, GMLP_BLOCK), f32),
        "b_gate": 0.01 * n(ks[10], (L, N_BRANCH, D_MODEL), f32),
        "w_branch": n(ks[11], (L, N_BRANCH, V_W, D_MODEL), f32) * V_W ** -0.5,
        "w_out": n(ks[12], (L, D_MODEL, D_MODEL), f32) * D_MODEL ** -0.5,
        "norm_mlp_w": 1.0 + 0.01 * n(ks[13], (L, D_MODEL), f32),
        "w_ff_up": n(ks[14], (L, D_MODEL, D_FF), f32) * D_MODEL ** -0.5,
        "w_ff_down": n(ks[15], (L, D_FF, D_MODEL), f32) * D_FF ** -0.5,
        "norm_final_w": 1.0 + 0.01 * n(ks[16], (D_MODEL,), f32),
    }


def reference(x, norm_mix_w, w_in, w_alpha_up, b_alpha, gla_norm_w, gmlp_ln_w, gmlp_ln_b,
              w_spatial, b_spatial, b_gate, w_branch, w_out, norm_mlp_w, w_ff_up,
              w_ff_down, norm_final_w):
    B, S, _ = x.shape
    split_idx = [int(i) for i in np.cumsum(SPLIT_SIZES)[:-1]]
    h = x
    for l in range(DEPTH):
        xn = rmsnorm(h, norm_mix_w[l])
        proj = xn @ w_in[l]
        q, k, v, r, glr, gu, gv, gates = jnp.split(proj, split_idx, axis=-1)
        o_gla = gla_branch(q, k, v, glr, r, w_alpha_up[l], b_alpha[l], gla_norm_w[l])
        o_gmlp = gmlp_branch(gu, gv, gmlp_ln_w[l], gmlp_ln_b[l], w_spatial[l], b_spatial[l])
        branches = jnp.stack([o_gla, o_gmlp], axis=2)
        branch_d = jnp.einsum('bsnc,ncd->bsnd', branches, w_branch[l])
        g = jax.nn.sigmoid(gates.reshape(B, S, N_BRANCH, D_MODEL) + b_gate[l])
        mixed = jnp.sum(g * branch_d, axis=2)
        h = h + mixed @ w_out[l]
        hn = rmsnorm(h, norm_mlp_w[l])
        h = h + jnp.square(jax.nn.relu(hn @ w_ff_up[l])) @ w_ff_down[l]
    return rmsnorm(h, norm_final_w)
```

```python
import numpy as np
import concourse.bass as bass
import concourse.mybir as mybir
from concourse.bass_utils import run_bass_kernel_spmd

F32 = mybir.dt.float32
BF16 = mybir.dt.bfloat16
AF = mybir.ActivationFunctionType
ALU = mybir.AluOpType

D = 2048
KC = 16
T = 1024
NT = 8
DIN = 14352
DFF = 8192
EPS = 1e-6
NBLK = 1
C_Q, C_K, C_V, C_R, C_GLR, C_GU, C_GV, C_G0, C_G1 = 0, 1024, 2048, 4096, 6144, 6160, 8208, 10256, 12304


class Res:
    __slots__ = ("w", "r")

    def __init__(self):
        self.w = None
        self.r = {}


class FW:
    def __init__(self, nc):
        self.nc = nc
        self.eng = {"pe": nc.tensor, "act": nc.scalar, "dve": nc.vector, "pool": nc.gpsimd, "sp": nc.sync}
        self.sems = {k: nc.alloc_semaphore("s_" + k) for k in self.eng}
        self.cnt = {k: 0 for k in self.eng}
        self.waited = {k: {} for k in self.eng}
        self.dma_cnt = {}
        self.same_engine_sync = {"act", "dve", "pool"}
        self.res = {}

    def R(self, name):
        r = self.res.get(name)
        if r is None:
            r = self.res[name] = Res()
        return r

    def _wait(self, e, dep):
        key, val = dep
        if key == e and e not in self.same_engine_sync:
            return
        if self.waited[e].get(key, 0) >= val:
            return
        self.eng[e].wait_ge(self.sems[key], val)
        self.waited[e][key] = val

    def _deps(self, reads, writes):
        deps = []
        for r in reads:
            r = self.R(r)
            if r.w:
                deps.append(r.w)
        for w in writes:
            w = self.R(w)
            if w.w:
                deps.append(w.w)
            deps.extend(w.r.items())
        return deps

    def _mark(self, ev, reads, writes):
        for r in reads:
            self.R(r).r[ev[0]] = ev[1]
        for w in writes:
            w = self.R(w)
            w.w = ev
            w.r = {}

    @staticmethod
    def _excl(reads, writes):
        r2 = [r for r in reads if not (isinstance(r, tuple) and r[0] == "ps")]
        if len(r2) != len(reads):
            writes = list(writes) + [r for r in reads if isinstance(r, tuple) and r[0] == "ps" and r not in writes]
        return r2, writes

    def op(self, e, fn, reads=(), writes=(), inc=True):
        reads, writes = self._excl(reads, writes)
        for d in self._deps(reads, writes):
            self._wait(e, d)
        ins = fn(self.eng[e])
        if inc:
            ins.then_inc(self.sems[e], 1)
            self.cnt[e] += 1
            ev = (e, self.cnt[e])
        else:
            ev = (e, self.cnt[e] + 1)
        self._mark(ev, reads, writes)
        return ins

    def dma(self, q, semkey, out, in_, reads=(), writes=(), **kw):
        if semkey not in self.sems:
            self.sems[semkey] = self.nc.alloc_semaphore("d_" + semkey)
            self.dma_cnt[semkey] = 0
        for d in self._deps(reads, writes):
            self._wait(q, d)
        ins = self.eng[q].dma_start(out=out, in_=in_, **kw)
        ins.then_inc(self.sems[semkey], 16)
        self.dma_cnt[semkey] += 16
        self._mark((semkey, self.dma_cnt[semkey]), reads, writes)
        return ins

    def barrier(self):
        for e in self.eng:
            for k in self.sems:
                if k == e:
                    continue
                v = self.cnt[k] if k in self.cnt else self.dma_cnt[k]
                if v > 0:
                    self._wait(e, (k, v))

    def wait_all_dma(self, e, prefix):
        for k, v in self.dma_cnt.items():
            if k.startswith(prefix) and v > 0:
                self._wait(e, (k, v))


def build_program(dbg=False, max_blocks=None):
    nc = bass.Bass(target_bir_lowering=False)
    fw = FW(nc)

    def din(name, shape):
        return nc.dram_tensor(name, list(shape), F32, kind="ExternalInput").ap()

    x_ext = din("x_ext", [NBLK * T, D])
    cmask_d = din("cmask", [128, 4])
    norm_mix_w = din("norm_mix_w", [D])
    w_in = din("w_in", [D, DIN])
    w_alpha_up = din("w_alpha_up", [16, 1024])
    b_alpha = din("b_alpha", [1, 1024])
    gla_norm_w = din("gla_norm_w", [512])
    gmlp_ln_w = din("gmlp_ln_w", [256])
    gmlp_ln_b = din("gmlp_ln_b", [256])
    w_spatial = din("w_spatial", [8, 128, 128])
    b_spatial = din("b_spatial", [8, 128])
    b_gate = din("b_gate", [2, D])
    w_branch = din("w_branch", [2, D, D])
    w_out = din("w_out", [D, D])
    norm_mlp_w = din("norm_mlp_w", [D])
    w_ff_up = din("w_ff_up", [D, DFF])
    w_ff_down = din("w_ff_down", [DFF, D])
    norm_final_w = din("norm_final_w", [D])
    y = nc.dram_tensor("y", [T, D], F32, kind="ExternalOutput").ap()
    aginP_t = nc.dram_tensor("aginP", [128, 512], F32)
    agoutP_t = nc.dram_tensor("agoutP", [4 * 128, 512], F32)
    aginL_t = [nc.dram_tensor(f"aginL{h}", [128, 1024], F32) for h in range(4)]
    agoutL_t = [nc.dram_tensor(f"agoutL{h}", [4 * 128, 1024], F32) for h in range(4)]
    dbg_out = {}
    if dbg:
        for nm in ("d_ogT", "d_omT", "d_mixT", "d_hnT"):
            dbg_out[nm] = nc.dram_tensor(nm, [128, KC * T], BF16, kind="ExternalOutput").ap()
        dbg_out["d_h1"] = nc.dram_tensor("d_h1", [128, NT * D], F32, kind="ExternalOutput").ap()

    w_in_v = w_in.rearrange("(kc p) n -> p kc n", p=128)
    w_br_v = [w_branch[n].rearrange("(kc p) n -> p kc n", p=128) for n in range(2)]
    w_out_v = w_out.rearrange("(kc p) n -> p kc n", p=128)
    w_up_v = w_ff_up.rearrange("(kc p) n -> p kc n", p=128)
    w_dn_v = [w_ff_down[2048 * i:2048 * (i + 1), :].rearrange("(kc p) n -> p kc n", p=128) for i in range(4)]

    K1 = 1024
    ARENA = 206 * K1
    arena = nc.alloc_sbuf_tensor("arena", [128, ARENA // 2], BF16)

    def V(off, shape, dt, parts=None):
        esz = 2 if dt == BF16 else 4
        n = int(np.prod(shape)) * esz
        assert off % 4 == 0 and off + n <= ARENA, (off, n)
        v = arena[:, off // 2:(off + n) // 2]
        if dt != BF16:
            v = v.bitcast(dt)
        if len(shape) == 2:
            v = v.rearrange("p (a b) -> p a b", a=shape[0])
        elif len(shape) == 3:
            v = v.rearrange("p (a b c) -> p a b c", a=shape[0], b=shape[1])
        return v

    class Alloc:
        def __init__(self, base, limit):
            self.o = base
            self.limit = limit

        def __call__(self, shape, dt):
            esz = 2 if dt == BF16 else 4
            n = (int(np.prod(shape)) * esz + 31) // 32 * 32
            v = V(self.o, shape, dt)
            self.o += n
            assert self.o <= self.limit, (self.o, self.limit)
            return v

    R_CONST = 0
    R_RING = 8 * K1
    R_A = 40 * K1
    R_B = 72 * K1
    R_C = 104 * K1
    R_E = 136 * K1
    R_F = 168 * K1
    R_END = ARENA

    ca = Alloc(R_CONST, R_RING)
    identf = ca([128], F32)
    identb = ca([128], BF16)
    Mgt = ca([128], F32)
    Ind = ca([2], F32)
    gnw_bc = ca([512], F32)
    wcol_mix = ca([16], F32)
    wcol_mlp = ca([16], F32)
    bgate = ca([2, 16], F32)
    bsp = ca([8], F32)
    wglr = ca([16, 16], BF16)
    st = ca([96], F32)
    dch = [ca([32], F32) for _ in range(4)]
    cmask = ca([4], F32)
    Ptile = ca([8], F32)
    Pg = [ca([8], F32) for _ in range(3)]
    acoef = [ca([8], F32) for _ in range(3)]
    csr = ca([8], F32)
    ssq = ca([32], F32)
    ones_c = ca([1], F32)
    ring = [V(R_RING + 16 * K1 * i, [16, 512], BF16) for i in range(2)]

    ps = [nc.alloc_psum_tensor(f"ps{i}", [128, 512], F32)[:, :] for i in range(8)]
    MM = [0, 1, 2]
    KV = [3, 4]
    OB = 5
    SPB = 6
    MISC = 7
    misc_bf = ps[MISC][:, 0:256].bitcast(BF16)
    csum_ps = ps[SPB][:, 0:64]

    mm_ctr = [0]

    def mm_bank():
        i = MM[mm_ctr[0] % len(MM)]
        mm_ctr[0] += 1
        return ps[i], ("ps", i)

    st_ctr = [0]

    def st_slot():
        k = st_ctr[0] % 32
        st_ctr[0] += 1
        return st[:, 3 * k:3 * k + 3], ("st", k)

    ev_ctr = [0]

    def copy_evac(out, in_, reads, writes, scale=None, eng=None):
        if eng is None:
            ev_ctr[0] += 1
            eng = ev_ctr[0] % 2
        if eng % 2 == 0:
            if scale is None:
                fw.op("act", lambda e: e.activation(out=out, in_=in_, func=AF.Copy), reads=reads, writes=writes)
            else:
                fw.op("act", lambda e: e.activation(out=out, in_=in_, func=AF.Copy, scale=scale), reads=reads, writes=writes)
        else:
            if scale is None:
                fw.op("dve", lambda e: e.tensor_copy(out=out, in_=in_), reads=reads, writes=writes)
            else:
                fw.op("dve", lambda e: e.tensor_scalar(out=out, in0=in_, scalar1=scale, scalar2=None, op0=ALU.mult),
                      reads=reads, writes=writes)

    deferred = []
    bg = []
    pump_n = [1]

    def defer(fn, delay):
        deferred.append([delay, fn])

    def pump(n=1):
        for _ in range(n):
            while bg:
                try:
                    next(bg[0])
                    break
                except StopIteration:
                    bg.pop(0)

    def drain_bg():
        while bg:
            pump()

    def tick_deferred():
        ready = []
        for d in deferred:
            d[0] -= 1
            if d[0] <= 0:
                ready.append(d)
        for d in ready:
            deferred.remove(d)
            d[1]()

    def tick():
        ready = []
        for d in deferred:
            d[0] -= 1
            if d[0] <= 0:
                ready.append(d)
        for d in ready:
            deferred.remove(d)
            d[1]()
        pump(pump_n[0])

    def flush_deferred():
        while deferred:
            d = deferred.pop(0)
            d[1]()

    def mm_group(out_ap, pairs, bank_res, reads, mid_tick=False):
        n = len(pairs)
        for i, (l, r) in enumerate(pairs):
            fw.op("pe", lambda e: e.matmul(out_ap, lhsT=l, rhs=r, start=(i == 0), stop=(i == n - 1)),
                  reads=reads, writes=[bank_res], inc=(i == n - 1))
            if mid_tick and i == n // 2 - 1:
                tick()

    blocks = []

    def add_block(loads, compute):
        blocks.append((loads, compute))

    def issue_load(i):
        loads, _ = blocks[i]
        slot = i % 2
        for (co, src, n) in loads:
            fw.dma("pool", f"ring{slot}", ring[slot][:, :, co:co + n], src, writes=[("ring", slot)])

    fw.op("pool", lambda e: e.memset(identf, 0.0), writes=["identf"])
    fw.op("pool", lambda e: e.affine_select(out=identf, in_=identf, pattern=[[-1, 128]], compare_op=ALU.not_equal,
                                            fill=1.0, base=0, channel_multiplier=1), reads=["identf"], writes=["identf"])
    fw.op("pool", lambda e: e.tensor_copy(out=identb, in_=identf), reads=["identf"], writes=["identb"])
    fw.op("pool", lambda e: e.memset(Mgt, 1.0), writes=["Mgt"])
    fw.op("pool", lambda e: e.affine_select(out=Mgt, in_=Mgt, pattern=[[-1, 128]], compare_op=ALU.is_gt,
                                            fill=0.0, base=0, channel_multiplier=1), reads=["Mgt"], writes=["Mgt"])
    fw.op("pool", lambda e: e.memset(Mgt[64:128, 0:64], 0.0), reads=["Mgt"], writes=["Mgt"])
    fw.op("pool", lambda e: e.memset(Ind, 0.0), writes=["Ind"])
    fw.op("pool", lambda e: e.memset(Ind[0:64, 0:1], 1.0), reads=["Ind"], writes=["Ind"])
    fw.op("pool", lambda e: e.memset(Ind[64:128, 1:2], 1.0), reads=["Ind"], writes=["Ind"])
    fw.dma("sp", "c0", gnw_bc, gla_norm_w.partition_broadcast(128), writes=["gnw_bc"])
    fw.dma("sp", "c0b", cmask, cmask_d, writes=["cmask"])
    fw.dma("sp", "c1", wcol_mix, norm_mix_w.rearrange("(kc p) -> p kc", p=128), writes=["wcol_mix"],
           allow_slow_non_contiguous=True)
    fw.dma("sp", "c2", wcol_mlp, norm_mlp_w.rearrange("(kc p) -> p kc", p=128), writes=["wcol_mlp"],
           allow_slow_non_contiguous=True)
    for n in range(2):
        fw.dma("sp", f"c3{n}", bgate[:, n, :], b_gate[n].rearrange("(kc p) -> p kc", p=128), writes=["bgate"],
               allow_slow_non_contiguous=True)
    fw.dma("sp", "c4", bsp, b_spatial.rearrange("g t -> t g"), writes=["bsp"], allow_slow_non_contiguous=True)
    fw.dma("pool", "c5", wglr, w_in_v[:, :, C_GLR:C_GLR + 16], writes=["wglr"])

    xnT = V(R_A, [16, T], BF16)
    ogT = V(R_B, [16, T], BF16)
    ac = Alloc(R_C, R_E)
    S = ac([8, 512], F32)
    lb_off = ac.o
    esp = [ac([512], F32) for _ in range(2)]
    expG = [ac([512], F32) for _ in range(2)]
    Lbuf = [V(lb_off + 4096 * i, [2, 512], F32) for i in range(2)]
    w_aug_off = ac.o
    w_aug = ac([1024], F32)
    glrT = ac([T], F32)
    ae = Alloc(R_E, R_F)
    kdec = [ae([8, 256], BF16) for _ in range(4)]
    qT = ae([2, T], BF16)
    qTb = [qT, V(w_aug_off, [2, T], BF16)]
    scan_prog = {}
    sr = ae([8, 512], BF16)
    Sb = [ae([2, 512], BF16) for _ in range(2)]
    af = Alloc(R_F, R_END)
    v_off = af.o
    vbuf = [af([8, 512], BF16) for _ in range(4)]
    xt = [V(v_off + 16 * K1 + 8 * K1 * i, [D], F32) for i in range(2)]
    tmpf = [af([512], F32) for _ in range(2)]
    ogb = [af([512], BF16) for _ in range(2)]

    fw.dma("sp", "c6", w_aug[0:16, :], w_alpha_up, writes=["w_aug"])
    fw.dma("sp", "c7", w_aug[16:17, :], b_alpha, writes=["w_aug"])
    fw.op("dve", lambda e: e.memset(glrT[0:32, :], 1.0), writes=["glrT"])
    fw.op("dve", lambda e: e.memset(S, 0.0), writes=[("S", i) for i in range(8)])

    xt_ctr = [0]

    def norm_tile_a(t, src_rows, dstT, dst_name, src_sb=None, src_res=None):
        slot = xt_ctr[0] % len(xt)
        xt_ctr[0] += 1
        xr = ("xt", slot)
        if src_sb is None:
            fw.dma("sp", f"x{slot}", xt[slot], src_rows(t), writes=[xr])
            src = xt[slot]
            sres = [xr]
        else:
            src = src_sb(t)
            sres = [src_res(t)]
        stv, sres_st = st_slot()
        junk = dstT[:, :, t * 128:(t + 1) * 128]
        fw.op("act", lambda e: e.activation(out=junk, in_=src.rearrange("p (a b) -> p a b", a=16), func=AF.Square,
                                            accum_out=stv[:, 0:1]),
              reads=sres, writes=[(dst_name, t), sres_st])
        fw.op("act", lambda e: e.activation(out=stv[:, 1:2], in_=stv[:, 0:1], func=AF.Sqrt, scale=1.0 / D, bias=EPS),
              reads=[sres_st], writes=[sres_st])
        fw.op("dve", lambda e: e.reciprocal(out=stv[:, 2:3], in_=stv[:, 1:2]), reads=[sres_st], writes=[sres_st])
        fw.op("pool", lambda e: e.tensor_scalar(out=xt[slot], in0=src, scalar1=stv[:, 2:3], scalar2=1.0,
                                                op0=ALU.mult, op1=ALU.mult),
              reads=sres + [sres_st], writes=[xr])
        return slot

    def norm_tile_b(t, slot, wcol, wcol_res, dstT, dst_name, do_pump=True):
        xr = ("xt", slot)
        for g in range(4):
            bank, bres = mm_bank()
            for i in range(4):
                kc = 4 * g + i
                fw.op("pe", lambda e: e.transpose(out=bank[:, i * 128:(i + 1) * 128],
                                                  in_=xt[slot][:, kc * 128:(kc + 1) * 128], identity=identf),
                      reads=[xr, "identf"], writes=[bres], inc=(i == 3))
            for i in range(4):
                kc = 4 * g + i
                copy_evac(dstT[:, kc, t * 128:(t + 1) * 128], bank[:, i * 128:(i + 1) * 128],
                          reads=[bres, wcol_res], writes=[(dst_name, t)], scale=wcol[:, kc:kc + 1], eng=g)
            if do_pump:
                pump(1)

    def norm_tile(t, src_rows, wcol, wcol_res, dstT, dst_name, src_sb=None, src_res=None, do_pump=True):
        slot = norm_tile_a(t, src_rows, dstT, dst_name, src_sb, src_res)
        norm_tile_b(t, slot, wcol, wcol_res, dstT, dst_name, do_pump)

    def defer_norm_tile(t, src_rows, wcol, wcol_res, dstT, dst_name, src_sb=None, src_res=None):
        box = {}

        def a():
            box["slot"] = norm_tile_a(t, src_rows, dstT, dst_name, src_sb, src_res)

        def b():
            norm_tile_b(t, box["slot"], wcol, wcol_res, dstT, dst_name, do_pump=False)
        defer(a, 1)
        defer(b, 1 + len(xt))

    def norm_transpose(src_rows, wcol, wcol_res, dstT, dst_name, src_sb=None, src_res=None):
        la = len(xt) - 1
        slots = {}
        for t in range(min(la, NT)):
            slots[t] = norm_tile_a(t, src_rows, dstT, dst_name, src_sb, src_res)
        for t in range(NT):
            norm_tile_b(t, slots[t], wcol, wcol_res, dstT, dst_name)
            if t + la < NT:
                slots[t + la] = norm_tile_a(t + la, src_rows, dstT, dst_name, src_sb, src_res)

    def glr_compute(blk):
        for tb in range(2):
            bank, bres = mm_bank()
            mm_group(bank[0:16, :], [(wglr[:, kc, :], xnT[:, kc, tb * 512:(tb + 1) * 512]) for kc in range(KC)], bres,
                     reads=["wglr"] + [("xnT", t) for t in range(4 * tb, 4 * tb + 4)])
            fw.op("act", lambda e: e.activation(out=glrT[0:16, tb * 512:(tb + 1) * 512], in_=bank[0:16, :], func=AF.Copy),
                  reads=[bres], writes=["glrT"])

    cc = nc.alloc_semaphore("cc_sem")
    fw.sems["cc"] = cc
    fw.dma_cnt["cc"] = 0
    rg = [[0, 1, 2, 3], [4, 5, 6, 7]]

    def gather(src_sb, in_t, out_t, width, rd, name):
        fw.dma("sp", "ag_" + name, in_t.ap()[:, 0:width], src_sb, reads=rd, writes=["agin_" + name])
        for d in fw._deps(["agin_" + name], ["agout_" + name]):
            fw._wait("pool", d)
        nc.gpsimd.collective_compute("AllGather", ALU.bypass, replica_groups=rg,
                                     ins=[in_t.ap().opt()], outs=[out_t.ap().opt()]).then_inc(cc)
        fw.dma_cnt["cc"] += 1
        fw._mark(("cc", fw.dma_cnt["cc"]), ["agin_" + name], ["agout_" + name])

    lb_ctr = [0]

    def start_state(h):
        if h == 0:
            agoutP = agoutP_t.ap()
            for i in range(3):
                fw.dma("sp", f"pg{i}", Pg[i], agoutP[i * 128:(i + 1) * 128, 0:8], reads=["agout_P"], writes=[("Pg", i)])
                fw.op("dve", lambda e: e.tensor_scalar(out=acoef[i], in0=Pg[i], scalar1=-1.0, scalar2=cmask[:, i:i + 1],
                                                       op0=ALU.add, op1=ALU.mult),
                      reads=[("Pg", i), "cmask"], writes=[("acoef", i)])
                fw.op("dve", lambda e: e.tensor_scalar(out=acoef[i], in0=acoef[i], scalar1=1.0, scalar2=None, op0=ALU.add),
                      reads=[("acoef", i)], writes=[("acoef", i)])
        agoutL = agoutL_t[h].ap()
        hs = [("S", 2 * h), ("S", 2 * h + 1)]
        fw.op("dve", lambda e: e.memset(S[:, 2 * h:2 * h + 2, :], 0.0), reads=hs, writes=hs)
        for i in range(3):
            k = lb_ctr[0] % 2
            lb_ctr[0] += 1
            lb = Lbuf[k]
            lr = ("Lbuf", k)
            alias = [("esp", 0), ("esp", 1)] if k == 0 else [("expG", 0), ("expG", 1)]
            fw.dma("sp", f"lb{k}", lb.rearrange("p a b -> p (a b)"), agoutL[i * 128:(i + 1) * 128, :],
                   reads=[f"agout_L{h}"], writes=[lr] + alias)
            for j in range(2):
                sj = 2 * h + j
                fw.op("act", lambda e: e.activation(out=lb[:, j, :], in_=lb[:, j, :], func=AF.Copy, scale=cmask[:, i:i + 1]),
                      reads=[lr, "cmask"], writes=[lr])
                fw.op("dve", lambda e: e.scalar_tensor_tensor(out=S[:, sj, :], in0=S[:, sj, :], scalar=acoef[i][:, sj:sj + 1],
                                                              in1=lb[:, j, :], op0=ALU.mult, op1=ALU.add),
                      reads=[("S", sj), ("acoef", i), lr], writes=[("S", sj)])

    unit_ctr = [0]

    def make_unit(h, own):
        blk = 0
        par = h
        hp = h % 2

        def z_pair(p):
            bank, bres = mm_bank()
            for i in range(2):
                t = 2 * p + i
                fw.op("pe", lambda e: e.matmul(bank[:, i * 256:(i + 1) * 256], lhsT=glrT[0:17, t * 128:(t + 1) * 128],
                                               rhs=w_aug[0:17, 256 * h:256 * h + 256], start=True, stop=True),
                      reads=["glrT", "w_aug"], writes=[bres], inc=(i == 1))
            er = ("esp", p % 2)
            fw.op("act", lambda e: e.activation(out=esp[p % 2], in_=bank, func=AF.Exp, scale=-1.0), reads=[bres], writes=[er])
            fw.op("act", lambda e: e.activation(out=esp[p % 2], in_=esp[p % 2], func=AF.Ln, bias=1.0), reads=[er], writes=[er])

        def g_pair(p):
            er = ("esp", p % 2)
            bank, bres = mm_bank()
            for i in range(2):
                fw.op("pe", lambda e: e.matmul(bank[:, i * 256:(i + 1) * 256], lhsT=Mgt, rhs=esp[p % 2][:, i * 256:(i + 1) * 256],
                                               start=True, stop=True),
                      reads=["Mgt", er], writes=[bres], inc=(i == 1))
            fw.op("act", lambda e: e.activation(out=expG[p % 2], in_=bank, func=AF.Exp, scale=-1.0 / 16),
                  reads=[bres], writes=[("expG", p % 2)])
            for i in range(2):
                t = 2 * p + i
                for j in range(2):
                    c0 = hp * 32 + j * 16 + 2 * t
                    last = (i == 1 and j == 1)
                    fw.op("pe", lambda e: e.matmul(csum_ps[:, c0:c0 + 2],
                                                   lhsT=esp[p % 2][:, i * 256 + j * 128:i * 256 + (j + 1) * 128], rhs=Ind,
                                                   start=True, stop=True),
                          reads=[er, "Ind"], writes=[("ps", SPB)], inc=last)

        def k_pair(p, slot):
            bank, bres = mm_bank()
            for i in range(2):
                t = 2 * p + i
                mm_group(bank[:, i * 256:(i + 1) * 256],
                         [(xnT[:, kc, t * 128:(t + 1) * 128], ring[slot][:, kc, 0:256]) for kc in range(KC)], bres,
                         reads=[("xnT", t), ("ring", slot)])
                tick()
            fw.op("dve", lambda e: e.tensor_tensor(out=kdec[par][:, 2 * p:2 * p + 2, :].rearrange("p a b -> p (a b)"),
                                                   in0=bank, in1=expG[p % 2], op=ALU.mult),
                  reads=[bres, ("expG", p % 2)], writes=[("kdec", par, 2 * p), ("kdec", par, 2 * p + 1)])

        def k_compute(slot):
            if h == 0:
                glr_compute(0)
            z_pair(0)
            z_pair(1)
            for p in range(4):
                g_pair(p)
                k_pair(p, slot)
                if p + 2 < 4:
                    z_pair(p + 2)
            fw.op("act", lambda e: e.activation(out=dch[par], in_=csum_ps[:, hp * 32:hp * 32 + 32], func=AF.Exp,
                                                scale=-1.0 / 16),
                  reads=[("ps", SPB)], writes=[("dch", par)])
            fw.op("dve", lambda e: e.tensor_reduce(out=csr[:, 2 * h:2 * h + 2],
                                                   in_=csum_ps[:, hp * 32:hp * 32 + 32].rearrange("p (a b) -> p a b", a=2),
                                                   axis=mybir.AxisListType.X, op=ALU.add),
                  reads=[("ps", SPB)], writes=["csr"])
            fw.op("act", lambda e: e.activation(out=Ptile[:, 2 * h:2 * h + 2], in_=csr[:, 2 * h:2 * h + 2], func=AF.Exp,
                                                scale=-1.0 / 16),
                  reads=["csr"], writes=["Ptile"])
            if h == 3:
                gather(Ptile, aginP_t, agoutP_t, 8, ["Ptile"], "P")
            bg.append(scan_gen())

        def v_compute(slot):
            for t in range(NT):
                if h == 0:
                    if t + 1 < NT:
                        norm_tile_b(t + 1, nslots[t + 1], wcol_mix, "wcol_mix", xnT, "xnT", do_pump=False)
                    if t + 2 < NT:
                        nslots[t + 2] = norm_tile_a(t + 2, xsrc, xnT, "xnT")
                bank, bres = mm_bank()
                mm_group(bank, [(xnT[:, kc, t * 128:(t + 1) * 128], ring[slot][:, kc, :]) for kc in range(KC)], bres,
                         reads=[("xnT", t), ("ring", slot)])
                copy_evac(vbuf[par][:, t, :], bank, reads=[bres], writes=[("v", par, t)])
                tick()

        def q_compute(slot):
            pump_n[0] = 1
            if h == 0:
                drain_bg()
            while bg and scan_prog.get(h - 2, 99) < 99:
                pump()
            for ct in range(2):
                for tb in range(2):
                    bank, bres = mm_bank()
                    mm_group(bank, [(ring[slot][:, kc, ct * 128:(ct + 1) * 128], xnT[:, kc, tb * 512:(tb + 1) * 512])
                                    for kc in range(KC)], bres,
                             reads=[("ring", slot)] + [("xnT", t) for t in range(4 * tb, 4 * tb + 4)], mid_tick=True)
                    fw.op("act", lambda e: e.activation(out=qTb[h % 2][:, ct, tb * 512:(tb + 1) * 512], in_=bank, func=AF.Copy,
                                                        scale=1.0 / 16),
                          reads=[bres], writes=[("qT", h % 2)])
                    tick()

        def r_compute(slot):
            for t in range(NT):
                while bg and scan_prog.get(h - 1, 99) < 2 * t + 3:
                    pump()
                bank, bres = mm_bank()
                mm_group(bank, [(xnT[:, kc, t * 128:(t + 1) * 128], ring[slot][:, kc, :]) for kc in range(KC)], bres,
                         reads=[("xnT", t), ("ring", slot)], mid_tick=True)
                fw.op("act", lambda e: e.activation(out=sr[:, t, :], in_=bank, func=AF.Silu), reads=[bres], writes=[("sr", t)])
                tick()
            start_state(h)
            scan_prog[h] = 0
            bg.append(scan_gen())

        def o_post_a(t):
            ob = ps[OB]
            k = t % 2
            stv, sres_st = st_slot()
            fw.op("act", lambda e: e.activation(out=ogb[k], in_=ob, func=AF.Square, accum_out=stv[:, 0:1]),
                  reads=[("ps", OB)], writes=[("ogb", k), sres_st])
            fw.op("act", lambda e: e.activation(out=stv[:, 1:2], in_=stv[:, 0:1], func=AF.Sqrt, scale=1.0 / 512, bias=EPS),
                  reads=[sres_st], writes=[sres_st])
            fw.op("dve", lambda e: e.reciprocal(out=stv[:, 2:3], in_=stv[:, 1:2]), reads=[sres_st], writes=[sres_st])
            fw.op("dve", lambda e: e.scalar_tensor_tensor(out=tmpf[k], in0=ob, scalar=stv[:, 2:3], in1=gnw_bc,
                                                          op0=ALU.mult, op1=ALU.mult),
                  reads=[("ps", OB), sres_st, "gnw_bc"], writes=[("tmpf", k)])
            fw.op("dve", lambda e: e.tensor_tensor(out=ogb[k], in0=tmpf[k], in1=sr[:, t, :], op=ALU.mult),
                  reads=[("tmpf", k), ("sr", t)], writes=[("ogb", k)])

        def o_post_b(t):
            k = t % 2
            for i in range(4):
                fw.op("pe", lambda e: e.transpose(out=misc_bf[:, i * 128:(i + 1) * 128], in_=ogb[k][:, i * 128:(i + 1) * 128],
                                                  identity=identb),
                      reads=[("ogb", k), "identb"], writes=[("ps", MISC)], inc=(i == 3))
            copy_evac(ogT[:, 4 * h:4 * h + 4, t * 128:(t + 1) * 128], misc_bf.rearrange("p (a b) -> p a b", a=4),
                      reads=[("ps", MISC)], writes=[("ogT", t)])

        def o_mm(c):
            r0 = 64 * (c % 2)
            for j in range(2):
                fw.op("pe", lambda e: e.matmul(ps[OB][r0:r0 + 64, :], lhsT=qTb[h % 2][:, j, c * 64:(c + 1) * 64],
                                               rhs=Sb[c % 2][:, j, :], start=(j == 0), stop=(j == 1)),
                      reads=[("qT", h % 2), ("Sb", c % 2, j)], writes=[("ps", OB)], inc=(j == 1))

        def scan_gen():
            for c in range(16 + (3 if own else 0)):
                if c < 16:
                    t, half = c // 2, c % 2
                    r0 = 64 * half
                    for j in range(2):
                        kb = KV[j]
                        fw.op("pe", lambda e: e.matmul(ps[kb], lhsT=kdec[par][r0:r0 + 64, t, j * 128:(j + 1) * 128],
                                                       rhs=vbuf[par][r0:r0 + 64, t, :], start=True, stop=True),
                              reads=[("kdec", par, t), ("v", par, t)], writes=[("ps", kb)])
                        sj = 2 * h + j
                        fw.op("dve", lambda e: e.scalar_tensor_tensor(out=S[:, sj, :], in0=S[:, sj, :],
                                                                      scalar=dch[par][:, j * 16 + c:j * 16 + c + 1], in1=ps[kb],
                                                                      op0=ALU.mult, op1=ALU.add),
                              reads=[("S", sj), ("dch", par), ("ps", kb)], writes=[("S", sj)])
                        if own:
                            fw.op("pool", lambda e: e.tensor_copy(out=Sb[c % 2][:, j, :], in_=S[:, sj, :]),
                                  reads=[("S", sj)], writes=[("Sb", c % 2, j)])
                if own:
                    if c >= 3 and (c - 3) % 2 == 1:
                        o_post_b((c - 3) // 2)
                    if 1 <= c <= 16:
                        o_mm(c - 1)
                        if (c - 1) % 2 == 1:
                            o_post_a((c - 1) // 2)
                    scan_prog[h] = c + 1
                yield
            if own:
                scan_prog[h] = 99
            if not own:
                gather(S[:, 2 * h:2 * h + 2, :].rearrange("p a b -> p (a b)"), aginL_t[h], agoutL_t[h], 1024,
                       [("S", 2 * h), ("S", 2 * h + 1)], f"L{h}")
                yield

        if not own:
            add_block([(0, w_in_v[:, :, C_V + 512 * h:C_V + 512 * h + 512], 512)], v_compute)
            add_block([(0, w_in_v[:, :, C_K + 256 * h:C_K + 256 * h + 256], 256)], k_compute)
        else:
            add_block([(0, w_in_v[:, :, C_Q + 256 * h:C_Q + 256 * h + 256], 256)], q_compute)
            add_block([(0, w_in_v[:, :, C_R + 512 * h:C_R + 512 * h + 512], 512)], r_compute)

    stage_hooks = {}

    def hook_before(fn):
        stage_hooks[len(blocks)] = fn

    nslots = {}

    def xsrc(t):
        return x_ext[t * 128:(t + 1) * 128, :]

    def pre0():
        nslots[0] = norm_tile_a(0, xsrc, xnT, "xnT")
        nslots[1] = norm_tile_a(1, xsrc, xnT, "xnT")
        norm_tile_b(0, nslots[0], wcol_mix, "wcol_mix", xnT, "xnT", do_pump=False)
    hook_before(pre0)
    for h in range(4):
        make_unit(h, False)

    for h in range(4):
        make_unit(h, True)

    omT = V(R_C, [16, T], BF16)
    ae2 = Alloc(R_E, R_F)
    gub = [ae2([8, 512], BF16) for _ in range(2)]
    vln = [ae2([8, 512], BF16) for _ in range(2)]
    af2 = Alloc(R_F, R_END)
    gvf = [af2([512], F32) for _ in range(2)]
    omb = [af2([512], BF16) for _ in range(4)]
    WsT = af2([8, 128], BF16)
    wsf = af2([8, 128], F32)
    lnw_bc = af2([256], F32)
    lnb_bc = af2([256], F32)

    def stage2_pre():
        pump_n[0] = 1
        drain_bg()
        flush_deferred()
        fw.barrier()
        fw.dma("sp", "c8", lnw_bc, gmlp_ln_w.partition_broadcast(128), writes=["lnw_bc"])
        fw.dma("sp", "c9", lnb_bc, gmlp_ln_b.partition_broadcast(128), writes=["lnb_bc"])
        fw.dma("sp", "c10", wsf, w_spatial.rearrange("g t s -> t g s"), writes=["wsf"])
        for g in range(8):
            bank, bres = mm_bank()
            fw.op("pe", lambda e: e.transpose(out=bank[:, 0:128], in_=wsf[:, g, :], identity=identf),
                  reads=["wsf", "identf"], writes=[bres])
            fw.op("dve", lambda e: e.tensor_copy(out=WsT[:, g, :], in_=bank[:, 0:128]), reads=[bres], writes=["WsT"])
        fw.op("dve", lambda e: e.memset(WsT[64:128, :, 0:64], 0.0), reads=["WsT"], writes=["WsT"])

    hook_before(stage2_pre)

    def make_gmlp(j):
        par = j % 2

        def gu_compute(slot):
            for t in range(NT):
                bank, bres = mm_bank()
                mm_group(bank, [(xnT[:, kc, t * 128:(t + 1) * 128], ring[slot][:, kc, :]) for kc in range(KC)], bres,
                         reads=[("xnT", t), ("ring", slot)])
                fw.op("act", lambda e: e.activation(out=gub[par][:, t, :], in_=bank, func=AF.Gelu),
                      reads=[bres], writes=[("gub", par, t)])
                tick()

        def spatial_mm(t):
            k = t % 4
            sb = [KV[0], KV[1], SPB][t % 3]
            for g in range(2):
                gg = 2 * j + g
                fw.op("pe", lambda e: e.matmul(ps[sb][:, g * 256:(g + 1) * 256], lhsT=WsT[:, gg, :],
                                               rhs=vln[par][:, t, g * 256:(g + 1) * 256], start=True, stop=True),
                      reads=["WsT", ("vln", par, t)], writes=[("ps", sb)], inc=(g == 1))
            for g in range(2):
                gg = 2 * j + g
                fw.op("dve", lambda e: e.scalar_tensor_tensor(out=omb[k][:, g * 256:(g + 1) * 256],
                                                              in0=ps[sb][:, g * 256:(g + 1) * 256],
                                                              scalar=bsp[:, gg:gg + 1],
                                                              in1=gub[par][:, t, g * 256:(g + 1) * 256],
                                                              op0=ALU.add, op1=ALU.mult),
                      reads=[("ps", sb), "bsp", ("gub", par, t)], writes=[("omb", k)])

        def spatial_tr(t):
            k = t % 4
            tb_i = [MISC, OB][t % 2]
            tbf = ps[tb_i][:, 0:256].bitcast(BF16)
            for i in range(4):
                fw.op("pe", lambda e: e.transpose(out=tbf[:, i * 128:(i + 1) * 128], in_=omb[k][:, i * 128:(i + 1) * 128],
                                                  identity=identb),
                      reads=[("omb", k), "identb"], writes=[("ps", tb_i)], inc=(i == 3))
            copy_evac(omT[:, 4 * j:4 * j + 4, t * 128:(t + 1) * 128], tbf.rearrange("p (a b) -> p a b", a=4),
                      reads=[("ps", tb_i)], writes=[("omT", t)], eng=0)

        def gv_compute(slot):
            for t in range(NT):
                bank, bres = mm_bank()
                mm_group(bank, [(xnT[:, kc, t * 128:(t + 1) * 128], ring[slot][:, kc, :]) for kc in range(KC)], bres,
                         reads=[("xnT", t), ("ring", slot)])
                k = t % 2
                tick_deferred()
                fw.op("act", lambda e: e.activation(out=gvf[k], in_=bank, func=AF.Gelu), reads=[bres], writes=[("gvf", k)])
                for g in range(2):
                    sl = gvf[k][:, g * 256:(g + 1) * 256]
                    stv, sres_st = st_slot()
                    stv2, sres_st2 = st_slot()
                    fw.op("dve", lambda e: e.bn_stats(out=bn6[k][:, g * 6:(g + 1) * 6], in_=sl),
                          reads=[("gvf", k)], writes=[("bn6", k, g)])
                    fw.op("dve", lambda e: e.bn_aggr(out=stv[:, 0:2], in_=bn6[k][:, g * 6:(g + 1) * 6]),
                          reads=[("bn6", k, g)], writes=[sres_st])
                    fw.op("act", lambda e: e.activation(out=stv2[:, 0:1], in_=stv[:, 1:2], func=AF.Sqrt, bias=EPS),
                          reads=[sres_st], writes=[sres_st2])
                    fw.op("dve", lambda e: e.reciprocal(out=stv2[:, 1:2], in_=stv2[:, 0:1]), reads=[sres_st2], writes=[sres_st2])
                    fw.op("dve", lambda e: e.tensor_scalar(out=sl, in0=sl, scalar1=stv[:, 0:1], scalar2=stv2[:, 1:2],
                                                           op0=ALU.subtract, op1=ALU.mult),
                          reads=[("gvf", k), sres_st, sres_st2], writes=[("gvf", k)])
                    fw.op("dve", lambda e: e.tensor_tensor(out=sl, in0=sl, in1=lnw_bc, op=ALU.mult),
                          reads=[("gvf", k), "lnw_bc"], writes=[("gvf", k)])
                    fw.op("pool", lambda e: e.tensor_tensor(out=vln[par][:, t, g * 256:(g + 1) * 256], in0=sl, in1=lnb_bc, op=ALU.add),
                          reads=[("gvf", k), "lnb_bc"], writes=[("vln", par, t)])
                defer(lambda t=t: spatial_mm(t), 3)
                defer(lambda t=t: spatial_tr(t), 5)
                pump(pump_n[0])

        add_block([(0, w_in_v[:, :, C_GU + 512 * j:C_GU + 512 * j + 512], 512)], gu_compute)
        add_block([(0, w_in_v[:, :, C_GV + 512 * j:C_GV + 512 * j + 512], 512)], gv_compute)

    bn6 = [af2([12], F32) for _ in range(2)]
    for j in range(4):
        make_gmlp(j)

    mixT = V(R_E, [16, T], BF16)
    af3 = Alloc(R_F, R_END)
    gt = [af3([4, T], BF16) for _ in range(2)]
    m0 = af3([4, T], F32)
    tmp3 = [af3([512], F32) for _ in range(2)]

    def stage3_pre():
        flush_deferred()
        drain_bg()
        fw.barrier()
        if dbg:
            fw.dma("sp", "dbg", dbg_out["d_ogT"], ogT.rearrange("p a b -> p (a b)"), reads=[("ogT", t) for t in range(NT)])
            fw.dma("sp", "dbg", dbg_out["d_omT"], omT.rearrange("p a b -> p (a b)"), reads=[("omT", t) for t in range(NT)])

    hook_before(stage3_pre)

    def make_stage3(j):
        def fm_groups(slot, actT, act_name, evac):
            for ct in range(4):
                for tb in range(2):
                    bank, bres = mm_bank()
                    mm_group(bank, [(ring[slot][:, kc, ct * 128:(ct + 1) * 128], actT[:, kc, tb * 512:(tb + 1) * 512])
                                    for kc in range(KC)], bres,
                             reads=[("ring", slot)] + [(act_name, t) for t in range(4 * tb, 4 * tb + 4)])
                    evac(bank, bres, ct, tb)
                    tick()

        def gate_compute(n):
            def f(slot):
                def evac(bank, bres, ct, tb):
                    fw.op("act", lambda e: e.activation(out=gt[n][:, ct, tb * 512:(tb + 1) * 512], in_=bank, func=AF.Sigmoid,
                                                        bias=bgate[:, n, 4 * j + ct:4 * j + ct + 1]),
                          reads=[bres, "bgate"], writes=[("gt", n, ct, tb)])
                fm_groups(slot, xnT, "xnT", evac)
            return f

        def br0_compute(slot):
            def evac(bank, bres, ct, tb):
                fw.op("dve", lambda e: e.tensor_tensor(out=m0[:, ct, tb * 512:(tb + 1) * 512], in0=bank,
                                                       in1=gt[0][:, ct, tb * 512:(tb + 1) * 512], op=ALU.mult),
                      reads=[bres, ("gt", 0, ct, tb)], writes=[("m0", ct, tb)])
            fm_groups(slot, ogT, "ogT", evac)

        def br1_compute(slot):
            def evac(bank, bres, ct, tb):
                k = (ct * 2 + tb) % 2
                fw.op("dve", lambda e: e.tensor_tensor(out=tmp3[k], in0=bank, in1=gt[1][:, ct, tb * 512:(tb + 1) * 512],
                                                       op=ALU.mult),
                      reads=[bres, ("gt", 1, ct, tb)], writes=[("tmp3", k)])
                fw.op("dve", lambda e: e.tensor_tensor(out=mixT[:, 4 * j + ct, tb * 512:(tb + 1) * 512], in0=tmp3[k],
                                                       in1=m0[:, ct, tb * 512:(tb + 1) * 512], op=ALU.add),
                      reads=[("tmp3", k), ("m0", ct, tb)], writes=[("mixT", tb)])
            fm_groups(slot, omT, "omT", evac)

        add_block([(0, w_in_v[:, :, C_G0 + 512 * j:C_G0 + 512 * j + 512], 512)], gate_compute(0))
        add_block([(0, w_br_v[0][:, :, 512 * j:512 * j + 512], 512)], br0_compute)
        add_block([(0, w_in_v[:, :, C_G1 + 512 * j:C_G1 + 512 * j + 512], 512)], gate_compute(1))
        add_block([(0, w_br_v[1][:, :, 512 * j:512 * j + 512], 512)], br1_compute)

    for j in range(4):
        make_stage3(j)

    hbuf = V(R_A, [NT, D], F32)
    hnT = V(R_C, [16, T], BF16)

    def stage4_pre():
        fw.barrier()
        xt[0], xt[1] = xt5[0], xt5[1]
        if dbg:
            fw.dma("sp", "dbg", dbg_out["d_mixT"], mixT.rearrange("p a b -> p (a b)"), reads=[("mixT", tb) for tb in range(2)])
        for t in range(NT):
            fw.dma("sp", f"hx{t}", hbuf[:, t, :], x_ext[t * 128:(t + 1) * 128, :],
                   writes=[("h", t)])

    hook_before(stage4_pre)

    def make_wout(j):
        def compute(slot):
            for t in range(NT):
                bank, bres = mm_bank()
                mm_group(bank, [(mixT[:, kc, t * 128:(t + 1) * 128], ring[slot][:, kc, :]) for kc in range(KC)], bres,
                         reads=[("mixT", t // 4), ("ring", slot)])
                hs = hbuf[:, t, 512 * j:512 * j + 512]
                fw.op("dve", lambda e: e.tensor_tensor(out=hs, in0=hs, in1=bank, op=ALU.add),
                      reads=[bres, ("h", t)], writes=[("h", t)])
                if j == 3:
                    defer_norm_tile(t, None, wcol_mlp, "wcol_mlp", hnT, "hnT", src_sb=lambda tt: hbuf[:, tt, :],
                                    src_res=lambda tt: ("h", tt))
                tick()
        add_block([(0, w_out_v[:, :, 512 * j:512 * j + 512], 512)], compute)

    for j in range(4):
        make_wout(j)

    aT = V(R_E, [16, T], BF16)
    af5 = Alloc(R_F, R_END)
    xt5 = [af5([D], F32) for _ in range(2)]
    rtmp = [af5([512], F32) for _ in range(2)]
    wbc = af5([D], F32)
    junk6 = af5([D], BF16)

    def stage5_pre():
        flush_deferred()
        fw.dma("sp", "c11", wbc, norm_final_w.partition_broadcast(128), writes=["wbc"])
        if dbg:
            fw.dma("sp", "dbg", dbg_out["d_hnT"], hnT.rearrange("p a b -> p (a b)"), reads=[("hnT", t) for t in range(NT)])
            fw.dma("sp", "dbg", dbg_out["d_h1"], hbuf.rearrange("p a b -> p (a b)"), reads=[("h", t) for t in range(NT)])

    hook_before(stage5_pre)

    def final_tile(t):
        stv, sres_st = st_slot()
        fw.op("dve", lambda e: e.tensor_reduce(out=stv[:, 0:1], in_=ssq[:, 4 * t:4 * t + 4], axis=mybir.AxisListType.X, op=ALU.add),
              reads=[("ssq", t, j) for j in range(4)], writes=[sres_st])
        fw.op("act", lambda e: e.activation(out=stv[:, 1:2], in_=stv[:, 0:1], func=AF.Sqrt, scale=1.0 / D, bias=EPS),
              reads=[sres_st], writes=[sres_st])
        fw.op("dve", lambda e: e.reciprocal(out=stv[:, 2:3], in_=stv[:, 1:2]), reads=[sres_st], writes=[sres_st])
        fw.op("dve", lambda e: e.scalar_tensor_tensor(out=hbuf[:, t, 0:1024], in0=hbuf[:, t, 0:1024], scalar=stv[:, 2:3],
                                                      in1=wbc[:, 0:1024], op0=ALU.mult, op1=ALU.mult),
              reads=[("h", t), sres_st, "wbc"], writes=[("hA", t)])
        fw.op("pool", lambda e: e.tensor_scalar(out=hbuf[:, t, 1024:2048], in0=hbuf[:, t, 1024:2048], scalar1=stv[:, 2:3],
                                                scalar2=1.0, op0=ALU.mult, op1=ALU.mult),
              reads=[("h", t), sres_st], writes=[("hB", t)])
        fw.op("pool", lambda e: e.tensor_tensor(out=hbuf[:, t, 1024:2048], in0=hbuf[:, t, 1024:2048], in1=wbc[:, 1024:2048],
                                                op=ALU.mult),
              reads=[("hB", t), "wbc"], writes=[("hB", t)])
        fw.dma("sp", f"yo{t}", y[t * 128:(t + 1) * 128, :], hbuf[:, t, :], reads=[("h", t), ("hA", t), ("hB", t)], writes=[("y", t)])

    def make_ffn(i):
        def up_compute(u):
            def f(slot):
                for ct in range(4):
                    for tb in range(2):
                        bank, bres = mm_bank()
                        mm_group(bank, [(ring[slot][:, kc, ct * 128:(ct + 1) * 128], hnT[:, kc, tb * 512:(tb + 1) * 512])
                                        for kc in range(KC)], bres,
                                 reads=[("ring", slot)] + [("hnT", t) for t in range(4 * tb, 4 * tb + 4)])
                        k = (ct * 2 + tb) % 2
                        fw.op("act", lambda e: e.activation(out=rtmp[k], in_=bank, func=AF.Relu), reads=[bres], writes=[("rtmp", k)])
                        fw.op("dve", lambda e: e.tensor_tensor(out=aT[:, 4 * u + ct, tb * 512:(tb + 1) * 512], in0=rtmp[k],
                                                               in1=rtmp[k], op=ALU.mult),
                              reads=[("rtmp", k)], writes=[("aT", tb)])
                        tick()
            return f

        def down_compute(j):
            def f(slot):
                for t in range(NT):
                    bank, bres = mm_bank()
                    mm_group(bank, [(aT[:, kc, t * 128:(t + 1) * 128], ring[slot][:, kc, :]) for kc in range(KC)], bres,
                             reads=[("aT", t // 4), ("ring", slot)])
                    hs = hbuf[:, t, 512 * j:512 * j + 512]
                    fw.op("dve", lambda e: e.tensor_tensor(out=hs, in0=hs, in1=bank, op=ALU.add),
                          reads=[bres, ("h", t)], writes=[("h", t)])
                    if i == 3:
                        fw.op("act", lambda e: e.activation(out=junk6[:, 0:512], in_=hs, func=AF.Square,
                                                            accum_out=ssq[:, 4 * t + j:4 * t + j + 1]),
                              reads=[("h", t)], writes=["junk6", ("ssq", t, j)])
                    if i == 3 and j == 3:
                        defer(lambda t=t: final_tile(t), 1)
                    tick()
            return f

        for u in range(4):
            c0 = 2048 * i + 512 * u
            add_block([(0, w_up_v[:, :, c0:c0 + 512], 512)], up_compute(u))
        for j in range(4):
            add_block([(0, w_dn_v[i][:, :, 512 * j:512 * j + 512], 512)], down_compute(j))

    for i in range(4):
        make_ffn(i)

    n = len(blocks)
    if max_blocks is not None:
        n = min(n, max_blocks)
    issue_load(0)
    for i in range(n):
        if i in stage_hooks:
            stage_hooks[i]()
        if i + 1 < n:
            issue_load(i + 1)
        blocks[i][1](i % 2)

    drain_bg()
    flush_deferred()
    fw.wait_all_dma("sp", "yo")
    fw.wait_all_dma("sp", "dbg")
    return nc


_CACHE = {}


def _prep_inputs(inputs):
    x = np.ascontiguousarray(inputs["x"], dtype=np.float32)
    B, Sq, _ = x.shape
    per_seq = Sq // T
    shared = {
        "norm_mix_w": inputs["norm_mix_w"].reshape(D),
        "w_in": inputs["w_in"].reshape(D, DIN),
        "w_alpha_up": inputs["w_alpha_up"].reshape(16, 1024),
        "b_alpha": inputs["b_alpha"].reshape(1, 1024),
        "gla_norm_w": inputs["gla_norm_w"].reshape(512),
        "gmlp_ln_w": inputs["gmlp_ln_w"].reshape(256),
        "gmlp_ln_b": inputs["gmlp_ln_b"].reshape(256),
        "w_spatial": inputs["w_spatial"].reshape(8, 128, 128),
        "b_spatial": inputs["b_spatial"].reshape(8, 128),
        "b_gate": inputs["b_gate"].reshape(2, D),
        "w_branch": inputs["w_branch"].reshape(2, D, D),
        "w_out": inputs["w_out"].reshape(D, D),
        "norm_mlp_w": inputs["norm_mlp_w"].reshape(D),
        "w_ff_up": inputs["w_ff_up"].reshape(D, DFF),
        "w_ff_down": inputs["w_ff_down"].reshape(DFF, D),
        "norm_final_w": inputs["norm_final_w"].reshape(D),
    }
    shared = {k: np.ascontiguousarray(v, dtype=np.float32) for k, v in shared.items()}
    in_maps = []
    for c in range(8):
        b, j = c // per_seq, c % per_seq
        m = dict(shared)
        m["x_ext"] = np.ascontiguousarray(x[b, j * T:(j + 1) * T])
        cm = np.zeros((128, 4), np.float32)
        cm[:, :j] = 1.0
        m["cmask"] = cm
        in_maps.append(m)
    return in_maps, B, Sq


def kernel(**inputs):
    in_maps, B, Sq = _prep_inputs(inputs)
    if "nc" not in _CACHE:
        _CACHE["nc"] = build_program()
    res = run_bass_kernel_spmd(_CACHE["nc"], in_maps, core_ids=list(range(8)))
    per_seq = Sq // T
    out = np.empty((B, Sq, D), np.float32)
    for c in range(8):
        b, j = c // per_seq, c % per_seq
        out[b, j * T:(j + 1) * T] = res.results[c]["y"]
    return out
```

```python
import numpy as np
import concourse.bass as bass
import concourse.mybir as mybir
from concourse.bass_utils import run_bass_kernel_spmd

F32 = mybir.dt.float32
BF16 = mybir.dt.bfloat16
AF = mybir.ActivationFunctionType
ALU = mybir.AluOpType

D = 2048
KC = 16
T = 1024
NT = 8
DIN = 14352
DFF = 8192
EPS = 1e-6
NBLK = 1
C_Q, C_K, C_V, C_R, C_GLR, C_GU, C_GV, C_G0, C_G1 = 0, 1024, 2048, 4096, 6144, 6160, 8208, 10256, 12304


class Res:
    __slots__ = ("w", "r")

    def __init__(self):
        self.w = None
        self.r = {}


class FW:
    def __init__(self, nc):
        self.nc = nc
        self.eng = {"pe": nc.tensor, "act": nc.scalar, "dve": nc.vector, "pool": nc.gpsimd, "sp": nc.sync}
        self.sems = {k: nc.alloc_semaphore("s_" + k) for k in self.eng}
        self.cnt = {k: 0 for k in self.eng}
        self.waited = {k: {} for k in self.eng}
        self.dma_cnt = {}
        self.same_engine_sync = {"act", "dve", "pool"}
        self.res = {}

    def R(self, name):
        r = self.res.get(name)
        if r is None:
            r = self.res[name] = Res()
        return r

    def _wait(self, e, dep):
        key, val = dep
        if key == e and e not in self.same_engine_sync:
            return
        if self.waited[e].get(key, 0) >= val:
            return
        self.eng[e].wait_ge(self.sems[key], val)
        self.waited[e][key] = val

    def _deps(self, reads, writes):
        deps = []
        for r in reads:
            r = self.R(r)
            if r.w:
                deps.append(r.w)
        for w in writes:
            w = self.R(w)
            if w.w:
                deps.append(w.w)
            deps.extend(w.r.items())
        return deps

    def _mark(self, ev, reads, writes):
        for r in reads:
            self.R(r).r[ev[0]] = ev[1]
        for w in writes:
            w = self.R(w)
            w.w = ev
            w.r = {}

    @staticmethod
    def _excl(reads, writes):
        r2 = [r for r in reads if not (isinstance(r, tuple) and r[0] == "ps")]
        if len(r2) != len(reads):
            writes = list(writes) + [r for r in reads if isinstance(r, tuple) and r[0] == "ps" and r not in writes]
        return r2, writes

    def op(self, e, fn, reads=(), writes=(), inc=True):
        reads, writes = self._excl(reads, writes)
        for d in self._deps(reads, writes):
            self._wait(e, d)
        ins = fn(self.eng[e])
        if inc:
            ins.then_inc(self.sems[e], 1)
            self.cnt[e] += 1
            ev = (e, self.cnt[e])
        else:
            ev = (e, self.cnt[e] + 1)
        self._mark(ev, reads, writes)
        return ins

    def dma(self, q, semkey, out, in_, reads=(), writes=(), **kw):
        if semkey not in self.sems:
            self.sems[semkey] = self.nc.alloc_semaphore("d_" + semkey)
            self.dma_cnt[semkey] = 0
        for d in self._deps(reads, writes):
            self._wait(q, d)
        ins = self.eng[q].dma_start(out=out, in_=in_, **kw)
        ins.then_inc(self.sems[semkey], 16)
        self.dma_cnt[semkey] += 16
        self._mark((semkey, self.dma_cnt[semkey]), reads, writes)
        return ins

    def barrier(self):
        for e in self.eng:
            for k in self.sems:
                if k == e:
                    continue
                v = self.cnt[k] if k in self.cnt else self.dma_cnt[k]
                if v > 0:
                    self._wait(e, (k, v))

    def wait_all_dma(self, e, prefix):
        for k, v in self.dma_cnt.items():
            if k.startswith(prefix) and v > 0:
                self._wait(e, (k, v))


def build_program(dbg=False, max_blocks=None):
    nc = bass.Bass(target_bir_lowering=False)
    fw = FW(nc)

    def din(name, shape):
        return nc.dram_tensor(name, list(shape), F32, kind="ExternalInput").ap()

    x_ext = din("x_ext", [NBLK * T, D])
    cmask_d = din("cmask", [128, 4])
    norm_mix_w = din("norm_mix_w", [D])
    w_in = din("w_in", [D, DIN])
    w_alpha_up = din("w_alpha_up", [16, 1024])
    b_alpha = din("b_alpha", [1, 1024])
    gla_norm_w = din("gla_norm_w", [512])
    gmlp_ln_w = din("gmlp_ln_w", [256])
    gmlp_ln_b = din("gmlp_ln_b", [256])
    w_spatial = din("w_spatial", [8, 128, 128])
    b_spatial = din("b_spatial", [8, 128])
    b_gate = din("b_gate", [2, D])
    w_branch = din("w_branch", [2, D, D])
    w_out = din("w_out", [D, D])
    norm_mlp_w = din("norm_mlp_w", [D])
    w_ff_up = din("w_ff_up", [D, DFF])
    w_ff_down = din("w_ff_down", [DFF, D])
    norm_final_w = din("norm_final_w", [D])
    y = nc.dram_tensor("y", [T, D], F32, kind="ExternalOutput").ap()
    aginP_t = nc.dram_tensor("aginP", [128, 512], F32)
    agoutP_t = nc.dram_tensor("agoutP", [4 * 128, 512], F32)
    aginL_t = [nc.dram_tensor(f"aginL{h}", [128, 1024], F32) for h in range(4)]
    agoutL_t = [nc.dram_tensor(f"agoutL{h}", [4 * 128, 1024], F32) for h in range(4)]
    dbg_out = {}
    if dbg:
        for nm in ("d_ogT", "d_omT", "d_mixT", "d_hnT"):
            dbg_out[nm] = nc.dram_tensor(nm, [128, KC * T], BF16, kind="ExternalOutput").ap()
        dbg_out["d_h1"] = nc.dram_tensor("d_h1", [128, NT * D], F32, kind="ExternalOutput").ap()

    w_in_v = w_in.rearrange("(kc p) n -> p kc n", p=128)
    w_br_v = [w_branch[n].rearrange("(kc p) n -> p kc n", p=128) for n in range(2)]
    w_out_v = w_out.rearrange("(kc p) n -> p kc n", p=128)
    w_up_v = w_ff_up.rearrange("(kc p) n -> p kc n", p=128)
    w_dn_v = [w_ff_down[2048 * i:2048 * (i + 1), :].rearrange("(kc p) n -> p kc n", p=128) for i in range(4)]

    K1 = 1024
    ARENA = 206 * K1
    arena = nc.alloc_sbuf_tensor("arena", [128, ARENA // 2], BF16)

    def V(off, shape, dt, parts=None):
        esz = 2 if dt == BF16 else 4
        n = int(np.prod(shape)) * esz
        assert off % 4 == 0 and off + n <= ARENA, (off, n)
        v = arena[:, off // 2:(off + n) // 2]
        if dt != BF16:
            v = v.bitcast(dt)
        if len(shape) == 2:
            v = v.rearrange("p (a b) -> p a b", a=shape[0])
        elif len(shape) == 3:
            v = v.rearrange("p (a b c) -> p a b c", a=shape[0], b=shape[1])
        return v

    class Alloc:
        def __init__(self, base, limit):
            self.o = base
            self.limit = limit

        def __call__(self, shape, dt):
            esz = 2 if dt == BF16 else 4
            n = (int(np.prod(shape)) * esz + 31) // 32 * 32
            v = V(self.o, shape, dt)
            self.o += n
            assert self.o <= self.limit, (self.o, self.limit)
            return v

    R_CONST = 0
    R_RING = 8 * K1
    R_A = 40 * K1
    R_B = 72 * K1
    R_C = 104 * K1
    R_E = 136 * K1
    R_F = 168 * K1
    R_END = ARENA

    ca = Alloc(R_CONST, R_RING)
    identf = ca([128], F32)
    identb = ca([128], BF16)
    Mgt = ca([128], F32)
    Ind = ca([2], F32)
    gnw_bc = ca([512], F32)
    wcol_mix = ca([16], F32)
    wcol_mlp = ca([16], F32)
    bgate = ca([2, 16], F32)
    bsp = ca([8], F32)
    wglr = ca([16, 16], BF16)
    st = ca([96], F32)
    dch = [ca([32], F32) for _ in range(4)]
    cmask = ca([4], F32)
    Ptile = ca([8], F32)
    Pg = [ca([8], F32) for _ in range(3)]
    acoef = [ca([8], F32) for _ in range(3)]
    csr = ca([8], F32)
    ssq = ca([32], F32)
    ones_c = ca([1], F32)
    ring = [V(R_RING + 16 * K1 * i, [16, 512], BF16) for i in range(2)]

    ps = [nc.alloc_psum_tensor(f"ps{i}", [128, 512], F32)[:, :] for i in range(8)]
    MM = [0, 1, 2]
    KV = [3, 4]
    OB = 5
    SPB = 6
    MISC = 7
    misc_bf = ps[MISC][:, 0:256].bitcast(BF16)
    csum_ps = ps[SPB][:, 0:64]

    mm_ctr = [0]

    def mm_bank():
        i = MM[mm_ctr[0] % len(MM)]
        mm_ctr[0] += 1
        return ps[i], ("ps", i)

    st_ctr = [0]

    def st_slot():
        k = st_ctr[0] % 32
        st_ctr[0] += 1
        return st[:, 3 * k:3 * k + 3], ("st", k)

    ev_ctr = [0]

    def copy_evac(out, in_, reads, writes, scale=None, eng=None):
        if eng is None:
            ev_ctr[0] += 1
            eng = ev_ctr[0] % 2
        if eng % 2 == 0:
            if scale is None:
                fw.op("act", lambda e: e.activation(out=out, in_=in_, func=AF.Copy), reads=reads, writes=writes)
            else:
                fw.op("act", lambda e: e.activation(out=out, in_=in_, func=AF.Copy, scale=scale), reads=reads, writes=writes)
        else:
            if scale is None:
                fw.op("dve", lambda e: e.tensor_copy(out=out, in_=in_), reads=reads, writes=writes)
            else:
                fw.op("dve", lambda e: e.tensor_scalar(out=out, in0=in_, scalar1=scale, scalar2=None, op0=ALU.mult),
                      reads=reads, writes=writes)

    deferred = []
    bg = []
    pump_n = [1]

    def defer(fn, delay):
        deferred.append([delay, fn])

    def pump(n=1):
        for _ in range(n):
            while bg:
                try:
                    next(bg[0])
                    break
                except StopIteration:
                    bg.pop(0)

    def drain_bg():
        while bg:
            pump()

    def tick_deferred():
        ready = []
        for d in deferred:
            d[0] -= 1
            if d[0] <= 0:
                ready.append(d)
        for d in ready:
            deferred.remove(d)
            d[1]()

    def tick():
        ready = []
        for d in deferred:
            d[0] -= 1
            if d[0] <= 0:
                ready.append(d)
        for d in ready:
            deferred.remove(d)
            d[1]()
        pump(pump_n[0])

    def flush_deferred():
        while deferred:
            d = deferred.pop(0)
            d[1]()

    def mm_group(out_ap, pairs, bank_res, reads, mid_tick=False):
        n = len(pairs)
        for i, (l, r) in enumerate(pairs):
            fw.op("pe", lambda e: e.matmul(out_ap, lhsT=l, rhs=r, start=(i == 0), stop=(i == n - 1)),
                  reads=reads, writes=[bank_res], inc=(i == n - 1))
            if mid_tick and i == n // 2 - 1:
                tick()

    blocks = []

    def add_block(loads, compute):
        blocks.append((loads, compute))

    def issue_load(i):
        loads, _ = blocks[i]
        slot = i % 2
        for (co, src, n) in loads:
            fw.dma("pool", f"ring{slot}", ring[slot][:, :, co:co + n], src, writes=[("ring", slot)])

    fw.op("pool", lambda e: e.memset(identf, 0.0), writes=["identf"])
    fw.op("pool", lambda e: e.affine_select(out=identf, in_=identf, pattern=[[-1, 128]], compare_op=ALU.not_equal,
                                            fill=1.0, base=0, channel_multiplier=1), reads=["identf"], writes=["identf"])
    fw.op("pool", lambda e: e.tensor_copy(out=identb, in_=identf), reads=["identf"], writes=["identb"])
    fw.op("pool", lambda e: e.memset(Mgt, 1.0), writes=["Mgt"])
    fw.op("pool", lambda e: e.affine_select(out=Mgt, in_=Mgt, pattern=[[-1, 128]], compare_op=ALU.is_gt,
                                            fill=0.0, base=0, channel_multiplier=1), reads=["Mgt"], writes=["Mgt"])
    fw.op("pool", lambda e: e.memset(Mgt[64:128, 0:64], 0.0), reads=["Mgt"], writes=["Mgt"])
    fw.op("pool", lambda e: e.memset(Ind, 0.0), writes=["Ind"])
    fw.op("pool", lambda e: e.memset(Ind[0:64, 0:1], 1.0), reads=["Ind"], writes=["Ind"])
    fw.op("pool", lambda e: e.memset(Ind[64:128, 1:2], 1.0), reads=["Ind"], writes=["Ind"])
    def early_consts():
        fw.dma("sp", "c1", wcol_mix, norm_mix_w.rearrange("(kc p) -> p kc", p=128), writes=["wcol_mix"],
               allow_slow_non_contiguous=True)

    def mid_consts():
        fw.dma("sp", "c0", gnw_bc, gla_norm_w.partition_broadcast(128), writes=["gnw_bc"])
        fw.dma("sp", "c0b", cmask, cmask_d, writes=["cmask"])
    def late_consts():
        fw.dma("sp", "c2", wcol_mlp, norm_mlp_w.rearrange("(kc p) -> p kc", p=128), writes=["wcol_mlp"],
               allow_slow_non_contiguous=True)
        for n in range(2):
            fw.dma("sp", f"c3{n}", bgate[:, n, :], b_gate[n].rearrange("(kc p) -> p kc", p=128), writes=["bgate"],
                   allow_slow_non_contiguous=True)
        fw.dma("sp", "c4", bsp, b_spatial.rearrange("g t -> t g"), writes=["bsp"], allow_slow_non_contiguous=True)
    fw.dma("pool", "c5", wglr, w_in_v[:, :, C_GLR:C_GLR + 16], writes=["wglr"])

    xnT = V(R_A, [16, T], BF16)
    ogT = V(R_B, [16, T], BF16)
    ac = Alloc(R_C, R_E)
    S = ac([8, 512], F32)
    lb_off = ac.o
    esp = [ac([512], F32) for _ in range(2)]
    expG = [ac([512], F32) for _ in range(2)]
    Lbuf = [V(lb_off + 4096 * i, [2, 512], F32) for i in range(2)]
    w_aug_off = ac.o
    w_aug = ac([1024], F32)
    glrT = ac([T], F32)
    ae = Alloc(R_E, R_F)
    kdec = [ae([8, 256], BF16) for _ in range(4)]
    qT = ae([2, T], BF16)
    qTb = [qT, V(w_aug_off, [2, T], BF16)]
    scan_prog = {}
    sr = ae([8, 512], BF16)
    Sb = [ae([2, 512], BF16) for _ in range(2)]
    af = Alloc(R_F, R_END)
    v_off = af.o
    vbuf = [af([8, 512], BF16) for _ in range(4)]
    xt = [V(v_off + 16 * K1 + 8 * K1 * i, [D], F32) for i in range(2)]
    tmpf = [af([512], F32) for _ in range(2)]
    ogb = [af([512], BF16) for _ in range(2)]

    def w_aug_load():
        fw.dma("sp", "c6", w_aug[0:16, :], w_alpha_up, writes=["w_aug"])
        fw.dma("sp", "c7", w_aug[16:17, :], b_alpha, writes=["w_aug"])
    fw.op("dve", lambda e: e.memset(glrT[0:32, :], 1.0), writes=["glrT"])
    fw.op("dve", lambda e: e.memset(S, 0.0), writes=[("S", i) for i in range(8)])

    xt_ctr = [0]

    def norm_tile_a(t, src_rows, dstT, dst_name, src_sb=None, src_res=None):
        slot = xt_ctr[0] % len(xt)
        xt_ctr[0] += 1
        xr = ("xt", slot)
        if src_sb is None:
            fw.dma("sp", f"x{slot}", xt[slot], src_rows(t), writes=[xr])
            src = xt[slot]
            sres = [xr]
        else:
            src = src_sb(t)
            sres = [src_res(t)]
        stv, sres_st = st_slot()
        junk = dstT[:, :, t * 128:(t + 1) * 128]
        fw.op("act", lambda e: e.activation(out=junk, in_=src.rearrange("p (a b) -> p a b", a=16), func=AF.Square,
                                            accum_out=stv[:, 0:1]),
              reads=sres, writes=[(dst_name, t), sres_st])
        fw.op("act", lambda e: e.activation(out=stv[:, 1:2], in_=stv[:, 0:1], func=AF.Sqrt, scale=1.0 / D, bias=EPS),
              reads=[sres_st], writes=[sres_st])
        fw.op("dve", lambda e: e.reciprocal(out=stv[:, 2:3], in_=stv[:, 1:2]), reads=[sres_st], writes=[sres_st])
        fw.op("pool", lambda e: e.tensor_scalar(out=xt[slot], in0=src, scalar1=stv[:, 2:3], scalar2=1.0,
                                                op0=ALU.mult, op1=ALU.mult),
              reads=sres + [sres_st], writes=[xr])
        return slot

    def norm_tile_b(t, slot, wcol, wcol_res, dstT, dst_name, do_pump=True):
        xr = ("xt", slot)
        for g in range(4):
            bank, bres = mm_bank()
            for i in range(4):
                kc = 4 * g + i
                fw.op("pe", lambda e: e.transpose(out=bank[:, i * 128:(i + 1) * 128],
                                                  in_=xt[slot][:, kc * 128:(kc + 1) * 128], identity=identf),
                      reads=[xr, "identf"], writes=[bres], inc=(i == 3))
            for i in range(4):
                kc = 4 * g + i
                copy_evac(dstT[:, kc, t * 128:(t + 1) * 128], bank[:, i * 128:(i + 1) * 128],
                          reads=[bres, wcol_res], writes=[(dst_name, t)], scale=wcol[:, kc:kc + 1], eng=g)
            if do_pump:
                pump(1)

    def norm_tile(t, src_rows, wcol, wcol_res, dstT, dst_name, src_sb=None, src_res=None, do_pump=True):
        slot = norm_tile_a(t, src_rows, dstT, dst_name, src_sb, src_res)
        norm_tile_b(t, slot, wcol, wcol_res, dstT, dst_name, do_pump)

    def defer_norm_tile(t, src_rows, wcol, wcol_res, dstT, dst_name, src_sb=None, src_res=None):
        box = {}

        def a():
            box["slot"] = norm_tile_a(t, src_rows, dstT, dst_name, src_sb, src_res)

        def b():
            norm_tile_b(t, box["slot"], wcol, wcol_res, dstT, dst_name, do_pump=False)
        defer(a, 1)
        defer(b, 1 + len(xt))

    def norm_transpose(src_rows, wcol, wcol_res, dstT, dst_name, src_sb=None, src_res=None):
        la = len(xt) - 1
        slots = {}
        for t in range(min(la, NT)):
            slots[t] = norm_tile_a(t, src_rows, dstT, dst_name, src_sb, src_res)
        for t in range(NT):
            norm_tile_b(t, slots[t], wcol, wcol_res, dstT, dst_name)
            if t + la < NT:
                slots[t + la] = norm_tile_a(t + la, src_rows, dstT, dst_name, src_sb, src_res)

    def glr_compute(blk):
        for tb in range(2):
            bank, bres = mm_bank()
            mm_group(bank[0:16, :], [(wglr[:, kc, :], xnT[:, kc, tb * 512:(tb + 1) * 512]) for kc in range(KC)], bres,
                     reads=["wglr"] + [("xnT", t) for t in range(4 * tb, 4 * tb + 4)])
            fw.op("act", lambda e: e.activation(out=glrT[0:16, tb * 512:(tb + 1) * 512], in_=bank[0:16, :], func=AF.Copy),
                  reads=[bres], writes=["glrT"])

    cc = nc.alloc_semaphore("cc_sem")
    fw.sems["cc"] = cc
    fw.dma_cnt["cc"] = 0
    rg = [[0, 1, 2, 3], [4, 5, 6, 7]]

    def gather(src_sb, in_t, out_t, width, rd, name):
        fw.dma("sp", "ag_" + name, in_t.ap()[:, 0:width], src_sb, reads=rd, writes=["agin_" + name])
        for d in fw._deps(["agin_" + name], ["agout_" + name]):
            fw._wait("pool", d)
        nc.gpsimd.collective_compute("AllGather", ALU.bypass, replica_groups=rg,
                                     ins=[in_t.ap().opt()], outs=[out_t.ap().opt()]).then_inc(cc)
        fw.dma_cnt["cc"] += 1
        fw._mark(("cc", fw.dma_cnt["cc"]), ["agin_" + name], ["agout_" + name])

    lb_ctr = [0]

    def start_state(h):
        if h == 0:
            agoutP = agoutP_t.ap()
            for i in range(3):
                fw.dma("sp", f"pg{i}", Pg[i], agoutP[i * 128:(i + 1) * 128, 0:8], reads=["agout_P"], writes=[("Pg", i)])
                fw.op("dve", lambda e: e.tensor_scalar(out=acoef[i], in0=Pg[i], scalar1=-1.0, scalar2=cmask[:, i:i + 1],
                                                       op0=ALU.add, op1=ALU.mult),
                      reads=[("Pg", i), "cmask"], writes=[("acoef", i)])
                fw.op("dve", lambda e: e.tensor_scalar(out=acoef[i], in0=acoef[i], scalar1=1.0, scalar2=None, op0=ALU.add),
                      reads=[("acoef", i)], writes=[("acoef", i)])
        agoutL = agoutL_t[h].ap()
        hs = [("S", 2 * h), ("S", 2 * h + 1)]
        fw.op("dve", lambda e: e.memset(S[:, 2 * h:2 * h + 2, :], 0.0), reads=hs, writes=hs)
        for i in range(3):
            k = lb_ctr[0] % 2
            lb_ctr[0] += 1
            lb = Lbuf[k]
            lr = ("Lbuf", k)
            alias = [("esp", 0), ("esp", 1)] if k == 0 else [("expG", 0), ("expG", 1)]
            fw.dma("sp", f"lb{k}", lb.rearrange("p a b -> p (a b)"), agoutL[i * 128:(i + 1) * 128, :],
                   reads=[f"agout_L{h}"], writes=[lr] + alias)
            for j in range(2):
                sj = 2 * h + j
                fw.op("act", lambda e: e.activation(out=lb[:, j, :], in_=lb[:, j, :], func=AF.Copy, scale=cmask[:, i:i + 1]),
                      reads=[lr, "cmask"], writes=[lr])
                fw.op("dve", lambda e: e.scalar_tensor_tensor(out=S[:, sj, :], in0=S[:, sj, :], scalar=acoef[i][:, sj:sj + 1],
                                                              in1=lb[:, j, :], op0=ALU.mult, op1=ALU.add),
                      reads=[("S", sj), ("acoef", i), lr], writes=[("S", sj)])

    unit_ctr = [0]

    def make_unit(h, own):
        blk = 0
        par = h
        hp = h % 2

        def z_pair(p):
            bank, bres = mm_bank()
            for i in range(2):
                t = 2 * p + i
                fw.op("pe", lambda e: e.matmul(bank[:, i * 256:(i + 1) * 256], lhsT=glrT[0:17, t * 128:(t + 1) * 128],
                                               rhs=w_aug[0:17, 256 * h:256 * h + 256], start=True, stop=True),
                      reads=["glrT", "w_aug"], writes=[bres], inc=(i == 1))
            er = ("esp", p % 2)
            fw.op("act", lambda e: e.activation(out=esp[p % 2], in_=bank, func=AF.Exp, scale=-1.0), reads=[bres], writes=[er])
            fw.op("act", lambda e: e.activation(out=esp[p % 2], in_=esp[p % 2], func=AF.Ln, bias=1.0), reads=[er], writes=[er])

        def g_pair(p):
            er = ("esp", p % 2)
            bank, bres = mm_bank()
            for i in range(2):
                fw.op("pe", lambda e: e.matmul(bank[:, i * 256:(i + 1) * 256], lhsT=Mgt, rhs=esp[p % 2][:, i * 256:(i + 1) * 256],
                                               start=True, stop=True),
                      reads=["Mgt", er], writes=[bres], inc=(i == 1))
            fw.op("act", lambda e: e.activation(out=expG[p % 2], in_=bank, func=AF.Exp, scale=-1.0 / 16),
                  reads=[bres], writes=[("expG", p % 2)])
            for i in range(2):
                t = 2 * p + i
                for j in range(2):
                    c0 = hp * 32 + j * 16 + 2 * t
                    last = (i == 1 and j == 1)
                    fw.op("pe", lambda e: e.matmul(csum_ps[:, c0:c0 + 2],
                                                   lhsT=esp[p % 2][:, i * 256 + j * 128:i * 256 + (j + 1) * 128], rhs=Ind,
                                                   start=True, stop=True),
                          reads=[er, "Ind"], writes=[("ps", SPB)], inc=last)

        def k_pair(p, slot):
            bank, bres = mm_bank()
            for i in range(2):
                t = 2 * p + i
                mm_group(bank[:, i * 256:(i + 1) * 256],
                         [(xnT[:, kc, t * 128:(t + 1) * 128], ring[slot][:, kc, 0:256]) for kc in range(KC)], bres,
                         reads=[("xnT", t), ("ring", slot)])
                tick()
            fw.op("dve", lambda e: e.tensor_tensor(out=kdec[par][:, 2 * p:2 * p + 2, :].rearrange("p a b -> p (a b)"),
                                                   in0=bank, in1=expG[p % 2], op=ALU.mult),
                  reads=[bres, ("expG", p % 2)], writes=[("kdec", par, 2 * p), ("kdec", par, 2 * p + 1)])

        def k_compute(slot):
            if h == 0:
                glr_compute(0)
            z_pair(0)
            z_pair(1)
            for p in range(4):
                g_pair(p)
                k_pair(p, slot)
                if p + 2 < 4:
                    z_pair(p + 2)
            fw.op("act", lambda e: e.activation(out=dch[par], in_=csum_ps[:, hp * 32:hp * 32 + 32], func=AF.Exp,
                                                scale=-1.0 / 16),
                  reads=[("ps", SPB)], writes=[("dch", par)])
            fw.op("dve", lambda e: e.tensor_reduce(out=csr[:, 2 * h:2 * h + 2],
                                                   in_=csum_ps[:, hp * 32:hp * 32 + 32].rearrange("p (a b) -> p a b", a=2),
                                                   axis=mybir.AxisListType.X, op=ALU.add),
                  reads=[("ps", SPB)], writes=["csr"])
            fw.op("act", lambda e: e.activation(out=Ptile[:, 2 * h:2 * h + 2], in_=csr[:, 2 * h:2 * h + 2], func=AF.Exp,
                                                scale=-1.0 / 16),
                  reads=["csr"], writes=["Ptile"])
            if h == 3:
                gather(Ptile, aginP_t, agoutP_t, 8, ["Ptile"], "P")
            bg.append(scan_gen())

        def v_compute(slot):
            for t in range(NT):
                if h == 0:
                    if t + 1 < NT:
                        norm_tile_b(t + 1, nslots[t + 1], wcol_mix, "wcol_mix", xnT, "xnT", do_pump=False)
                    if t + 2 < NT:
                        nslots[t + 2] = norm_tile_a(t + 2, xsrc, xnT, "xnT")
                bank, bres = mm_bank()
                mm_group(bank, [(xnT[:, kc, t * 128:(t + 1) * 128], ring[slot][:, kc, :]) for kc in range(KC)], bres,
                         reads=[("xnT", t), ("ring", slot)])
                copy_evac(vbuf[par][:, t, :], bank, reads=[bres], writes=[("v", par, t)])
                tick()

        def q_compute(slot):
            pump_n[0] = 1
            if h == 0:
                drain_bg()
            while bg and scan_prog.get(h - 2, 99) < 99:
                pump()
            for ct in range(2):
                for tb in range(2):
                    bank, bres = mm_bank()
                    mm_group(bank, [(ring[slot][:, kc, ct * 128:(ct + 1) * 128], xnT[:, kc, tb * 512:(tb + 1) * 512])
                                    for kc in range(KC)], bres,
                             reads=[("ring", slot)] + [("xnT", t) for t in range(4 * tb, 4 * tb + 4)], mid_tick=True)
                    fw.op("act", lambda e: e.activation(out=qTb[h % 2][:, ct, tb * 512:(tb + 1) * 512], in_=bank, func=AF.Copy,
                                                        scale=1.0 / 16),
                          reads=[bres], writes=[("qT", h % 2)])
                    tick()

        def r_compute(slot):
            for t in range(NT):
                while bg and scan_prog.get(h - 1, 99) < 2 * t + 3:
                    pump()
                bank, bres = mm_bank()
                mm_group(bank, [(xnT[:, kc, t * 128:(t + 1) * 128], ring[slot][:, kc, :]) for kc in range(KC)], bres,
                         reads=[("xnT", t), ("ring", slot)], mid_tick=True)
                fw.op("act", lambda e: e.activation(out=sr[:, t, :], in_=bank, func=AF.Silu), reads=[bres], writes=[("sr", t)])
                tick()
            start_state(h)
            scan_prog[h] = 0
            bg.append(scan_gen())

        def o_post_a(t):
            ob = ps[OB]
            k = t % 2
            stv, sres_st = st_slot()
            fw.op("act", lambda e: e.activation(out=ogb[k], in_=ob, func=AF.Square, accum_out=stv[:, 0:1]),
                  reads=[("ps", OB)], writes=[("ogb", k), sres_st])
            fw.op("act", lambda e: e.activation(out=stv[:, 1:2], in_=stv[:, 0:1], func=AF.Sqrt, scale=1.0 / 512, bias=EPS),
                  reads=[sres_st], writes=[sres_st])
            fw.op("dve", lambda e: e.reciprocal(out=stv[:, 2:3], in_=stv[:, 1:2]), reads=[sres_st], writes=[sres_st])
            fw.op("dve", lambda e: e.scalar_tensor_tensor(out=tmpf[k], in0=ob, scalar=stv[:, 2:3], in1=gnw_bc,
                                                          op0=ALU.mult, op1=ALU.mult),
                  reads=[("ps", OB), sres_st, "gnw_bc"], writes=[("tmpf", k)])
            fw.op("dve", lambda e: e.tensor_tensor(out=ogb[k], in0=tmpf[k], in1=sr[:, t, :], op=ALU.mult),
                  reads=[("tmpf", k), ("sr", t)], writes=[("ogb", k)])

        def o_post_b(t):
            k = t % 2
            for i in range(4):
                fw.op("pe", lambda e: e.transpose(out=misc_bf[:, i * 128:(i + 1) * 128], in_=ogb[k][:, i * 128:(i + 1) * 128],
                                                  identity=identb),
                      reads=[("ogb", k), "identb"], writes=[("ps", MISC)], inc=(i == 3))
            copy_evac(ogT[:, 4 * h:4 * h + 4, t * 128:(t + 1) * 128], misc_bf.rearrange("p (a b) -> p a b", a=4),
                      reads=[("ps", MISC)], writes=[("ogT", t)])

        def o_mm(c):
            r0 = 64 * (c % 2)
            for j in range(2):
                fw.op("pe", lambda e: e.matmul(ps[OB][r0:r0 + 64, :], lhsT=qTb[h % 2][:, j, c * 64:(c + 1) * 64],
                                               rhs=Sb[c % 2][:, j, :], start=(j == 0), stop=(j == 1)),
                      reads=[("qT", h % 2), ("Sb", c % 2, j)], writes=[("ps", OB)], inc=(j == 1))

        def scan_gen():
            for c in range(16 + (3 if own else 0)):
                if c < 16:
                    t, half = c // 2, c % 2
                    r0 = 64 * half
                    for j in range(2):
                        kb = KV[j]
                        fw.op("pe", lambda e: e.matmul(ps[kb], lhsT=kdec[par][r0:r0 + 64, t, j * 128:(j + 1) * 128],
                                                       rhs=vbuf[par][r0:r0 + 64, t, :], start=True, stop=True),
                              reads=[("kdec", par, t), ("v", par, t)], writes=[("ps", kb)])
                        sj = 2 * h + j
                        fw.op("dve", lambda e: e.scalar_tensor_tensor(out=S[:, sj, :], in0=S[:, sj, :],
                                                                      scalar=dch[par][:, j * 16 + c:j * 16 + c + 1], in1=ps[kb],
                                                                      op0=ALU.mult, op1=ALU.add),
                              reads=[("S", sj), ("dch", par), ("ps", kb)], writes=[("S", sj)])
                        if own:
                            fw.op("act", lambda e: e.activation(out=Sb[c % 2][:, j, :], in_=S[:, sj, :], func=AF.Copy),
                                  reads=[("S", sj)], writes=[("Sb", c % 2, j)])
                if own:
                    if c >= 3 and (c - 3) % 2 == 1:
                        o_post_b((c - 3) // 2)
                    if 1 <= c <= 16:
                        o_mm(c - 1)
                        if (c - 1) % 2 == 1:
                            o_post_a((c - 1) // 2)
                    scan_prog[h] = c + 1
                yield
            if own:
                scan_prog[h] = 99
            if not own:
                gather(S[:, 2 * h:2 * h + 2, :].rearrange("p a b -> p (a b)"), aginL_t[h], agoutL_t[h], 1024,
                       [("S", 2 * h), ("S", 2 * h + 1)], f"L{h}")
                yield

        if not own:
            add_block([(0, w_in_v[:, :, C_V + 512 * h:C_V + 512 * h + 512], 512)], v_compute)
            add_block([(0, w_in_v[:, :, C_K + 256 * h:C_K + 256 * h + 256], 256)], k_compute)
        else:
            add_block([(0, w_in_v[:, :, C_Q + 256 * h:C_Q + 256 * h + 256], 256)], q_compute)
            add_block([(0, w_in_v[:, :, C_R + 512 * h:C_R + 512 * h + 512], 512)], r_compute)

    stage_hooks = {}

    def hook_before(fn):
        stage_hooks[len(blocks)] = fn

    nslots = {}

    def xsrc(t):
        return x_ext[t * 128:(t + 1) * 128, :]

    def pre0():
        nslots[0] = norm_tile_a(0, xsrc, xnT, "xnT")
        nslots[1] = norm_tile_a(1, xsrc, xnT, "xnT")
        early_consts()
        norm_tile_b(0, nslots[0], wcol_mix, "wcol_mix", xnT, "xnT", do_pump=False)
        w_aug_load()
        mid_consts()
        late_consts()
    hook_before(pre0)
    for h in range(4):
        make_unit(h, False)

    for h in range(4):
        make_unit(h, True)

    omT = V(R_C, [16, T], BF16)
    ae2 = Alloc(R_E, R_F)
    gub = [ae2([8, 512], BF16) for _ in range(2)]
    vln = [ae2([8, 512], BF16) for _ in range(2)]
    af2 = Alloc(R_F, R_END)
    gvf = [af2([512], F32) for _ in range(2)]
    omb = [af2([512], BF16) for _ in range(4)]
    WsT = af2([8, 128], BF16)
    wsf = af2([8, 128], F32)
    lnw_bc = af2([256], F32)
    lnb_bc = af2([256], F32)

    def stage2_pre():
        pump_n[0] = 1
        drain_bg()
        flush_deferred()
        fw.barrier()
        fw.dma("sp", "c8", lnw_bc, gmlp_ln_w.partition_broadcast(128), writes=["lnw_bc"])
        fw.dma("sp", "c9", lnb_bc, gmlp_ln_b.partition_broadcast(128), writes=["lnb_bc"])
        fw.dma("sp", "c10", wsf, w_spatial.rearrange("g t s -> t g s"), writes=["wsf"])
        for g in range(8):
            bank, bres = mm_bank()
            fw.op("pe", lambda e: e.transpose(out=bank[:, 0:128], in_=wsf[:, g, :], identity=identf),
                  reads=["wsf", "identf"], writes=[bres])
            fw.op("dve", lambda e: e.tensor_copy(out=WsT[:, g, :], in_=bank[:, 0:128]), reads=[bres], writes=["WsT"])
        fw.op("dve", lambda e: e.memset(WsT[64:128, :, 0:64], 0.0), reads=["WsT"], writes=["WsT"])

    hook_before(stage2_pre)

    def make_gmlp(j):
        par = j % 2

        def gu_compute(slot):
            for t in range(NT):
                bank, bres = mm_bank()
                mm_group(bank, [(xnT[:, kc, t * 128:(t + 1) * 128], ring[slot][:, kc, :]) for kc in range(KC)], bres,
                         reads=[("xnT", t), ("ring", slot)])
                fw.op("act", lambda e: e.activation(out=gub[par][:, t, :], in_=bank, func=AF.Gelu),
                      reads=[bres], writes=[("gub", par, t)])
                tick()

        def spatial_mm(t):
            k = t % 4
            sb = [KV[0], KV[1], SPB][t % 3]
            for g in range(2):
                gg = 2 * j + g
                fw.op("pe", lambda e: e.matmul(ps[sb][:, g * 256:(g + 1) * 256], lhsT=WsT[:, gg, :],
                                               rhs=vln[par][:, t, g * 256:(g + 1) * 256], start=True, stop=True),
                      reads=["WsT", ("vln", par, t)], writes=[("ps", sb)], inc=(g == 1))
            for g in range(2):
                gg = 2 * j + g
                fw.op("dve", lambda e: e.scalar_tensor_tensor(out=omb[k][:, g * 256:(g + 1) * 256],
                                                              in0=ps[sb][:, g * 256:(g + 1) * 256],
                                                              scalar=bsp[:, gg:gg + 1],
                                                              in1=gub[par][:, t, g * 256:(g + 1) * 256],
                                                              op0=ALU.add, op1=ALU.mult),
                      reads=[("ps", sb), "bsp", ("gub", par, t)], writes=[("omb", k)])

        def spatial_tr(t):
            k = t % 4
            tb_i = [MISC, OB][t % 2]
            tbf = ps[tb_i][:, 0:256].bitcast(BF16)
            for i in range(4):
                fw.op("pe", lambda e: e.transpose(out=tbf[:, i * 128:(i + 1) * 128], in_=omb[k][:, i * 128:(i + 1) * 128],
                                                  identity=identb),
                      reads=[("omb", k), "identb"], writes=[("ps", tb_i)], inc=(i == 3))
            copy_evac(omT[:, 4 * j:4 * j + 4, t * 128:(t + 1) * 128], tbf.rearrange("p (a b) -> p a b", a=4),
                      reads=[("ps", tb_i)], writes=[("omT", t)], eng=0)

        def gv_compute(slot):
            for t in range(NT):
                bank, bres = mm_bank()
                mm_group(bank, [(xnT[:, kc, t * 128:(t + 1) * 128], ring[slot][:, kc, :]) for kc in range(KC)], bres,
                         reads=[("xnT", t), ("ring", slot)])
                k = t % 2
                tick_deferred()
                fw.op("act", lambda e: e.activation(out=gvf[k], in_=bank, func=AF.Gelu), reads=[bres], writes=[("gvf", k)])
                for g in range(2):
                    sl = gvf[k][:, g * 256:(g + 1) * 256]
                    stv, sres_st = st_slot()
                    stv2, sres_st2 = st_slot()
                    fw.op("dve", lambda e: e.bn_stats(out=bn6[k][:, g * 6:(g + 1) * 6], in_=sl),
                          reads=[("gvf", k)], writes=[("bn6", k, g)])
                    fw.op("dve", lambda e: e.bn_aggr(out=stv[:, 0:2], in_=bn6[k][:, g * 6:(g + 1) * 6]),
                          reads=[("bn6", k, g)], writes=[sres_st])
                    fw.op("act", lambda e: e.activation(out=stv2[:, 0:1], in_=stv[:, 1:2], func=AF.Sqrt, bias=EPS),
                          reads=[sres_st], writes=[sres_st2])
                    fw.op("dve", lambda e: e.reciprocal(out=stv2[:, 1:2], in_=stv2[:, 0:1]), reads=[sres_st2], writes=[sres_st2])
                    fw.op("dve", lambda e: e.tensor_scalar(out=sl, in0=sl, scalar1=stv[:, 0:1], scalar2=stv2[:, 1:2],
                                                           op0=ALU.subtract, op1=ALU.mult),
                          reads=[("gvf", k), sres_st, sres_st2], writes=[("gvf", k)])
                    fw.op("dve", lambda e: e.tensor_tensor(out=sl, in0=sl, in1=lnw_bc, op=ALU.mult),
                          reads=[("gvf", k), "lnw_bc"], writes=[("gvf", k)])
                    fw.op("pool", lambda e: e.tensor_tensor(out=vln[par][:, t, g * 256:(g + 1) * 256], in0=sl, in1=lnb_bc, op=ALU.add),
                          reads=[("gvf", k), "lnb_bc"], writes=[("vln", par, t)])
                defer(lambda t=t: spatial_mm(t), 3)
                defer(lambda t=t: spatial_tr(t), 5)
                pump(pump_n[0])

        add_block([(0, w_in_v[:, :, C_GU + 512 * j:C_GU + 512 * j + 512], 512)], gu_compute)
        add_block([(0, w_in_v[:, :, C_GV + 512 * j:C_GV + 512 * j + 512], 512)], gv_compute)

    bn6 = [af2([12], F32) for _ in range(2)]
    for j in range(4):
        make_gmlp(j)

    mixT = V(R_E, [16, T], BF16)
    af3 = Alloc(R_F, R_END)
    gt = [af3([4, T], BF16) for _ in range(2)]
    m0 = af3([4, T], F32)
    tmp3 = [af3([512], F32) for _ in range(2)]

    def stage3_pre():
        flush_deferred()
        drain_bg()
        fw.barrier()
        if dbg:
            fw.dma("sp", "dbg", dbg_out["d_ogT"], ogT.rearrange("p a b -> p (a b)"), reads=[("ogT", t) for t in range(NT)])
            fw.dma("sp", "dbg", dbg_out["d_omT"], omT.rearrange("p a b -> p (a b)"), reads=[("omT", t) for t in range(NT)])

    hook_before(stage3_pre)

    def make_stage3(j):
        def fm_groups(slot, actT, act_name, evac):
            for ct in range(4):
                for tb in range(2):
                    bank, bres = mm_bank()
                    mm_group(bank, [(ring[slot][:, kc, ct * 128:(ct + 1) * 128], actT[:, kc, tb * 512:(tb + 1) * 512])
                                    for kc in range(KC)], bres,
                             reads=[("ring", slot)] + [(act_name, t) for t in range(4 * tb, 4 * tb + 4)])
                    evac(bank, bres, ct, tb)
                    tick()

        def gate_compute(n):
            def f(slot):
                def evac(bank, bres, ct, tb):
                    fw.op("act", lambda e: e.activation(out=gt[n][:, ct, tb * 512:(tb + 1) * 512], in_=bank, func=AF.Sigmoid,
                                                        bias=bgate[:, n, 4 * j + ct:4 * j + ct + 1]),
                          reads=[bres, "bgate"], writes=[("gt", n, ct, tb)])
                fm_groups(slot, xnT, "xnT", evac)
            return f

        def br0_compute(slot):
            def evac(bank, bres, ct, tb):
                fw.op("dve", lambda e: e.tensor_tensor(out=m0[:, ct, tb * 512:(tb + 1) * 512], in0=bank,
                                                       in1=gt[0][:, ct, tb * 512:(tb + 1) * 512], op=ALU.mult),
                      reads=[bres, ("gt", 0, ct, tb)], writes=[("m0", ct, tb)])
            fm_groups(slot, ogT, "ogT", evac)

        def br1_compute(slot):
            def evac(bank, bres, ct, tb):
                k = (ct * 2 + tb) % 2
                fw.op("dve", lambda e: e.tensor_tensor(out=tmp3[k], in0=bank, in1=gt[1][:, ct, tb * 512:(tb + 1) * 512],
                                                       op=ALU.mult),
                      reads=[bres, ("gt", 1, ct, tb)], writes=[("tmp3", k)])
                fw.op("dve", lambda e: e.tensor_tensor(out=mixT[:, 4 * j + ct, tb * 512:(tb + 1) * 512], in0=tmp3[k],
                                                       in1=m0[:, ct, tb * 512:(tb + 1) * 512], op=ALU.add),
                      reads=[("tmp3", k), ("m0", ct, tb)], writes=[("mixT", tb)])
            fm_groups(slot, omT, "omT", evac)

        add_block([(0, w_in_v[:, :, C_G0 + 512 * j:C_G0 + 512 * j + 512], 512)], gate_compute(0))
        add_block([(0, w_br_v[0][:, :, 512 * j:512 * j + 512], 512)], br0_compute)
        add_block([(0, w_in_v[:, :, C_G1 + 512 * j:C_G1 + 512 * j + 512], 512)], gate_compute(1))
        add_block([(0, w_br_v[1][:, :, 512 * j:512 * j + 512], 512)], br1_compute)

    for j in range(4):
        make_stage3(j)

    hbuf = V(R_A, [NT, D], F32)
    hnT = V(R_C, [16, T], BF16)

    def stage4_pre():
        fw.barrier()
        xt[0], xt[1] = xt5[0], xt5[1]
        if dbg:
            fw.dma("sp", "dbg", dbg_out["d_mixT"], mixT.rearrange("p a b -> p (a b)"), reads=[("mixT", tb) for tb in range(2)])
        for t in range(NT):
            fw.dma("sp", f"hx{t}", hbuf[:, t, :], x_ext[t * 128:(t + 1) * 128, :],
                   writes=[("h", t)])

    hook_before(stage4_pre)

    def make_wout(j):
        def compute(slot):
            for t in range(NT):
                bank, bres = mm_bank()
                mm_group(bank, [(mixT[:, kc, t * 128:(t + 1) * 128], ring[slot][:, kc, :]) for kc in range(KC)], bres,
                         reads=[("mixT", t // 4), ("ring", slot)])
                hs = hbuf[:, t, 512 * j:512 * j + 512]
                fw.op("dve", lambda e: e.tensor_tensor(out=hs, in0=hs, in1=bank, op=ALU.add),
                      reads=[bres, ("h", t)], writes=[("h", t)])
                if j == 3:
                    defer_norm_tile(t, None, wcol_mlp, "wcol_mlp", hnT, "hnT", src_sb=lambda tt: hbuf[:, tt, :],
                                    src_res=lambda tt: ("h", tt))
                tick()
        add_block([(0, w_out_v[:, :, 512 * j:512 * j + 512], 512)], compute)

    for j in range(4):
        make_wout(j)

    aT = V(R_E, [16, T], BF16)
    af5 = Alloc(R_F, R_END)
    xt5 = [af5([D], F32) for _ in range(2)]
    rtmp = [af5([512], F32) for _ in range(2)]
    wbc = af5([D], F32)
    junk6 = af5([D], BF16)

    def stage5_pre():
        flush_deferred()
        fw.dma("sp", "c11", wbc, norm_final_w.partition_broadcast(128), writes=["wbc"])
        if dbg:
            fw.dma("sp", "dbg", dbg_out["d_hnT"], hnT.rearrange("p a b -> p (a b)"), reads=[("hnT", t) for t in range(NT)])
            fw.dma("sp", "dbg", dbg_out["d_h1"], hbuf.rearrange("p a b -> p (a b)"), reads=[("h", t) for t in range(NT)])

    hook_before(stage5_pre)

    def final_tile(t):
        stv, sres_st = st_slot()
        fw.op("dve", lambda e: e.tensor_reduce(out=stv[:, 0:1], in_=ssq[:, 4 * t:4 * t + 4], axis=mybir.AxisListType.X, op=ALU.add),
              reads=[("ssq", t, j) for j in range(4)], writes=[sres_st])
        fw.op("act", lambda e: e.activation(out=stv[:, 1:2], in_=stv[:, 0:1], func=AF.Sqrt, scale=1.0 / D, bias=EPS),
              reads=[sres_st], writes=[sres_st])
        fw.op("dve", lambda e: e.reciprocal(out=stv[:, 2:3], in_=stv[:, 1:2]), reads=[sres_st], writes=[sres_st])
        fw.op("dve", lambda e: e.scalar_tensor_tensor(out=hbuf[:, t, 0:1024], in0=hbuf[:, t, 0:1024], scalar=stv[:, 2:3],
                                                      in1=wbc[:, 0:1024], op0=ALU.mult, op1=ALU.mult),
              reads=[("h", t), sres_st, "wbc"], writes=[("hA", t)])
        fw.op("pool", lambda e: e.tensor_scalar(out=hbuf[:, t, 1024:2048], in0=hbuf[:, t, 1024:2048], scalar1=stv[:, 2:3],
                                                scalar2=1.0, op0=ALU.mult, op1=ALU.mult),
              reads=[("h", t), sres_st], writes=[("hB", t)])
        fw.op("pool", lambda e: e.tensor_tensor(out=hbuf[:, t, 1024:2048], in0=hbuf[:, t, 1024:2048], in1=wbc[:, 1024:2048],
                                                op=ALU.mult),
              reads=[("hB", t), "wbc"], writes=[("hB", t)])
        fw.dma("sp", f"yo{t}", y[t * 128:(t + 1) * 128, :], hbuf[:, t, :], reads=[("h", t), ("hA", t), ("hB", t)], writes=[("y", t)])

    def make_ffn(i):
        def up_compute(u):
            def f(slot):
                for ct in range(4):
                    for tb in range(2):
                        bank, bres = mm_bank()
                        mm_group(bank, [(ring[slot][:, kc, ct * 128:(ct + 1) * 128], hnT[:, kc, tb * 512:(tb + 1) * 512])
                                        for kc in range(KC)], bres,
                                 reads=[("ring", slot)] + [("hnT", t) for t in range(4 * tb, 4 * tb + 4)])
                        k = (ct * 2 + tb) % 2
                        fw.op("act", lambda e: e.activation(out=rtmp[k], in_=bank, func=AF.Relu), reads=[bres], writes=[("rtmp", k)])
                        fw.op("dve", lambda e: e.tensor_tensor(out=aT[:, 4 * u + ct, tb * 512:(tb + 1) * 512], in0=rtmp[k],
                                                               in1=rtmp[k], op=ALU.mult),
                              reads=[("rtmp", k)], writes=[("aT", tb)])
                        tick()
            return f

        def down_compute(j):
            def f(slot):
                for t in range(NT):
                    bank, bres = mm_bank()
                    mm_group(bank, [(aT[:, kc, t * 128:(t + 1) * 128], ring[slot][:, kc, :]) for kc in range(KC)], bres,
                             reads=[("aT", t // 4), ("ring", slot)])
                    hs = hbuf[:, t, 512 * j:512 * j + 512]
                    fw.op("dve", lambda e: e.tensor_tensor(out=hs, in0=hs, in1=bank, op=ALU.add),
                          reads=[bres, ("h", t)], writes=[("h", t)])
                    if i == 3:
                        fw.op("act", lambda e: e.activation(out=junk6[:, 0:512], in_=hs, func=AF.Square,
                                                            accum_out=ssq[:, 4 * t + j:4 * t + j + 1]),
                              reads=[("h", t)], writes=["junk6", ("ssq", t, j)])
                    if i == 3 and j == 3:
                        defer(lambda t=t: final_tile(t), 1)
                    tick()
            return f

        for u in range(4):
            c0 = 2048 * i + 512 * u
            add_block([(0, w_up_v[:, :, c0:c0 + 512], 512)], up_compute(u))
        for j in range(4):
            add_block([(0, w_dn_v[i][:, :, 512 * j:512 * j + 512], 512)], down_compute(j))

    for i in range(4):
        make_ffn(i)

    n = len(blocks)
    if max_blocks is not None:
        n = min(n, max_blocks)
    issue_load(0)
    for i in range(n):
        if i in stage_hooks:
            stage_hooks[i]()
        if i + 1 < n:
            issue_load(i + 1)
        blocks[i][1](i % 2)

    drain_bg()
    flush_deferred()
    fw.wait_all_dma("sp", "yo")
    fw.wait_all_dma("sp", "dbg")
    return nc


_CACHE = {}


def _prep_inputs(inputs):
    x = np.ascontiguousarray(inputs["x"], dtype=np.float32)
    B, Sq, _ = x.shape
    per_seq = Sq // T
    shared = {
        "norm_mix_w": inputs["norm_mix_w"].reshape(D),
        "w_in": inputs["w_in"].reshape(D, DIN),
        "w_alpha_up": inputs["w_alpha_up"].reshape(16, 1024),
        "b_alpha": inputs["b_alpha"].reshape(1, 1024),
        "gla_norm_w": inputs["gla_norm_w"].reshape(512),
        "gmlp_ln_w": inputs["gmlp_ln_w"].reshape(256),
        "gmlp_ln_b": inputs["gmlp_ln_b"].reshape(256),
        "w_spatial": inputs["w_spatial"].reshape(8, 128, 128),
        "b_spatial": inputs["b_spatial"].reshape(8, 128),
        "b_gate": inputs["b_gate"].reshape(2, D),
        "w_branch": inputs["w_branch"].reshape(2, D, D),
        "w_out": inputs["w_out"].reshape(D, D),
        "norm_mlp_w": inputs["norm_mlp_w"].reshape(D),
        "w_ff_up": inputs["w_ff_up"].reshape(D, DFF),
        "w_ff_down": inputs["w_ff_down"].reshape(DFF, D),
        "norm_final_w": inputs["norm_final_w"].reshape(D),
    }
    shared = {k: np.ascontiguousarray(v, dtype=np.float32) for k, v in shared.items()}
    in_maps = []
    for c in range(8):
        b, j = c // per_seq, c % per_seq
        m = dict(shared)
        m["x_ext"] = np.ascontiguousarray(x[b, j * T:(j + 1) * T])
        cm = np.zeros((128, 4), np.float32)
        cm[:, :j] = 1.0
        m["cmask"] = cm
        in_maps.append(m)
    return in_maps, B, Sq


def kernel(**inputs):
    in_maps, B, Sq = _prep_inputs(inputs)
    if "nc" not in _CACHE:
        _CACHE["nc"] = build_program()
    res = run_bass_kernel_spmd(_CACHE["nc"], in_maps, core_ids=list(range(8)))
    per_seq = Sq // T
    out = np.empty((B, Sq, D), np.float32)
    for c in range(8):
        b, j = c // per_seq, c % per_seq
        out[b, j * T:(j + 1) * T] = res.results[c]["y"]
    return out
```

```python
import numpy as np
import concourse.bass as bass
import concourse.mybir as mybir
from concourse.bass_utils import run_bass_kernel_spmd

F32 = mybir.dt.float32
BF16 = mybir.dt.bfloat16
AF = mybir.ActivationFunctionType
ALU = mybir.AluOpType

D = 2048
KC = 16
T = 1024
NT = 8
DIN = 14352
DFF = 8192
EPS = 1e-6
NBLK = 1
C_Q, C_K, C_V, C_R, C_GLR, C_GU, C_GV, C_G0, C_G1 = 0, 1024, 2048, 4096, 6144, 6160, 8208, 10256, 12304


class Res:
    __slots__ = ("w", "r")

    def __init__(self):
        self.w = None
        self.r = {}


class FW:
    def __init__(self, nc):
        self.nc = nc
        self.eng = {"pe": nc.tensor, "act": nc.scalar, "dve": nc.vector, "pool": nc.gpsimd, "sp": nc.sync}
        self.sems = {k: nc.alloc_semaphore("s_" + k) for k in self.eng}
        self.cnt = {k: 0 for k in self.eng}
        self.waited = {k: {} for k in self.eng}
        self.dma_cnt = {}
        self.same_engine_sync = {"act", "dve", "pool"}
        self.res = {}

    def R(self, name):
        r = self.res.get(name)
        if r is None:
            r = self.res[name] = Res()
        return r

    def _wait(self, e, dep):
        key, val = dep
        if key == e and e not in self.same_engine_sync:
            return
        if self.waited[e].get(key, 0) >= val:
            return
        self.eng[e].wait_ge(self.sems[key], val)
        self.waited[e][key] = val

    def _deps(self, reads, writes):
        deps = []
        for r in reads:
            r = self.R(r)
            if r.w:
                deps.append(r.w)
        for w in writes:
            w = self.R(w)
            if w.w:
                deps.append(w.w)
            deps.extend(w.r.items())
        return deps

    def _mark(self, ev, reads, writes):
        for r in reads:
            self.R(r).r[ev[0]] = ev[1]
        for w in writes:
            w = self.R(w)
            w.w = ev
            w.r = {}

    @staticmethod
    def _excl(reads, writes):
        r2 = [r for r in reads if not (isinstance(r, tuple) and r[0] == "ps")]
        if len(r2) != len(reads):
            writes = list(writes) + [r for r in reads if isinstance(r, tuple) and r[0] == "ps" and r not in writes]
        return r2, writes

    def op(self, e, fn, reads=(), writes=(), inc=True):
        reads, writes = self._excl(reads, writes)
        for d in self._deps(reads, writes):
            self._wait(e, d)
        ins = fn(self.eng[e])
        if inc:
            ins.then_inc(self.sems[e], 1)
            self.cnt[e] += 1
            ev = (e, self.cnt[e])
        else:
            ev = (e, self.cnt[e] + 1)
        self._mark(ev, reads, writes)
        return ins

    def dma(self, q, semkey, out, in_, reads=(), writes=(), **kw):
        if semkey not in self.sems:
            self.sems[semkey] = self.nc.alloc_semaphore("d_" + semkey)
            self.dma_cnt[semkey] = 0
        for d in self._deps(reads, writes):
            self._wait(q, d)
        ins = self.eng[q].dma_start(out=out, in_=in_, **kw)
        ins.then_inc(self.sems[semkey], 16)
        self.dma_cnt[semkey] += 16
        self._mark((semkey, self.dma_cnt[semkey]), reads, writes)
        return ins

    def barrier(self):
        for e in self.eng:
            for k in self.sems:
                if k == e:
                    continue
                v = self.cnt[k] if k in self.cnt else self.dma_cnt[k]
                if v > 0:
                    self._wait(e, (k, v))

    def wait_all_dma(self, e, prefix):
        for k, v in self.dma_cnt.items():
            if k.startswith(prefix) and v > 0:
                self._wait(e, (k, v))


def build_program(dbg=False, max_blocks=None):
    nc = bass.Bass(target_bir_lowering=False)
    fw = FW(nc)

    def din(name, shape):
        return nc.dram_tensor(name, list(shape), F32, kind="ExternalInput").ap()

    x_ext = din("x_ext", [NBLK * T, D])
    cmask_d = din("cmask", [128, 4])
    norm_mix_w = din("norm_mix_w", [D])
    w_in = din("w_in", [D, DIN])
    w_alpha_up = din("w_alpha_up", [16, 1024])
    b_alpha = din("b_alpha", [1, 1024])
    gla_norm_w = din("gla_norm_w", [512])
    gmlp_ln_w = din("gmlp_ln_w", [256])
    gmlp_ln_b = din("gmlp_ln_b", [256])
    w_spatial = din("w_spatial", [8, 128, 128])
    b_spatial = din("b_spatial", [8, 128])
    b_gate = din("b_gate", [2, D])
    w_branch = din("w_branch", [2, D, D])
    w_out = din("w_out", [D, D])
    norm_mlp_w = din("norm_mlp_w", [D])
    w_ff_up = din("w_ff_up", [D, DFF])
    w_ff_down = din("w_ff_down", [DFF, D])
    norm_final_w = din("norm_final_w", [D])
    y = nc.dram_tensor("y", [T, D], F32, kind="ExternalOutput").ap()
    aginP_t = nc.dram_tensor("aginP", [128, 512], F32)
    agoutP_t = nc.dram_tensor("agoutP", [4 * 128, 512], F32)
    aginL_t = [nc.dram_tensor(f"aginL{h}", [128, 1024], F32) for h in range(4)]
    agoutL_t = [nc.dram_tensor(f"agoutL{h}", [4 * 128, 1024], F32) for h in range(4)]
    dbg_out = {}
    if dbg:
        for nm in ("d_ogT", "d_omT", "d_mixT", "d_hnT"):
            dbg_out[nm] = nc.dram_tensor(nm, [128, KC * T], BF16, kind="ExternalOutput").ap()
        dbg_out["d_h1"] = nc.dram_tensor("d_h1", [128, NT * D], F32, kind="ExternalOutput").ap()

    w_in_v = w_in.rearrange("(kc p) n -> p kc n", p=128)
    w_br_v = [w_branch[n].rearrange("(kc p) n -> p kc n", p=128) for n in range(2)]
    w_out_v = w_out.rearrange("(kc p) n -> p kc n", p=128)
    w_up_v = w_ff_up.rearrange("(kc p) n -> p kc n", p=128)
    w_dn_v = [w_ff_down[2048 * i:2048 * (i + 1), :].rearrange("(kc p) n -> p kc n", p=128) for i in range(4)]

    K1 = 1024
    ARENA = 206 * K1
    arena = nc.alloc_sbuf_tensor("arena", [128, ARENA // 2], BF16)

    def V(off, shape, dt, parts=None):
        esz = 2 if dt == BF16 else 4
        n = int(np.prod(shape)) * esz
        assert off % 4 == 0 and off + n <= ARENA, (off, n)
        v = arena[:, off // 2:(off + n) // 2]
        if dt != BF16:
            v = v.bitcast(dt)
        if len(shape) == 2:
            v = v.rearrange("p (a b) -> p a b", a=shape[0])
        elif len(shape) == 3:
            v = v.rearrange("p (a b c) -> p a b c", a=shape[0], b=shape[1])
        return v

    class Alloc:
        def __init__(self, base, limit):
            self.o = base
            self.limit = limit

        def __call__(self, shape, dt):
            esz = 2 if dt == BF16 else 4
            n = (int(np.prod(shape)) * esz + 31) // 32 * 32
            v = V(self.o, shape, dt)
            self.o += n
            assert self.o <= self.limit, (self.o, self.limit)
            return v

    R_CONST = 0
    R_RING = 8 * K1
    R_A = 40 * K1
    R_B = 72 * K1
    R_C = 104 * K1
    R_E = 136 * K1
    R_F = 168 * K1
    R_END = ARENA

    ca = Alloc(R_CONST, R_RING)
    identf = ca([128], F32)
    identb = ca([128], BF16)
    Mgt = ca([128], F32)
    Ind = ca([2], F32)
    gnw_bc = ca([512], F32)
    wcol_mix = ca([16], F32)
    wcol_mlp = ca([16], F32)
    bgate = ca([2, 16], F32)
    bsp = ca([8], F32)
    wglr = ca([16, 16], BF16)
    st = ca([96], F32)
    dch = [ca([32], F32) for _ in range(4)]
    cmask = ca([4], F32)
    Ptile = ca([8], F32)
    Pg = [ca([8], F32) for _ in range(3)]
    acoef = [ca([8], F32) for _ in range(3)]
    csr = ca([8], F32)
    ssq = ca([32], F32)
    neg_half = ca([1], F32)
    ones_c = ca([1], F32)
    ring = [V(R_RING + 16 * K1 * i, [16, 512], BF16) for i in range(2)]

    ps = [nc.alloc_psum_tensor(f"ps{i}", [128, 512], F32)[:, :] for i in range(8)]
    MM = [0, 1, 2]
    KV = [3, 4]
    OB = 5
    SPB = 6
    MISC = 7
    misc_bf = ps[MISC][:, 0:256].bitcast(BF16)
    csum_ps = ps[SPB][:, 0:64]

    mm_ctr = [0]

    def mm_bank():
        i = MM[mm_ctr[0] % len(MM)]
        mm_ctr[0] += 1
        return ps[i], ("ps", i)

    st_ctr = [0]

    def st_slot():
        k = st_ctr[0] % 32
        st_ctr[0] += 1
        return st[:, 3 * k:3 * k + 3], ("st", k)

    def rstd_op(src, tmp, dst, res, scale):
        fw.op("dve", lambda e: e.tensor_scalar(out=tmp, in0=src, scalar1=scale, scalar2=EPS, op0=ALU.mult, op1=ALU.add),
              reads=res, writes=res)
        fw.op("pool", lambda e: e.tensor_tensor(out=dst, in0=tmp, in1=neg_half, op=ALU.pow),
              reads=res + ["neg_half"], writes=res)

    ev_ctr = [0]

    def copy_evac(out, in_, reads, writes, scale=None, eng=None):
        if eng is None:
            ev_ctr[0] += 1
            eng = ev_ctr[0] % 2
        if eng % 2 == 0:
            if scale is None:
                fw.op("act", lambda e: e.activation(out=out, in_=in_, func=AF.Copy), reads=reads, writes=writes)
            else:
                fw.op("act", lambda e: e.activation(out=out, in_=in_, func=AF.Copy, scale=scale), reads=reads, writes=writes)
        else:
            if scale is None:
                fw.op("dve", lambda e: e.tensor_copy(out=out, in_=in_), reads=reads, writes=writes)
            else:
                fw.op("dve", lambda e: e.tensor_scalar(out=out, in0=in_, scalar1=scale, scalar2=None, op0=ALU.mult),
                      reads=reads, writes=writes)

    deferred = []
    bg = []
    pump_n = [1]

    def defer(fn, delay):
        deferred.append([delay, fn])

    def pump(n=1):
        for _ in range(n):
            while bg:
                try:
                    next(bg[0])
                    break
                except StopIteration:
                    bg.pop(0)

    def drain_bg():
        while bg:
            pump()

    def tick_deferred():
        ready = []
        for d in deferred:
            d[0] -= 1
            if d[0] <= 0:
                ready.append(d)
        for d in ready:
            deferred.remove(d)
            d[1]()

    def tick():
        ready = []
        for d in deferred:
            d[0] -= 1
            if d[0] <= 0:
                ready.append(d)
        for d in ready:
            deferred.remove(d)
            d[1]()
        pump(pump_n[0])

    def flush_deferred():
        while deferred:
            d = deferred.pop(0)
            d[1]()

    def mm_group(out_ap, pairs, bank_res, reads, mid_tick=False):
        n = len(pairs)
        for i, (l, r) in enumerate(pairs):
            fw.op("pe", lambda e: e.matmul(out_ap, lhsT=l, rhs=r, start=(i == 0), stop=(i == n - 1)),
                  reads=reads, writes=[bank_res], inc=(i == n - 1))
            if mid_tick and i == n // 2 - 1:
                tick()

    blocks = []

    def add_block(loads, compute):
        blocks.append((loads, compute))

    def issue_load(i):
        loads, _ = blocks[i]
        slot = i % 2
        for (co, src, n) in loads:
            fw.dma("pool", f"ring{slot}", ring[slot][:, :, co:co + n], src, writes=[("ring", slot)])

    fw.op("pool", lambda e: e.memset(identf, 0.0), writes=["identf"])
    fw.op("pool", lambda e: e.affine_select(out=identf, in_=identf, pattern=[[-1, 128]], compare_op=ALU.not_equal,
                                            fill=1.0, base=0, channel_multiplier=1), reads=["identf"], writes=["identf"])
    fw.op("pool", lambda e: e.tensor_copy(out=identb, in_=identf), reads=["identf"], writes=["identb"])
    fw.op("pool", lambda e: e.memset(Mgt, 1.0), writes=["Mgt"])
    fw.op("pool", lambda e: e.affine_select(out=Mgt, in_=Mgt, pattern=[[-1, 128]], compare_op=ALU.is_gt,
                                            fill=0.0, base=0, channel_multiplier=1), reads=["Mgt"], writes=["Mgt"])
    fw.op("pool", lambda e: e.memset(Mgt[64:128, 0:64], 0.0), reads=["Mgt"], writes=["Mgt"])
    fw.op("pool", lambda e: e.memset(neg_half, -0.5), writes=["neg_half"])
    fw.op("pool", lambda e: e.memset(Ind, 0.0), writes=["Ind"])
    fw.op("pool", lambda e: e.memset(Ind[0:64, 0:1], 1.0), reads=["Ind"], writes=["Ind"])
    fw.op("pool", lambda e: e.memset(Ind[64:128, 1:2], 1.0), reads=["Ind"], writes=["Ind"])
    def early_consts():
        fw.dma("sp", "c1", wcol_mix, norm_mix_w.rearrange("(kc p) -> p kc", p=128), writes=["wcol_mix"],
               allow_slow_non_contiguous=True)

    def mid_consts():
        fw.dma("sp", "c0", gnw_bc, gla_norm_w.partition_broadcast(128), writes=["gnw_bc"])
        fw.dma("sp", "c0b", cmask, cmask_d, writes=["cmask"])
    def late_consts():
        fw.dma("sp", "c2", wcol_mlp, norm_mlp_w.rearrange("(kc p) -> p kc", p=128), writes=["wcol_mlp"],
               allow_slow_non_contiguous=True)
        for n in range(2):
            fw.dma("sp", f"c3{n}", bgate[:, n, :], b_gate[n].rearrange("(kc p) -> p kc", p=128), writes=["bgate"],
                   allow_slow_non_contiguous=True)
        fw.dma("sp", "c4", bsp, b_spatial.rearrange("g t -> t g"), writes=["bsp"], allow_slow_non_contiguous=True)
    fw.dma("pool", "c5", wglr, w_in_v[:, :, C_GLR:C_GLR + 16], writes=["wglr"])

    xnT = V(R_A, [16, T], BF16)
    ogT = V(R_B, [16, T], BF16)
    ac = Alloc(R_C, R_E)
    S = ac([8, 512], F32)
    lb_off = ac.o
    esp = [ac([512], F32) for _ in range(2)]
    expG = [ac([512], F32) for _ in range(2)]
    Lbuf = [V(lb_off + 4096 * i, [2, 512], F32) for i in range(2)]
    w_aug_off = ac.o
    w_aug = ac([1024], F32)
    glrT = ac([T], F32)
    ae = Alloc(R_E, R_F)
    kdec = [ae([8, 256], BF16) for _ in range(4)]
    qT = ae([2, T], BF16)
    qTb = [qT, V(w_aug_off, [2, T], BF16)]
    scan_prog = {}
    sr = ae([8, 512], BF16)
    Sb = [ae([2, 512], BF16) for _ in range(2)]
    af = Alloc(R_F, R_END)
    v_off = af.o
    vbuf = [af([8, 512], BF16) for _ in range(4)]
    xt = [V(v_off + 16 * K1 + 8 * K1 * i, [D], F32) for i in range(2)]
    tmpf = [af([512], F32) for _ in range(2)]
    ogb = [af([512], BF16) for _ in range(2)]

    def w_aug_load():
        fw.dma("sp", "c6", w_aug[0:16, :], w_alpha_up, writes=["w_aug"])
        fw.dma("sp", "c7", w_aug[16:17, :], b_alpha, writes=["w_aug"])
    fw.op("dve", lambda e: e.memset(glrT[0:32, :], 1.0), writes=["glrT"])
    fw.op("dve", lambda e: e.memset(S, 0.0), writes=[("S", i) for i in range(8)])

    xt_ctr = [0]

    def norm_tile_a(t, src_rows, dstT, dst_name, src_sb=None, src_res=None):
        slot = xt_ctr[0] % len(xt)
        xt_ctr[0] += 1
        xr = ("xt", slot)
        if src_sb is None:
            fw.dma("sp", f"x{slot}", xt[slot], src_rows(t), writes=[xr])
            src = xt[slot]
            sres = [xr]
        else:
            src = src_sb(t)
            sres = [src_res(t)]
        stv, sres_st = st_slot()
        junk = dstT[:, :, t * 128:(t + 1) * 128]
        fw.op("act", lambda e: e.activation(out=junk, in_=src.rearrange("p (a b) -> p a b", a=16), func=AF.Square,
                                            accum_out=stv[:, 0:1]),
              reads=sres, writes=[(dst_name, t), sres_st])
        rstd_op(stv[:, 0:1], stv[:, 1:2], stv[:, 2:3], [sres_st], 1.0 / D)
        fw.op("pool", lambda e: e.tensor_scalar(out=xt[slot], in0=src, scalar1=stv[:, 2:3], scalar2=1.0,
                                                op0=ALU.mult, op1=ALU.mult),
              reads=sres + [sres_st], writes=[xr])
        return slot

    def norm_tile_b(t, slot, wcol, wcol_res, dstT, dst_name, do_pump=True):
        xr = ("xt", slot)
        for g in range(4):
            bank, bres = mm_bank()
            for i in range(4):
                kc = 4 * g + i
                fw.op("pe", lambda e: e.transpose(out=bank[:, i * 128:(i + 1) * 128],
                                                  in_=xt[slot][:, kc * 128:(kc + 1) * 128], identity=identf),
                      reads=[xr, "identf"], writes=[bres], inc=(i == 3))
            for i in range(4):
                kc = 4 * g + i
                copy_evac(dstT[:, kc, t * 128:(t + 1) * 128], bank[:, i * 128:(i + 1) * 128],
                          reads=[bres, wcol_res], writes=[(dst_name, t)], scale=wcol[:, kc:kc + 1], eng=g)
            if do_pump:
                pump(1)

    def norm_tile(t, src_rows, wcol, wcol_res, dstT, dst_name, src_sb=None, src_res=None, do_pump=True):
        slot = norm_tile_a(t, src_rows, dstT, dst_name, src_sb, src_res)
        norm_tile_b(t, slot, wcol, wcol_res, dstT, dst_name, do_pump)

    def defer_norm_tile(t, src_rows, wcol, wcol_res, dstT, dst_name, src_sb=None, src_res=None):
        box = {}

        def a():
            box["slot"] = norm_tile_a(t, src_rows, dstT, dst_name, src_sb, src_res)

        def b():
            norm_tile_b(t, box["slot"], wcol, wcol_res, dstT, dst_name, do_pump=False)
        defer(a, 1)
        defer(b, 1 + len(xt))

    def norm_transpose(src_rows, wcol, wcol_res, dstT, dst_name, src_sb=None, src_res=None):
        la = len(xt) - 1
        slots = {}
        for t in range(min(la, NT)):
            slots[t] = norm_tile_a(t, src_rows, dstT, dst_name, src_sb, src_res)
        for t in range(NT):
            norm_tile_b(t, slots[t], wcol, wcol_res, dstT, dst_name)
            if t + la < NT:
                slots[t + la] = norm_tile_a(t + la, src_rows, dstT, dst_name, src_sb, src_res)

    def glr_compute(blk):
        for tb in range(2):
            bank, bres = mm_bank()
            mm_group(bank[0:16, :], [(wglr[:, kc, :], xnT[:, kc, tb * 512:(tb + 1) * 512]) for kc in range(KC)], bres,
                     reads=["wglr"] + [("xnT", t) for t in range(4 * tb, 4 * tb + 4)])
            fw.op("act", lambda e: e.activation(out=glrT[0:16, tb * 512:(tb + 1) * 512], in_=bank[0:16, :], func=AF.Copy),
                  reads=[bres], writes=["glrT"])

    cc = nc.alloc_semaphore("cc_sem")
    fw.sems["cc"] = cc
    fw.dma_cnt["cc"] = 0
    rg = [[0, 1, 2, 3], [4, 5, 6, 7]]

    def gather(src_sb, in_t, out_t, width, rd, name):
        fw.dma("sp", "ag_" + name, in_t.ap()[:, 0:width], src_sb, reads=rd, writes=["agin_" + name])
        for d in fw._deps(["agin_" + name], ["agout_" + name]):
            fw._wait("pool", d)
        nc.gpsimd.collective_compute("AllGather", ALU.bypass, replica_groups=rg,
                                     ins=[in_t.ap().opt()], outs=[out_t.ap().opt()]).then_inc(cc)
        fw.dma_cnt["cc"] += 1
        fw._mark(("cc", fw.dma_cnt["cc"]), ["agin_" + name], ["agout_" + name])

    lb_ctr = [0]

    def start_state(h):
        if h == 0:
            agoutP = agoutP_t.ap()
            for i in range(3):
                fw.dma("sp", f"pg{i}", Pg[i], agoutP[i * 128:(i + 1) * 128, 0:8], reads=["agout_P"], writes=[("Pg", i)])
                fw.op("dve", lambda e: e.tensor_scalar(out=acoef[i], in0=Pg[i], scalar1=-1.0, scalar2=cmask[:, i:i + 1],
                                                       op0=ALU.add, op1=ALU.mult),
                      reads=[("Pg", i), "cmask"], writes=[("acoef", i)])
                fw.op("dve", lambda e: e.tensor_scalar(out=acoef[i], in0=acoef[i], scalar1=1.0, scalar2=None, op0=ALU.add),
                      reads=[("acoef", i)], writes=[("acoef", i)])
        agoutL = agoutL_t[h].ap()
        hs = [("S", 2 * h), ("S", 2 * h + 1)]
        fw.op("dve", lambda e: e.memset(S[:, 2 * h:2 * h + 2, :], 0.0), reads=hs, writes=hs)
        for i in range(3):
            k = lb_ctr[0] % 2
            lb_ctr[0] += 1
            lb = Lbuf[k]
            lr = ("Lbuf", k)
            alias = [("esp", 0), ("esp", 1)] if k == 0 else [("expG", 0), ("expG", 1)]
            fw.dma("sp", f"lb{k}", lb.rearrange("p a b -> p (a b)"), agoutL[i * 128:(i + 1) * 128, :],
                   reads=[f"agout_L{h}"], writes=[lr] + alias)
            for j in range(2):
                sj = 2 * h + j
                fw.op("act", lambda e: e.activation(out=lb[:, j, :], in_=lb[:, j, :], func=AF.Copy, scale=cmask[:, i:i + 1]),
                      reads=[lr, "cmask"], writes=[lr])
                fw.op("dve", lambda e: e.scalar_tensor_tensor(out=S[:, sj, :], in0=S[:, sj, :], scalar=acoef[i][:, sj:sj + 1],
                                                              in1=lb[:, j, :], op0=ALU.mult, op1=ALU.add),
                      reads=[("S", sj), ("acoef", i), lr], writes=[("S", sj)])

    unit_ctr = [0]

    def make_unit(h, own):
        blk = 0
        par = h
        hp = h % 2

        def z_pair(p):
            bank, bres = mm_bank()
            for i in range(2):
                t = 2 * p + i
                fw.op("pe", lambda e: e.matmul(bank[:, i * 256:(i + 1) * 256], lhsT=glrT[0:17, t * 128:(t + 1) * 128],
                                               rhs=w_aug[0:17, 256 * h:256 * h + 256], start=True, stop=True),
                      reads=["glrT", "w_aug"], writes=[bres], inc=(i == 1))
            er = ("esp", p % 2)
            fw.op("act", lambda e: e.activation(out=esp[p % 2], in_=bank, func=AF.Exp, scale=-1.0), reads=[bres], writes=[er])
            fw.op("act", lambda e: e.activation(out=esp[p % 2], in_=esp[p % 2], func=AF.Ln, bias=1.0), reads=[er], writes=[er])

        def g_pair(p):
            er = ("esp", p % 2)
            bank, bres = mm_bank()
            for i in range(2):
                fw.op("pe", lambda e: e.matmul(bank[:, i * 256:(i + 1) * 256], lhsT=Mgt, rhs=esp[p % 2][:, i * 256:(i + 1) * 256],
                                               start=True, stop=True),
                      reads=["Mgt", er], writes=[bres], inc=(i == 1))
            fw.op("act", lambda e: e.activation(out=expG[p % 2], in_=bank, func=AF.Exp, scale=-1.0 / 16),
                  reads=[bres], writes=[("expG", p % 2)])
            for i in range(2):
                t = 2 * p + i
                for j in range(2):
                    c0 = hp * 32 + j * 16 + 2 * t
                    last = (i == 1 and j == 1)
                    fw.op("pe", lambda e: e.matmul(csum_ps[:, c0:c0 + 2],
                                                   lhsT=esp[p % 2][:, i * 256 + j * 128:i * 256 + (j + 1) * 128], rhs=Ind,
                                                   start=True, stop=True),
                          reads=[er, "Ind"], writes=[("ps", SPB)], inc=last)

        def k_pair(p, slot):
            bank, bres = mm_bank()
            for i in range(2):
                t = 2 * p + i
                mm_group(bank[:, i * 256:(i + 1) * 256],
                         [(xnT[:, kc, t * 128:(t + 1) * 128], ring[slot][:, kc, 0:256]) for kc in range(KC)], bres,
                         reads=[("xnT", t), ("ring", slot)])
                tick()
            fw.op("dve", lambda e: e.tensor_tensor(out=kdec[par][:, 2 * p:2 * p + 2, :].rearrange("p a b -> p (a b)"),
                                                   in0=bank, in1=expG[p % 2], op=ALU.mult),
                  reads=[bres, ("expG", p % 2)], writes=[("kdec", par, 2 * p), ("kdec", par, 2 * p + 1)])

        def k_compute(slot):
            if h == 0:
                glr_compute(0)
            z_pair(0)
            z_pair(1)
            for p in range(4):
                g_pair(p)
                k_pair(p, slot)
                if p + 2 < 4:
                    z_pair(p + 2)
            fw.op("act", lambda e: e.activation(out=dch[par], in_=csum_ps[:, hp * 32:hp * 32 + 32], func=AF.Exp,
                                                scale=-1.0 / 16),
                  reads=[("ps", SPB)], writes=[("dch", par)])
            fw.op("dve", lambda e: e.tensor_reduce(out=csr[:, 2 * h:2 * h + 2],
                                                   in_=csum_ps[:, hp * 32:hp * 32 + 32].rearrange("p (a b) -> p a b", a=2),
                                                   axis=mybir.AxisListType.X, op=ALU.add),
                  reads=[("ps", SPB)], writes=["csr"])
            fw.op("act", lambda e: e.activation(out=Ptile[:, 2 * h:2 * h + 2], in_=csr[:, 2 * h:2 * h + 2], func=AF.Exp,
                                                scale=-1.0 / 16),
                  reads=["csr"], writes=["Ptile"])
            if h == 3:
                gather(Ptile, aginP_t, agoutP_t, 8, ["Ptile"], "P")
            bg.append(scan_gen())

        def v_compute(slot):
            for t in range(NT):
                if h == 0:
                    if t + 1 < NT:
                        norm_tile_b(t + 1, nslots[t + 1], wcol_mix, "wcol_mix", xnT, "xnT", do_pump=False)
                    if t + 2 < NT:
                        nslots[t + 2] = norm_tile_a(t + 2, xsrc, xnT, "xnT")
                bank, bres = mm_bank()
                mm_group(bank, [(xnT[:, kc, t * 128:(t + 1) * 128], ring[slot][:, kc, :]) for kc in range(KC)], bres,
                         reads=[("xnT", t), ("ring", slot)])
                copy_evac(vbuf[par][:, t, :], bank, reads=[bres], writes=[("v", par, t)])
                tick()

        def q_compute(slot):
            pump_n[0] = 1
            if h == 0:
                drain_bg()
            while bg and scan_prog.get(h - 2, 99) < 99:
                pump()
            for ct in range(2):
                for tb in range(2):
                    bank, bres = mm_bank()
                    mm_group(bank, [(ring[slot][:, kc, ct * 128:(ct + 1) * 128], xnT[:, kc, tb * 512:(tb + 1) * 512])
                                    for kc in range(KC)], bres,
                             reads=[("ring", slot)] + [("xnT", t) for t in range(4 * tb, 4 * tb + 4)], mid_tick=True)
                    fw.op("act", lambda e: e.activation(out=qTb[h % 2][:, ct, tb * 512:(tb + 1) * 512], in_=bank, func=AF.Copy,
                                                        scale=1.0 / 16),
                          reads=[bres], writes=[("qT", h % 2)])
                    tick()

        def r_compute(slot):
            for t in range(NT):
                while bg and scan_prog.get(h - 1, 99) < 2 * t + 3:
                    pump()
                bank, bres = mm_bank()
                mm_group(bank, [(xnT[:, kc, t * 128:(t + 1) * 128], ring[slot][:, kc, :]) for kc in range(KC)], bres,
                         reads=[("xnT", t), ("ring", slot)], mid_tick=True)
                fw.op("act", lambda e: e.activation(out=sr[:, t, :], in_=bank, func=AF.Silu), reads=[bres], writes=[("sr", t)])
                tick()
            start_state(h)
            scan_prog[h] = 0
            bg.append(scan_gen())

        def o_post_a(t):
            ob = ps[OB]
            k = t % 2
            stv, sres_st = st_slot()
            fw.op("act", lambda e: e.activation(out=ogb[k], in_=ob, func=AF.Square, accum_out=stv[:, 0:1]),
                  reads=[("ps", OB)], writes=[("ogb", k), sres_st])
            rstd_op(stv[:, 0:1], stv[:, 1:2], stv[:, 2:3], [sres_st], 1.0 / 512)
            fw.op("dve", lambda e: e.scalar_tensor_tensor(out=tmpf[k], in0=ob, scalar=stv[:, 2:3], in1=gnw_bc,
                                                          op0=ALU.mult, op1=ALU.mult),
                  reads=[("ps", OB), sres_st, "gnw_bc"], writes=[("tmpf", k)])
            fw.op("dve", lambda e: e.tensor_tensor(out=ogb[k], in0=tmpf[k], in1=sr[:, t, :], op=ALU.mult),
                  reads=[("tmpf", k), ("sr", t)], writes=[("ogb", k)])

        def o_post_b(t):
            k = t % 2
            for i in range(4):
                fw.op("pe", lambda e: e.transpose(out=misc_bf[:, i * 128:(i + 1) * 128], in_=ogb[k][:, i * 128:(i + 1) * 128],
                                                  identity=identb),
                      reads=[("ogb", k), "identb"], writes=[("ps", MISC)], inc=(i == 3))
            copy_evac(ogT[:, 4 * h:4 * h + 4, t * 128:(t + 1) * 128], misc_bf.rearrange("p (a b) -> p a b", a=4),
                      reads=[("ps", MISC)], writes=[("ogT", t)])

        def o_mm(c):
            r0 = 64 * (c % 2)
            for j in range(2):
                fw.op("pe", lambda e: e.matmul(ps[OB][r0:r0 + 64, :], lhsT=qTb[h % 2][:, j, c * 64:(c + 1) * 64],
                                               rhs=Sb[c % 2][:, j, :], start=(j == 0), stop=(j == 1)),
                      reads=[("qT", h % 2), ("Sb", c % 2, j)], writes=[("ps", OB)], inc=(j == 1))

        def scan_gen():
            for c in range(16 + (3 if own else 0)):
                if c < 16:
                    t, half = c // 2, c % 2
                    r0 = 64 * half
                    for j in range(2):
                        kb = KV[j]
                        fw.op("pe", lambda e: e.matmul(ps[kb], lhsT=kdec[par][r0:r0 + 64, t, j * 128:(j + 1) * 128],
                                                       rhs=vbuf[par][r0:r0 + 64, t, :], start=True, stop=True),
                              reads=[("kdec", par, t), ("v", par, t)], writes=[("ps", kb)])
                        sj = 2 * h + j
                        fw.op("dve", lambda e: e.scalar_tensor_tensor(out=S[:, sj, :], in0=S[:, sj, :],
                                                                      scalar=dch[par][:, j * 16 + c:j * 16 + c + 1], in1=ps[kb],
                                                                      op0=ALU.mult, op1=ALU.add),
                              reads=[("S", sj), ("dch", par), ("ps", kb)], writes=[("S", sj)])
                        if own:
                            fw.op("act", lambda e: e.activation(out=Sb[c % 2][:, j, :], in_=S[:, sj, :], func=AF.Copy),
                                  reads=[("S", sj)], writes=[("Sb", c % 2, j)])
                if own:
                    if c >= 3 and (c - 3) % 2 == 1:
                        o_post_b((c - 3) // 2)
                    if 1 <= c <= 16:
                        o_mm(c - 1)
                        if (c - 1) % 2 == 1:
                            o_post_a((c - 1) // 2)
                    scan_prog[h] = c + 1
                yield
            if own:
                scan_prog[h] = 99
            if not own:
                gather(S[:, 2 * h:2 * h + 2, :].rearrange("p a b -> p (a b)"), aginL_t[h], agoutL_t[h], 1024,
                       [("S", 2 * h), ("S", 2 * h + 1)], f"L{h}")
                yield

        if not own:
            add_block([(0, w_in_v[:, :, C_V + 512 * h:C_V + 512 * h + 512], 512)], v_compute)
            add_block([(0, w_in_v[:, :, C_K + 256 * h:C_K + 256 * h + 256], 256)], k_compute)
        else:
            add_block([(0, w_in_v[:, :, C_Q + 256 * h:C_Q + 256 * h + 256], 256)], q_compute)
            add_block([(0, w_in_v[:, :, C_R + 512 * h:C_R + 512 * h + 512], 512)], r_compute)

    stage_hooks = {}

    def hook_before(fn):
        stage_hooks[len(blocks)] = fn

    nslots = {}

    def xsrc(t):
        return x_ext[t * 128:(t + 1) * 128, :]

    def pre0():
        nslots[0] = norm_tile_a(0, xsrc, xnT, "xnT")
        nslots[1] = norm_tile_a(1, xsrc, xnT, "xnT")
        early_consts()
        norm_tile_b(0, nslots[0], wcol_mix, "wcol_mix", xnT, "xnT", do_pump=False)
        w_aug_load()
        mid_consts()
        late_consts()
    hook_before(pre0)
    for h in range(4):
        make_unit(h, False)

    for h in range(4):
        make_unit(h, True)

    omT = V(R_C, [16, T], BF16)
    ae2 = Alloc(R_E, R_F)
    gub = [ae2([8, 512], BF16) for _ in range(2)]
    vln = [ae2([8, 512], BF16) for _ in range(2)]
    af2 = Alloc(R_F, R_END)
    gvf = [af2([512], F32) for _ in range(2)]
    omb = [af2([512], BF16) for _ in range(4)]
    WsT = af2([8, 128], BF16)
    wsf = af2([8, 128], F32)
    lnw_bc = af2([256], F32)
    lnb_bc = af2([256], F32)

    def stage2_pre():
        pump_n[0] = 1
        drain_bg()
        flush_deferred()
        fw.barrier()
        fw.dma("sp", "c8", lnw_bc, gmlp_ln_w.partition_broadcast(128), writes=["lnw_bc"])
        fw.dma("sp", "c9", lnb_bc, gmlp_ln_b.partition_broadcast(128), writes=["lnb_bc"])
        fw.dma("sp", "c10", wsf, w_spatial.rearrange("g t s -> t g s"), writes=["wsf"])
        for g in range(8):
            bank, bres = mm_bank()
            fw.op("pe", lambda e: e.transpose(out=bank[:, 0:128], in_=wsf[:, g, :], identity=identf),
                  reads=["wsf", "identf"], writes=[bres])
            fw.op("dve", lambda e: e.tensor_copy(out=WsT[:, g, :], in_=bank[:, 0:128]), reads=[bres], writes=["WsT"])
        fw.op("dve", lambda e: e.memset(WsT[64:128, :, 0:64], 0.0), reads=["WsT"], writes=["WsT"])

    hook_before(stage2_pre)

    def make_gmlp(j):
        par = j % 2

        def gu_compute(slot):
            for t in range(NT):
                bank, bres = mm_bank()
                mm_group(bank, [(xnT[:, kc, t * 128:(t + 1) * 128], ring[slot][:, kc, :]) for kc in range(KC)], bres,
                         reads=[("xnT", t), ("ring", slot)])
                fw.op("act", lambda e: e.activation(out=gub[par][:, t, :], in_=bank, func=AF.Gelu),
                      reads=[bres], writes=[("gub", par, t)])
                tick()

        def spatial_mm(t):
            k = t % 4
            sb = [KV[0], KV[1], SPB][t % 3]
            for g in range(2):
                gg = 2 * j + g
                fw.op("pe", lambda e: e.matmul(ps[sb][:, g * 256:(g + 1) * 256], lhsT=WsT[:, gg, :],
                                               rhs=vln[par][:, t, g * 256:(g + 1) * 256], start=True, stop=True),
                      reads=["WsT", ("vln", par, t)], writes=[("ps", sb)], inc=(g == 1))
            for g in range(2):
                gg = 2 * j + g
                fw.op("dve", lambda e: e.scalar_tensor_tensor(out=omb[k][:, g * 256:(g + 1) * 256],
                                                              in0=ps[sb][:, g * 256:(g + 1) * 256],
                                                              scalar=bsp[:, gg:gg + 1],
                                                              in1=gub[par][:, t, g * 256:(g + 1) * 256],
                                                              op0=ALU.add, op1=ALU.mult),
                      reads=[("ps", sb), "bsp", ("gub", par, t)], writes=[("omb", k)])

        def spatial_tr(t):
            k = t % 4
            tb_i = [MISC, OB][t % 2]
            tbf = ps[tb_i][:, 0:256].bitcast(BF16)
            for i in range(4):
                fw.op("pe", lambda e: e.transpose(out=tbf[:, i * 128:(i + 1) * 128], in_=omb[k][:, i * 128:(i + 1) * 128],
                                                  identity=identb),
                      reads=[("omb", k), "identb"], writes=[("ps", tb_i)], inc=(i == 3))
            copy_evac(omT[:, 4 * j:4 * j + 4, t * 128:(t + 1) * 128], tbf.rearrange("p (a b) -> p a b", a=4),
                      reads=[("ps", tb_i)], writes=[("omT", t)], eng=0)

        def gv_compute(slot):
            for t in range(NT):
                bank, bres = mm_bank()
                mm_group(bank, [(xnT[:, kc, t * 128:(t + 1) * 128], ring[slot][:, kc, :]) for kc in range(KC)], bres,
                         reads=[("xnT", t), ("ring", slot)])
                k = t % 2
                tick_deferred()
                fw.op("act", lambda e: e.activation(out=gvf[k], in_=bank, func=AF.Gelu), reads=[bres], writes=[("gvf", k)])
                for g in range(2):
                    sl = gvf[k][:, g * 256:(g + 1) * 256]
                    stv, sres_st = st_slot()
                    stv2, sres_st2 = st_slot()
                    fw.op("dve", lambda e: e.bn_stats(out=bn6[k][:, g * 6:(g + 1) * 6], in_=sl),
                          reads=[("gvf", k)], writes=[("bn6", k, g)])
                    fw.op("dve", lambda e: e.bn_aggr(out=stv[:, 0:2], in_=bn6[k][:, g * 6:(g + 1) * 6]),
                          reads=[("bn6", k, g)], writes=[sres_st])
                    rstd_op(stv[:, 1:2], stv2[:, 0:1], stv2[:, 1:2], [sres_st, sres_st2], 1.0)
                    fw.op("dve", lambda e: e.tensor_scalar(out=sl, in0=sl, scalar1=stv[:, 0:1], scalar2=stv2[:, 1:2],
                                                           op0=ALU.subtract, op1=ALU.mult),
                          reads=[("gvf", k), sres_st, sres_st2], writes=[("gvf", k)])
                    fw.op("dve", lambda e: e.tensor_tensor(out=sl, in0=sl, in1=lnw_bc, op=ALU.mult),
                          reads=[("gvf", k), "lnw_bc"], writes=[("gvf", k)])
                    fw.op("pool", lambda e: e.tensor_tensor(out=vln[par][:, t, g * 256:(g + 1) * 256], in0=sl, in1=lnb_bc, op=ALU.add),
                          reads=[("gvf", k), "lnb_bc"], writes=[("vln", par, t)])
                defer(lambda t=t: spatial_mm(t), 3)
                defer(lambda t=t: spatial_tr(t), 5)
                pump(pump_n[0])

        add_block([(0, w_in_v[:, :, C_GU + 512 * j:C_GU + 512 * j + 512], 512)], gu_compute)
        add_block([(0, w_in_v[:, :, C_GV + 512 * j:C_GV + 512 * j + 512], 512)], gv_compute)

    bn6 = [af2([12], F32) for _ in range(2)]
    for j in range(4):
        make_gmlp(j)

    mixT = V(R_E, [16, T], BF16)
    af3 = Alloc(R_F, R_END)
    gt = [af3([4, T], BF16) for _ in range(2)]
    m0 = af3([4, T], F32)
    tmp3 = [af3([512], F32) for _ in range(2)]

    def stage3_pre():
        flush_deferred()
        drain_bg()
        fw.barrier()
        if dbg:
            fw.dma("sp", "dbg", dbg_out["d_ogT"], ogT.rearrange("p a b -> p (a b)"), reads=[("ogT", t) for t in range(NT)])
            fw.dma("sp", "dbg", dbg_out["d_omT"], omT.rearrange("p a b -> p (a b)"), reads=[("omT", t) for t in range(NT)])

    hook_before(stage3_pre)

    def make_stage3(j):
        def fm_groups(slot, actT, act_name, evac):
            for ct in range(4):
                for tb in range(2):
                    bank, bres = mm_bank()
                    mm_group(bank, [(ring[slot][:, kc, ct * 128:(ct + 1) * 128], actT[:, kc, tb * 512:(tb + 1) * 512])
                                    for kc in range(KC)], bres,
                             reads=[("ring", slot)] + [(act_name, t) for t in range(4 * tb, 4 * tb + 4)])
                    evac(bank, bres, ct, tb)
                    tick()

        def gate_compute(n):
            def f(slot):
                def evac(bank, bres, ct, tb):
                    fw.op("act", lambda e: e.activation(out=gt[n][:, ct, tb * 512:(tb + 1) * 512], in_=bank, func=AF.Sigmoid,
                                                        bias=bgate[:, n, 4 * j + ct:4 * j + ct + 1]),
                          reads=[bres, "bgate"], writes=[("gt", n, ct, tb)])
                fm_groups(slot, xnT, "xnT", evac)
            return f

        def br0_compute(slot):
            def evac(bank, bres, ct, tb):
                fw.op("dve", lambda e: e.tensor_tensor(out=m0[:, ct, tb * 512:(tb + 1) * 512], in0=bank,
                                                       in1=gt[0][:, ct, tb * 512:(tb + 1) * 512], op=ALU.mult),
                      reads=[bres, ("gt", 0, ct, tb)], writes=[("m0", ct, tb)])
            fm_groups(slot, ogT, "ogT", evac)

        def br1_compute(slot):
            def evac(bank, bres, ct, tb):
                k = (ct * 2 + tb) % 2
                fw.op("dve", lambda e: e.tensor_tensor(out=tmp3[k], in0=bank, in1=gt[1][:, ct, tb * 512:(tb + 1) * 512],
                                                       op=ALU.mult),
                      reads=[bres, ("gt", 1, ct, tb)], writes=[("tmp3", k)])
                fw.op("dve", lambda e: e.tensor_tensor(out=mixT[:, 4 * j + ct, tb * 512:(tb + 1) * 512], in0=tmp3[k],
                                                       in1=m0[:, ct, tb * 512:(tb + 1) * 512], op=ALU.add),
                      reads=[("tmp3", k), ("m0", ct, tb)], writes=[("mixT", tb)])
            fm_groups(slot, omT, "omT", evac)

        add_block([(0, w_in_v[:, :, C_G0 + 512 * j:C_G0 + 512 * j + 512], 512)], gate_compute(0))
        add_block([(0, w_br_v[0][:, :, 512 * j:512 * j + 512], 512)], br0_compute)
        add_block([(0, w_in_v[:, :, C_G1 + 512 * j:C_G1 + 512 * j + 512], 512)], gate_compute(1))
        add_block([(0, w_br_v[1][:, :, 512 * j:512 * j + 512], 512)], br1_compute)

    for j in range(4):
        make_stage3(j)

    hbuf = V(R_A, [NT, D], F32)
    hnT = V(R_C, [16, T], BF16)

    def stage4_pre():
        fw.barrier()
        xt[0], xt[1] = xt5[0], xt5[1]
        if dbg:
            fw.dma("sp", "dbg", dbg_out["d_mixT"], mixT.rearrange("p a b -> p (a b)"), reads=[("mixT", tb) for tb in range(2)])
        for t in range(NT):
            fw.dma("sp", f"hx{t}", hbuf[:, t, :], x_ext[t * 128:(t + 1) * 128, :],
                   writes=[("h", t)])

    hook_before(stage4_pre)

    def make_wout(j):
        def compute(slot):
            for t in range(NT):
                bank, bres = mm_bank()
                mm_group(bank, [(mixT[:, kc, t * 128:(t + 1) * 128], ring[slot][:, kc, :]) for kc in range(KC)], bres,
                         reads=[("mixT", t // 4), ("ring", slot)])
                hs = hbuf[:, t, 512 * j:512 * j + 512]
                fw.op("dve", lambda e: e.tensor_tensor(out=hs, in0=hs, in1=bank, op=ALU.add),
                      reads=[bres, ("h", t)], writes=[("h", t)])
                if j == 3:
                    defer_norm_tile(t, None, wcol_mlp, "wcol_mlp", hnT, "hnT", src_sb=lambda tt: hbuf[:, tt, :],
                                    src_res=lambda tt: ("h", tt))
                tick()
        add_block([(0, w_out_v[:, :, 512 * j:512 * j + 512], 512)], compute)

    for j in range(4):
        make_wout(j)

    aT = V(R_E, [16, T], BF16)
    af5 = Alloc(R_F, R_END)
    xt5 = [af5([D], F32) for _ in range(2)]
    rtmp = [af5([512], F32) for _ in range(2)]
    wbc = af5([D], F32)
    junk6 = af5([D], BF16)

    def stage5_pre():
        flush_deferred()
        fw.dma("sp", "c11", wbc, norm_final_w.partition_broadcast(128), writes=["wbc"])
        if dbg:
            fw.dma("sp", "dbg", dbg_out["d_hnT"], hnT.rearrange("p a b -> p (a b)"), reads=[("hnT", t) for t in range(NT)])
            fw.dma("sp", "dbg", dbg_out["d_h1"], hbuf.rearrange("p a b -> p (a b)"), reads=[("h", t) for t in range(NT)])

    hook_before(stage5_pre)

    def final_tile(t):
        stv, sres_st = st_slot()
        fw.op("dve", lambda e: e.tensor_reduce(out=stv[:, 0:1], in_=ssq[:, 4 * t:4 * t + 4], axis=mybir.AxisListType.X, op=ALU.add),
              reads=[("ssq", t, j) for j in range(4)], writes=[sres_st])
        rstd_op(stv[:, 0:1], stv[:, 1:2], stv[:, 2:3], [sres_st], 1.0 / D)
        fw.op("dve", lambda e: e.scalar_tensor_tensor(out=hbuf[:, t, 0:1024], in0=hbuf[:, t, 0:1024], scalar=stv[:, 2:3],
                                                      in1=wbc[:, 0:1024], op0=ALU.mult, op1=ALU.mult),
              reads=[("h", t), sres_st, "wbc"], writes=[("hA", t)])
        fw.op("pool", lambda e: e.tensor_scalar(out=hbuf[:, t, 1024:2048], in0=hbuf[:, t, 1024:2048], scalar1=stv[:, 2:3],
                                                scalar2=1.0, op0=ALU.mult, op1=ALU.mult),
              reads=[("h", t), sres_st], writes=[("hB", t)])
        fw.op("pool", lambda e: e.tensor_tensor(out=hbuf[:, t, 1024:2048], in0=hbuf[:, t, 1024:2048], in1=wbc[:, 1024:2048],
                                                op=ALU.mult),
              reads=[("hB", t), "wbc"], writes=[("hB", t)])
        fw.dma("sp", f"yo{t}", y[t * 128:(t + 1) * 128, :], hbuf[:, t, :], reads=[("h", t), ("hA", t), ("hB", t)], writes=[("y", t)])

    def make_ffn(i):
        def up_compute(u):
            def f(slot):
                for ct in range(4):
                    for tb in range(2):
                        bank, bres = mm_bank()
                        mm_group(bank, [(ring[slot][:, kc, ct * 128:(ct + 1) * 128], hnT[:, kc, tb * 512:(tb + 1) * 512])
                                        for kc in range(KC)], bres,
                                 reads=[("ring", slot)] + [("hnT", t) for t in range(4 * tb, 4 * tb + 4)])
                        k = (ct * 2 + tb) % 2
                        fw.op("act", lambda e: e.activation(out=rtmp[k], in_=bank, func=AF.Relu), reads=[bres], writes=[("rtmp", k)])
                        fw.op("dve", lambda e: e.tensor_tensor(out=aT[:, 4 * u + ct, tb * 512:(tb + 1) * 512], in0=rtmp[k],
                                                               in1=rtmp[k], op=ALU.mult),
                              reads=[("rtmp", k)], writes=[("aT", tb)])
                        tick()
            return f

        def down_compute(j):
            def f(slot):
                for t in range(NT):
                    bank, bres = mm_bank()
                    mm_group(bank, [(aT[:, kc, t * 128:(t + 1) * 128], ring[slot][:, kc, :]) for kc in range(KC)], bres,
                             reads=[("aT", t // 4), ("ring", slot)])
                    hs = hbuf[:, t, 512 * j:512 * j + 512]
                    fw.op("dve", lambda e: e.tensor_tensor(out=hs, in0=hs, in1=bank, op=ALU.add),
                          reads=[bres, ("h", t)], writes=[("h", t)])
                    if i == 3:
                        fw.op("act", lambda e: e.activation(out=junk6[:, 0:512], in_=hs, func=AF.Square,
                                                            accum_out=ssq[:, 4 * t + j:4 * t + j + 1]),
                              reads=[("h", t)], writes=["junk6", ("ssq", t, j)])
                    if i == 3 and j == 3:
                        defer(lambda t=t: final_tile(t), 1)
                    tick()
            return f

        for u in range(4):
            c0 = 2048 * i + 512 * u
            add_block([(0, w_up_v[:, :, c0:c0 + 512], 512)], up_compute(u))
        for j in range(4):
            add_block([(0, w_dn_v[i][:, :, 512 * j:512 * j + 512], 512)], down_compute(j))

    for i in range(4):
        make_ffn(i)

    n = len(blocks)
    if max_blocks is not None:
        n = min(n, max_blocks)
    issue_load(0)
    for i in range(n):
        if i in stage_hooks:
            stage_hooks[i]()
        if i + 1 < n:
            issue_load(i + 1)
        blocks[i][1](i % 2)

    drain_bg()
    flush_deferred()
    fw.wait_all_dma("sp", "yo")
    fw.wait_all_dma("sp", "dbg")
    return nc


_CACHE = {}


def _prep_inputs(inputs):
    x = np.ascontiguousarray(inputs["x"], dtype=np.float32)
    B, Sq, _ = x.shape
    per_seq = Sq // T
    shared = {
        "norm_mix_w": inputs["norm_mix_w"].reshape(D),
        "w_in": inputs["w_in"].reshape(D, DIN),
        "w_alpha_up": inputs["w_alpha_up"].reshape(16, 1024),
        "b_alpha": inputs["b_alpha"].reshape(1, 1024),
        "gla_norm_w": inputs["gla_norm_w"].reshape(512),
        "gmlp_ln_w": inputs["gmlp_ln_w"].reshape(256),
        "gmlp_ln_b": inputs["gmlp_ln_b"].reshape(256),
        "w_spatial": inputs["w_spatial"].reshape(8, 128, 128),
        "b_spatial": inputs["b_spatial"].reshape(8, 128),
        "b_gate": inputs["b_gate"].reshape(2, D),
        "w_branch": inputs["w_branch"].reshape(2, D, D),
        "w_out": inputs["w_out"].reshape(D, D),
        "norm_mlp_w": inputs["norm_mlp_w"].reshape(D),
        "w_ff_up": inputs["w_ff_up"].reshape(D, DFF),
        "w_ff_down": inputs["w_ff_down"].reshape(DFF, D),
        "norm_final_w": inputs["norm_final_w"].reshape(D),
    }
    shared = {k: np.ascontiguousarray(v, dtype=np.float32) for k, v in shared.items()}
    in_maps = []
    for c in range(8):
        b, j = c // per_seq, c % per_seq
        m = dict(shared)
        m["x_ext"] = np.ascontiguousarray(x[b, j * T:(j + 1) * T])
        cm = np.zeros((128, 4), np.float32)
        cm[:, :j] = 1.0
        m["cmask"] = cm
        in_maps.append(m)
    return in_maps, B, Sq


def kernel(**inputs):
    in_maps, B, Sq = _prep_inputs(inputs)
    if "nc" not in _CACHE:
        _CACHE["nc"] = build_program()
    res = run_bass_kernel_spmd(_CACHE["nc"], in_maps, core_ids=list(range(8)))
    per_seq = Sq // T
    out = np.empty((B, Sq, D), np.float32)
    for c in range(8):
        b, j = c // per_seq, c % per_seq
        out[b, j * T:(j + 1) * T] = res.results[c]["y"]
    return out
```

```python
import numpy as np
import concourse.bass as bass
import concourse.mybir as mybir
from concourse.bass_utils import run_bass_kernel_spmd

F32 = mybir.dt.float32
BF16 = mybir.dt.bfloat16
AF = mybir.ActivationFunctionType
ALU = mybir.AluOpType

D = 2048
KC = 16
T = 1024
NT = 8
DIN = 14352
DFF = 8192
EPS = 1e-6
NBLK = 1
C_Q, C_K, C_V, C_R, C_GLR, C_GU, C_GV, C_G0, C_G1 = 0, 1024, 2048, 4096, 6144, 6160, 8208, 10256, 12304


class Res:
    __slots__ = ("w", "r")

    def __init__(self):
        self.w = None
        self.r = {}


class FW:
    def __init__(self, nc):
        self.nc = nc
        self.eng = {"pe": nc.tensor, "act": nc.scalar, "dve": nc.vector, "pool": nc.gpsimd, "sp": nc.sync}
        self.sems = {k: nc.alloc_semaphore("s_" + k) for k in self.eng}
        self.cnt = {k: 0 for k in self.eng}
        self.waited = {k: {} for k in self.eng}
        self.dma_cnt = {}
        self.same_engine_sync = {"act", "dve", "pool"}
        self.res = {}

    def R(self, name):
        r = self.res.get(name)
        if r is None:
            r = self.res[name] = Res()
        return r

    def _wait(self, e, dep):
        key, val = dep
        if key == e and e not in self.same_engine_sync:
            return
        if self.waited[e].get(key, 0) >= val:
            return
        self.eng[e].wait_ge(self.sems[key], val)
        self.waited[e][key] = val

    def _deps(self, reads, writes):
        deps = []
        for r in reads:
            r = self.R(r)
            if r.w:
                deps.append(r.w)
        for w in writes:
            w = self.R(w)
            if w.w:
                deps.append(w.w)
            deps.extend(w.r.items())
        return deps

    def _mark(self, ev, reads, writes):
        for r in reads:
            self.R(r).r[ev[0]] = ev[1]
        for w in writes:
            w = self.R(w)
            w.w = ev
            w.r = {}

    @staticmethod
    def _excl(reads, writes):
        r2 = [r for r in reads if not (isinstance(r, tuple) and r[0] == "ps")]
        if len(r2) != len(reads):
            writes = list(writes) + [r for r in reads if isinstance(r, tuple) and r[0] == "ps" and r not in writes]
        return r2, writes

    def op(self, e, fn, reads=(), writes=(), inc=True):
        reads, writes = self._excl(reads, writes)
        for d in self._deps(reads, writes):
            self._wait(e, d)
        ins = fn(self.eng[e])
        if inc:
            ins.then_inc(self.sems[e], 1)
            self.cnt[e] += 1
            ev = (e, self.cnt[e])
        else:
            ev = (e, self.cnt[e] + 1)
        self._mark(ev, reads, writes)
        return ins

    def dma(self, q, semkey, out, in_, reads=(), writes=(), **kw):
        if semkey not in self.sems:
            self.sems[semkey] = self.nc.alloc_semaphore("d_" + semkey)
            self.dma_cnt[semkey] = 0
        for d in self._deps(reads, writes):
            self._wait(q, d)
        ins = self.eng[q].dma_start(out=out, in_=in_, **kw)
        ins.then_inc(self.sems[semkey], 16)
        self.dma_cnt[semkey] += 16
        self._mark((semkey, self.dma_cnt[semkey]), reads, writes)
        return ins

    def barrier(self):
        for e in self.eng:
            for k in self.sems:
                if k == e:
                    continue
                v = self.cnt[k] if k in self.cnt else self.dma_cnt[k]
                if v > 0:
                    self._wait(e, (k, v))

    def wait_all_dma(self, e, prefix):
        for k, v in self.dma_cnt.items():
            if k.startswith(prefix) and v > 0:
                self._wait(e, (k, v))


def build_program(dbg=False, max_blocks=None):
    nc = bass.Bass(target_bir_lowering=False)
    fw = FW(nc)

    def din(name, shape):
        return nc.dram_tensor(name, list(shape), F32, kind="ExternalInput").ap()

    x_ext = din("x_ext", [NBLK * T, D])
    cmask_d = din("cmask", [128, 4])
    norm_mix_w = din("norm_mix_w", [D])
    w_in = din("w_in", [D, DIN])
    w_alpha_up = din("w_alpha_up", [16, 1024])
    b_alpha = din("b_alpha", [1, 1024])
    gla_norm_w = din("gla_norm_w", [512])
    gmlp_ln_w = din("gmlp_ln_w", [256])
    gmlp_ln_b = din("gmlp_ln_b", [256])
    w_spatial = din("w_spatial", [8, 128, 128])
    b_spatial = din("b_spatial", [8, 128])
    b_gate = din("b_gate", [2, D])
    w_branch = din("w_branch", [2, D, D])
    w_out = din("w_out", [D, D])
    norm_mlp_w = din("norm_mlp_w", [D])
    w_ff_up = din("w_ff_up", [D, DFF])
    w_ff_down = din("w_ff_down", [DFF, D])
    norm_final_w = din("norm_final_w", [D])
    y = nc.dram_tensor("y", [T, D], F32, kind="ExternalOutput").ap()
    aginP_t = nc.dram_tensor("aginP", [128, 512], F32)
    agoutP_t = nc.dram_tensor("agoutP", [4 * 128, 512], F32)
    aginL_t = [nc.dram_tensor(f"aginL{h}", [128, 1024], F32) for h in range(4)]
    agoutL_t = [nc.dram_tensor(f"agoutL{h}", [4 * 128, 1024], F32) for h in range(4)]
    dbg_out = {}
    if dbg:
        for nm in ("d_ogT", "d_omT", "d_mixT", "d_hnT"):
            dbg_out[nm] = nc.dram_tensor(nm, [128, KC * T], BF16, kind="ExternalOutput").ap()
        dbg_out["d_h1"] = nc.dram_tensor("d_h1", [128, NT * D], F32, kind="ExternalOutput").ap()

    w_in_v = w_in.rearrange("(kc p) n -> p kc n", p=128)
    w_br_v = [w_branch[n].rearrange("(kc p) n -> p kc n", p=128) for n in range(2)]
    w_out_v = w_out.rearrange("(kc p) n -> p kc n", p=128)
    w_up_v = w_ff_up.rearrange("(kc p) n -> p kc n", p=128)
    w_dn_v = [w_ff_down[2048 * i:2048 * (i + 1), :].rearrange("(kc p) n -> p kc n", p=128) for i in range(4)]

    K1 = 1024
    ARENA = 206 * K1
    arena = nc.alloc_sbuf_tensor("arena", [128, ARENA // 2], BF16)

    def V(off, shape, dt, parts=None):
        esz = 2 if dt == BF16 else 4
        n = int(np.prod(shape)) * esz
        assert off % 4 == 0 and off + n <= ARENA, (off, n)
        v = arena[:, off // 2:(off + n) // 2]
        if dt != BF16:
            v = v.bitcast(dt)
        if len(shape) == 2:
            v = v.rearrange("p (a b) -> p a b", a=shape[0])
        elif len(shape) == 3:
            v = v.rearrange("p (a b c) -> p a b c", a=shape[0], b=shape[1])
        return v

    class Alloc:
        def __init__(self, base, limit):
            self.o = base
            self.limit = limit

        def __call__(self, shape, dt):
            esz = 2 if dt == BF16 else 4
            n = (int(np.prod(shape)) * esz + 31) // 32 * 32
            v = V(self.o, shape, dt)
            self.o += n
            assert self.o <= self.limit, (self.o, self.limit)
            return v

    R_CONST = 0
    R_RING = 8 * K1
    R_A = 40 * K1
    R_B = 72 * K1
    R_C = 104 * K1
    R_E = 136 * K1
    R_F = 168 * K1
    R_END = ARENA

    ca = Alloc(R_CONST, R_RING)
    identf = ca([128], F32)
    identb = ca([128], BF16)
    Mgt = ca([128], F32)
    Ind = ca([2], F32)
    gnw_bc = ca([512], F32)
    wcol_mix = ca([16], F32)
    wcol_mlp = ca([16], F32)
    bgate = ca([2, 16], F32)
    bsp = ca([8], F32)
    wglr = ca([16, 16], BF16)
    st = ca([96], F32)
    dch = [ca([32], F32) for _ in range(4)]
    cmask = ca([4], F32)
    Pfull = ca([512], F32)
    Ptile = Pfull[:, 0:8]
    Pg = [ca([8], F32) for _ in range(3)]
    acoef = [ca([8], F32) for _ in range(3)]
    csr = ca([8], F32)
    ssq = ca([32], F32)
    neg_half = ca([1], F32)
    ones_c = ca([1], F32)
    ring = [V(R_RING + 16 * K1 * i, [16, 512], BF16) for i in range(2)]

    ps = [nc.alloc_psum_tensor(f"ps{i}", [128, 512], F32)[:, :] for i in range(8)]
    MM = [0, 1, 2]
    KV = [3, 4]
    OB = 5
    SPB = 6
    MISC = 7
    misc_bf = ps[MISC][:, 0:256].bitcast(BF16)
    csum_ps = ps[SPB][:, 0:64]

    mm_ctr = [0]

    def mm_bank():
        i = MM[mm_ctr[0] % len(MM)]
        mm_ctr[0] += 1
        return ps[i], ("ps", i)

    st_ctr = [0]

    def st_slot():
        k = st_ctr[0] % 32
        st_ctr[0] += 1
        return st[:, 3 * k:3 * k + 3], ("st", k)

    def rstd_op(src, tmp, dst, res, scale):
        fw.op("dve", lambda e: e.tensor_scalar(out=tmp, in0=src, scalar1=scale, scalar2=EPS, op0=ALU.mult, op1=ALU.add),
              reads=res, writes=res)
        fw.op("pool", lambda e: e.tensor_tensor(out=dst, in0=tmp, in1=neg_half, op=ALU.pow),
              reads=res + ["neg_half"], writes=res)

    ev_ctr = [0]

    def copy_evac(out, in_, reads, writes, scale=None, eng=None):
        if eng is None:
            ev_ctr[0] += 1
            eng = ev_ctr[0] % 2
        if eng % 2 == 0:
            if scale is None:
                fw.op("act", lambda e: e.activation(out=out, in_=in_, func=AF.Copy), reads=reads, writes=writes)
            else:
                fw.op("act", lambda e: e.activation(out=out, in_=in_, func=AF.Copy, scale=scale), reads=reads, writes=writes)
        else:
            if scale is None:
                fw.op("dve", lambda e: e.tensor_copy(out=out, in_=in_), reads=reads, writes=writes)
            else:
                fw.op("dve", lambda e: e.tensor_scalar(out=out, in0=in_, scalar1=scale, scalar2=None, op0=ALU.mult),
                      reads=reads, writes=writes)

    deferred = []
    bg = []
    pump_n = [1]

    def defer(fn, delay):
        deferred.append([delay, fn])

    def pump(n=1):
        for _ in range(n):
            while bg:
                try:
                    next(bg[0])
                    break
                except StopIteration:
                    bg.pop(0)

    def drain_bg():
        while bg:
            pump()

    def tick_deferred():
        ready = []
        for d in deferred:
            d[0] -= 1
            if d[0] <= 0:
                ready.append(d)
        for d in ready:
            deferred.remove(d)
            d[1]()

    def tick():
        ready = []
        for d in deferred:
            d[0] -= 1
            if d[0] <= 0:
                ready.append(d)
        for d in ready:
            deferred.remove(d)
            d[1]()
        pump(pump_n[0])

    def flush_deferred():
        while deferred:
            d = deferred.pop(0)
            d[1]()

    def mm_group(out_ap, pairs, bank_res, reads, mid_tick=False):
        n = len(pairs)
        for i, (l, r) in enumerate(pairs):
            fw.op("pe", lambda e: e.matmul(out_ap, lhsT=l, rhs=r, start=(i == 0), stop=(i == n - 1)),
                  reads=reads, writes=[bank_res], inc=(i == n - 1))
            if mid_tick and i == n // 2 - 1:
                tick()

    blocks = []

    def add_block(loads, compute):
        blocks.append((loads, compute))

    def issue_load(i):
        loads, _ = blocks[i]
        slot = i % 2
        for (co, src, n) in loads:
            fw.dma("pool", f"ring{slot}", ring[slot][:, :, co:co + n], src, writes=[("ring", slot)])

    fw.op("pool", lambda e: e.memset(identf, 0.0), writes=["identf"])
    fw.op("pool", lambda e: e.affine_select(out=identf, in_=identf, pattern=[[-1, 128]], compare_op=ALU.not_equal,
                                            fill=1.0, base=0, channel_multiplier=1), reads=["identf"], writes=["identf"])
    fw.op("pool", lambda e: e.tensor_copy(out=identb, in_=identf), reads=["identf"], writes=["identb"])
    fw.op("pool", lambda e: e.memset(Mgt, 1.0), writes=["Mgt"])
    fw.op("pool", lambda e: e.affine_select(out=Mgt, in_=Mgt, pattern=[[-1, 128]], compare_op=ALU.is_gt,
                                            fill=0.0, base=0, channel_multiplier=1), reads=["Mgt"], writes=["Mgt"])
    fw.op("pool", lambda e: e.memset(Mgt[64:128, 0:64], 0.0), reads=["Mgt"], writes=["Mgt"])
    fw.op("pool", lambda e: e.memset(neg_half, -0.5), writes=["neg_half"])
    fw.op("pool", lambda e: e.memset(Pfull, 0.0), writes=["Ptile"])
    fw.op("pool", lambda e: e.memset(Ind, 0.0), writes=["Ind"])
    fw.op("pool", lambda e: e.memset(Ind[0:64, 0:1], 1.0), reads=["Ind"], writes=["Ind"])
    fw.op("pool", lambda e: e.memset(Ind[64:128, 1:2], 1.0), reads=["Ind"], writes=["Ind"])
    def early_consts():
        fw.dma("sp", "c1", wcol_mix, norm_mix_w.rearrange("(kc p) -> p kc", p=128), writes=["wcol_mix"],
               allow_slow_non_contiguous=True)

    def mid_consts():
        fw.dma("sp", "c0", gnw_bc, gla_norm_w.partition_broadcast(128), writes=["gnw_bc"])
        fw.dma("sp", "c0b", cmask, cmask_d, writes=["cmask"])
    def late_consts():
        fw.dma("sp", "c2", wcol_mlp, norm_mlp_w.rearrange("(kc p) -> p kc", p=128), writes=["wcol_mlp"],
               allow_slow_non_contiguous=True)
        for n in range(2):
            fw.dma("sp", f"c3{n}", bgate[:, n, :], b_gate[n].rearrange("(kc p) -> p kc", p=128), writes=["bgate"],
                   allow_slow_non_contiguous=True)
        fw.dma("sp", "c4", bsp, b_spatial.rearrange("g t -> t g"), writes=["bsp"], allow_slow_non_contiguous=True)
    fw.dma("pool", "c5", wglr, w_in_v[:, :, C_GLR:C_GLR + 16], writes=["wglr"])

    xnT = V(R_A, [16, T], BF16)
    ogT = V(R_B, [16, T], BF16)
    ac = Alloc(R_C, R_E)
    S = ac([8, 512], F32)
    lb_off = ac.o
    esp = [ac([512], F32) for _ in range(2)]
    expG = [ac([512], F32) for _ in range(2)]
    Lbuf = [V(lb_off + 4096 * i, [2, 512], F32) for i in range(2)]
    w_aug_off = ac.o
    w_aug = ac([1024], F32)
    glrT = ac([T], F32)
    ae = Alloc(R_E, R_F)
    kdec = [ae([8, 256], BF16) for _ in range(4)]
    qT = ae([2, T], BF16)
    qTb = [qT, V(w_aug_off, [2, T], BF16)]
    scan_prog = {}
    sr = ae([8, 512], BF16)
    Sb = [ae([2, 512], BF16) for _ in range(2)]
    af = Alloc(R_F, R_END)
    v_off = af.o
    vbuf = [af([8, 512], BF16) for _ in range(4)]
    xt = [V(v_off + 16 * K1 + 8 * K1 * i, [D], F32) for i in range(2)]
    tmpf = [af([512], F32) for _ in range(2)]
    ogb = [af([512], BF16) for _ in range(2)]

    def w_aug_load():
        fw.dma("sp", "c6", w_aug[0:16, :], w_alpha_up, writes=["w_aug"])
        fw.dma("sp", "c7", w_aug[16:17, :], b_alpha, writes=["w_aug"])
    fw.op("dve", lambda e: e.memset(glrT[0:32, :], 1.0), writes=["glrT"])
    fw.op("dve", lambda e: e.memset(S, 0.0), writes=[("S", i) for i in range(8)])

    xt_ctr = [0]

    def norm_tile_a(t, src_rows, dstT, dst_name, src_sb=None, src_res=None):
        slot = xt_ctr[0] % len(xt)
        xt_ctr[0] += 1
        xr = ("xt", slot)
        if src_sb is None:
            fw.dma("sp", f"x{slot}", xt[slot], src_rows(t), writes=[xr])
            src = xt[slot]
            sres = [xr]
        else:
            src = src_sb(t)
            sres = [src_res(t)]
        stv, sres_st = st_slot()
        junk = dstT[:, :, t * 128:(t + 1) * 128]
        fw.op("act", lambda e: e.activation(out=junk, in_=src.rearrange("p (a b) -> p a b", a=16), func=AF.Square,
                                            accum_out=stv[:, 0:1]),
              reads=sres, writes=[(dst_name, t), sres_st])
        rstd_op(stv[:, 0:1], stv[:, 1:2], stv[:, 2:3], [sres_st], 1.0 / D)
        fw.op("pool", lambda e: e.tensor_scalar(out=xt[slot], in0=src, scalar1=stv[:, 2:3], scalar2=1.0,
                                                op0=ALU.mult, op1=ALU.mult),
              reads=sres + [sres_st], writes=[xr])
        return slot

    def norm_tile_b(t, slot, wcol, wcol_res, dstT, dst_name, do_pump=True):
        xr = ("xt", slot)
        for g in range(4):
            bank, bres = mm_bank()
            for i in range(4):
                kc = 4 * g + i
                fw.op("pe", lambda e: e.transpose(out=bank[:, i * 128:(i + 1) * 128],
                                                  in_=xt[slot][:, kc * 128:(kc + 1) * 128], identity=identf),
                      reads=[xr, "identf"], writes=[bres], inc=(i == 3))
            for i in range(4):
                kc = 4 * g + i
                copy_evac(dstT[:, kc, t * 128:(t + 1) * 128], bank[:, i * 128:(i + 1) * 128],
                          reads=[bres, wcol_res], writes=[(dst_name, t)], scale=wcol[:, kc:kc + 1], eng=g)
            if do_pump:
                pump(1)

    def norm_tile(t, src_rows, wcol, wcol_res, dstT, dst_name, src_sb=None, src_res=None, do_pump=True):
        slot = norm_tile_a(t, src_rows, dstT, dst_name, src_sb, src_res)
        norm_tile_b(t, slot, wcol, wcol_res, dstT, dst_name, do_pump)

    def defer_norm_tile(t, src_rows, wcol, wcol_res, dstT, dst_name, src_sb=None, src_res=None):
        box = {}

        def a():
            box["slot"] = norm_tile_a(t, src_rows, dstT, dst_name, src_sb, src_res)

        def b():
            norm_tile_b(t, box["slot"], wcol, wcol_res, dstT, dst_name, do_pump=False)
        defer(a, 1)
        defer(b, 1 + len(xt))

    def norm_transpose(src_rows, wcol, wcol_res, dstT, dst_name, src_sb=None, src_res=None):
        la = len(xt) - 1
        slots = {}
        for t in range(min(la, NT)):
            slots[t] = norm_tile_a(t, src_rows, dstT, dst_name, src_sb, src_res)
        for t in range(NT):
            norm_tile_b(t, slots[t], wcol, wcol_res, dstT, dst_name)
            if t + la < NT:
                slots[t + la] = norm_tile_a(t + la, src_rows, dstT, dst_name, src_sb, src_res)

    def glr_compute(blk):
        for tb in range(2):
            bank, bres = mm_bank()
            mm_group(bank[0:16, :], [(wglr[:, kc, :], xnT[:, kc, tb * 512:(tb + 1) * 512]) for kc in range(KC)], bres,
                     reads=["wglr"] + [("xnT", t) for t in range(4 * tb, 4 * tb + 4)])
            fw.op("act", lambda e: e.activation(out=glrT[0:16, tb * 512:(tb + 1) * 512], in_=bank[0:16, :], func=AF.Copy),
                  reads=[bres], writes=["glrT"])

    cc = nc.alloc_semaphore("cc_sem")
    fw.sems["cc"] = cc
    fw.dma_cnt["cc"] = 0
    rg = [[0, 1, 2, 3], [4, 5, 6, 7]]

    def gather(src_sb, in_t, out_t, width, rd, name):
        fw.dma("sp", "ag_" + name, in_t.ap()[:, 0:width], src_sb, reads=rd, writes=["agin_" + name])
        for d in fw._deps(["agin_" + name], ["agout_" + name]):
            fw._wait("pool", d)
        nc.gpsimd.collective_compute("AllGather", ALU.bypass, replica_groups=rg,
                                     ins=[in_t.ap().opt()], outs=[out_t.ap().opt()]).then_inc(cc)
        fw.dma_cnt["cc"] += 1
        fw._mark(("cc", fw.dma_cnt["cc"]), ["agin_" + name], ["agout_" + name])

    lb_ctr = [0]

    def start_state(h):
        if h == 0:
            agoutP = agoutP_t.ap()
            for i in range(3):
                fw.dma("sp", f"pg{i}", Pg[i], agoutP[i * 128:(i + 1) * 128, 0:8], reads=["agout_P"], writes=[("Pg", i)])
                fw.op("dve", lambda e: e.tensor_scalar(out=acoef[i], in0=Pg[i], scalar1=-1.0, scalar2=cmask[:, i:i + 1],
                                                       op0=ALU.add, op1=ALU.mult),
                      reads=[("Pg", i), "cmask"], writes=[("acoef", i)])
                fw.op("dve", lambda e: e.tensor_scalar(out=acoef[i], in0=acoef[i], scalar1=1.0, scalar2=None, op0=ALU.add),
                      reads=[("acoef", i)], writes=[("acoef", i)])
        agoutL = agoutL_t[h].ap()
        hs = [("S", 2 * h), ("S", 2 * h + 1)]
        fw.op("dve", lambda e: e.memset(S[:, 2 * h:2 * h + 2, :], 0.0), reads=hs, writes=hs)
        for i in range(3):
            k = lb_ctr[0] % 2
            lb_ctr[0] += 1
            lb = Lbuf[k]
            lr = ("Lbuf", k)
            alias = [("esp", 0), ("esp", 1)] if k == 0 else [("expG", 0), ("expG", 1)]
            fw.dma("sp", f"lb{k}", lb.rearrange("p a b -> p (a b)"), agoutL[i * 128:(i + 1) * 128, :],
                   reads=[f"agout_L{h}"], writes=[lr] + alias)
            for j in range(2):
                sj = 2 * h + j
                fw.op("act", lambda e: e.activation(out=lb[:, j, :], in_=lb[:, j, :], func=AF.Copy, scale=cmask[:, i:i + 1]),
                      reads=[lr, "cmask"], writes=[lr])
                fw.op("dve", lambda e: e.scalar_tensor_tensor(out=S[:, sj, :], in0=S[:, sj, :], scalar=acoef[i][:, sj:sj + 1],
                                                              in1=lb[:, j, :], op0=ALU.mult, op1=ALU.add),
                      reads=[("S", sj), ("acoef", i), lr], writes=[("S", sj)])

    unit_ctr = [0]

    def make_unit(h, own):
        blk = 0
        par = h
        hp = h % 2

        def z_pair(p):
            bank, bres = mm_bank()
            for i in range(2):
                t = 2 * p + i
                fw.op("pe", lambda e: e.matmul(bank[:, i * 256:(i + 1) * 256], lhsT=glrT[0:17, t * 128:(t + 1) * 128],
                                               rhs=w_aug[0:17, 256 * h:256 * h + 256], start=True, stop=True),
                      reads=["glrT", "w_aug"], writes=[bres], inc=(i == 1))
            er = ("esp", p % 2)
            fw.op("act", lambda e: e.activation(out=esp[p % 2], in_=bank, func=AF.Exp, scale=-1.0), reads=[bres], writes=[er])
            fw.op("act", lambda e: e.activation(out=esp[p % 2], in_=esp[p % 2], func=AF.Ln, bias=1.0), reads=[er], writes=[er])

        def g_pair(p):
            er = ("esp", p % 2)
            bank, bres = mm_bank()
            for i in range(2):
                fw.op("pe", lambda e: e.matmul(bank[:, i * 256:(i + 1) * 256], lhsT=Mgt, rhs=esp[p % 2][:, i * 256:(i + 1) * 256],
                                               start=True, stop=True),
                      reads=["Mgt", er], writes=[bres], inc=(i == 1))
            fw.op("act", lambda e: e.activation(out=expG[p % 2], in_=bank, func=AF.Exp, scale=-1.0 / 16),
                  reads=[bres], writes=[("expG", p % 2)])
            for i in range(2):
                t = 2 * p + i
                for j in range(2):
                    c0 = hp * 32 + j * 16 + 2 * t
                    last = (i == 1 and j == 1)
                    fw.op("pe", lambda e: e.matmul(csum_ps[:, c0:c0 + 2],
                                                   lhsT=esp[p % 2][:, i * 256 + j * 128:i * 256 + (j + 1) * 128], rhs=Ind,
                                                   start=True, stop=True),
                          reads=[er, "Ind"], writes=[("ps", SPB)], inc=last)

        def k_pair(p, slot):
            bank, bres = mm_bank()
            for i in range(2):
                t = 2 * p + i
                mm_group(bank[:, i * 256:(i + 1) * 256],
                         [(xnT[:, kc, t * 128:(t + 1) * 128], ring[slot][:, kc, 0:256]) for kc in range(KC)], bres,
                         reads=[("xnT", t), ("ring", slot)])
                tick()
            fw.op("dve", lambda e: e.tensor_tensor(out=kdec[par][:, 2 * p:2 * p + 2, :].rearrange("p a b -> p (a b)"),
                                                   in0=bank, in1=expG[p % 2], op=ALU.mult),
                  reads=[bres, ("expG", p % 2)], writes=[("kdec", par, 2 * p), ("kdec", par, 2 * p + 1)])

        def k_compute(slot):
            if h == 0:
                glr_compute(0)
            z_pair(0)
            z_pair(1)
            for p in range(4):
                g_pair(p)
                k_pair(p, slot)
                if p + 2 < 4:
                    z_pair(p + 2)
            fw.op("act", lambda e: e.activation(out=dch[par], in_=csum_ps[:, hp * 32:hp * 32 + 32], func=AF.Exp,
                                                scale=-1.0 / 16),
                  reads=[("ps", SPB)], writes=[("dch", par)])
            fw.op("dve", lambda e: e.tensor_reduce(out=csr[:, 2 * h:2 * h + 2],
                                                   in_=csum_ps[:, hp * 32:hp * 32 + 32].rearrange("p (a b) -> p a b", a=2),
                                                   axis=mybir.AxisListType.X, op=ALU.add),
                  reads=[("ps", SPB)], writes=["csr"])
            fw.op("act", lambda e: e.activation(out=Ptile[:, 2 * h:2 * h + 2], in_=csr[:, 2 * h:2 * h + 2], func=AF.Exp,
                                                scale=-1.0 / 16),
                  reads=["csr"], writes=["Ptile"])
            if h == 3:
                gather(Pfull, aginP_t, agoutP_t, 512, ["Ptile"], "P")
            bg.append(scan_gen())

        def v_compute(slot):
            for t in range(NT):
                if h == 0:
                    if t + 1 < NT:
                        norm_tile_b(t + 1, nslots[t + 1], wcol_mix, "wcol_mix", xnT, "xnT", do_pump=False)
                    if t + 2 < NT:
                        nslots[t + 2] = norm_tile_a(t + 2, xsrc, xnT, "xnT")
                bank, bres = mm_bank()
                mm_group(bank, [(xnT[:, kc, t * 128:(t + 1) * 128], ring[slot][:, kc, :]) for kc in range(KC)], bres,
                         reads=[("xnT", t), ("ring", slot)])
                copy_evac(vbuf[par][:, t, :], bank, reads=[bres], writes=[("v", par, t)])
                tick()

        def q_compute(slot):
            pump_n[0] = 1
            if h == 0:
                drain_bg()
            while bg and scan_prog.get(h - 2, 99) < 99:
                pump()
            for ct in range(2):
                for tb in range(2):
                    bank, bres = mm_bank()
                    mm_group(bank, [(ring[slot][:, kc, ct * 128:(ct + 1) * 128], xnT[:, kc, tb * 512:(tb + 1) * 512])
                                    for kc in range(KC)], bres,
                             reads=[("ring", slot)] + [("xnT", t) for t in range(4 * tb, 4 * tb + 4)], mid_tick=True)
                    fw.op("act", lambda e: e.activation(out=qTb[h % 2][:, ct, tb * 512:(tb + 1) * 512], in_=bank, func=AF.Copy,
                                                        scale=1.0 / 16),
                          reads=[bres], writes=[("qT", h % 2)])
                    tick()

        def r_compute(slot):
            for t in range(NT):
                while bg and scan_prog.get(h - 1, 99) < 2 * t + 3:
                    pump()
                bank, bres = mm_bank()
                mm_group(bank, [(xnT[:, kc, t * 128:(t + 1) * 128], ring[slot][:, kc, :]) for kc in range(KC)], bres,
                         reads=[("xnT", t), ("ring", slot)], mid_tick=True)
                fw.op("act", lambda e: e.activation(out=sr[:, t, :], in_=bank, func=AF.Silu), reads=[bres], writes=[("sr", t)])
                tick()
            start_state(h)
            scan_prog[h] = 0
            bg.append(scan_gen())

        def o_post_a(t):
            ob = ps[OB]
            k = t % 2
            stv, sres_st = st_slot()
            fw.op("act", lambda e: e.activation(out=ogb[k], in_=ob, func=AF.Square, accum_out=stv[:, 0:1]),
                  reads=[("ps", OB)], writes=[("ogb", k), sres_st])
            rstd_op(stv[:, 0:1], stv[:, 1:2], stv[:, 2:3], [sres_st], 1.0 / 512)
            fw.op("dve", lambda e: e.scalar_tensor_tensor(out=tmpf[k], in0=ob, scalar=stv[:, 2:3], in1=gnw_bc,
                                                          op0=ALU.mult, op1=ALU.mult),
                  reads=[("ps", OB), sres_st, "gnw_bc"], writes=[("tmpf", k)])
            fw.op("dve", lambda e: e.tensor_tensor(out=ogb[k], in0=tmpf[k], in1=sr[:, t, :], op=ALU.mult),
                  reads=[("tmpf", k), ("sr", t)], writes=[("ogb", k)])

        def o_post_b(t):
            k = t % 2
            for i in range(4):
                fw.op("pe", lambda e: e.transpose(out=misc_bf[:, i * 128:(i + 1) * 128], in_=ogb[k][:, i * 128:(i + 1) * 128],
                                                  identity=identb),
                      reads=[("ogb", k), "identb"], writes=[("ps", MISC)], inc=(i == 3))
            copy_evac(ogT[:, 4 * h:4 * h + 4, t * 128:(t + 1) * 128], misc_bf.rearrange("p (a b) -> p a b", a=4),
                      reads=[("ps", MISC)], writes=[("ogT", t)])

        def o_mm(c):
            r0 = 64 * (c % 2)
            for j in range(2):
                fw.op("pe", lambda e: e.matmul(ps[OB][r0:r0 + 64, :], lhsT=qTb[h % 2][:, j, c * 64:(c + 1) * 64],
                                               rhs=Sb[c % 2][:, j, :], start=(j == 0), stop=(j == 1)),
                      reads=[("qT", h % 2), ("Sb", c % 2, j)], writes=[("ps", OB)], inc=(j == 1))

        def scan_gen():
            for c in range(16 + (3 if own else 0)):
                if c < 16:
                    t, half = c // 2, c % 2
                    r0 = 64 * half
                    for j in range(2):
                        kb = KV[j]
                        fw.op("pe", lambda e: e.matmul(ps[kb], lhsT=kdec[par][r0:r0 + 64, t, j * 128:(j + 1) * 128],
                                                       rhs=vbuf[par][r0:r0 + 64, t, :], start=True, stop=True),
                              reads=[("kdec", par, t), ("v", par, t)], writes=[("ps", kb)])
                        sj = 2 * h + j
                        fw.op("dve", lambda e: e.scalar_tensor_tensor(out=S[:, sj, :], in0=S[:, sj, :],
                                                                      scalar=dch[par][:, j * 16 + c:j * 16 + c + 1], in1=ps[kb],
                                                                      op0=ALU.mult, op1=ALU.add),
                              reads=[("S", sj), ("dch", par), ("ps", kb)], writes=[("S", sj)])
                        if own:
                            fw.op("act", lambda e: e.activation(out=Sb[c % 2][:, j, :], in_=S[:, sj, :], func=AF.Copy),
                                  reads=[("S", sj)], writes=[("Sb", c % 2, j)])
                if own:
                    if c >= 3 and (c - 3) % 2 == 1:
                        o_post_b((c - 3) // 2)
                    if 1 <= c <= 16:
                        o_mm(c - 1)
                        if (c - 1) % 2 == 1:
                            o_post_a((c - 1) // 2)
                    scan_prog[h] = c + 1
                yield
            if own:
                scan_prog[h] = 99
            if not own:
                gather(S[:, 2 * h:2 * h + 2, :].rearrange("p a b -> p (a b)"), aginL_t[h], agoutL_t[h], 1024,
                       [("S", 2 * h), ("S", 2 * h + 1)], f"L{h}")
                yield

        if not own:
            add_block([(0, w_in_v[:, :, C_V + 512 * h:C_V + 512 * h + 512], 512)], v_compute)
            add_block([(0, w_in_v[:, :, C_K + 256 * h:C_K + 256 * h + 256], 256)], k_compute)
        else:
            add_block([(0, w_in_v[:, :, C_Q + 256 * h:C_Q + 256 * h + 256], 256)], q_compute)
            add_block([(0, w_in_v[:, :, C_R + 512 * h:C_R + 512 * h + 512], 512)], r_compute)

    stage_hooks = {}

    def hook_before(fn):
        stage_hooks[len(blocks)] = fn

    nslots = {}

    def xsrc(t):
        return x_ext[t * 128:(t + 1) * 128, :]

    def pre0():
        nslots[0] = norm_tile_a(0, xsrc, xnT, "xnT")
        nslots[1] = norm_tile_a(1, xsrc, xnT, "xnT")
        early_consts()
        norm_tile_b(0, nslots[0], wcol_mix, "wcol_mix", xnT, "xnT", do_pump=False)
        w_aug_load()
        mid_consts()
        late_consts()
    hook_before(pre0)
    for h in range(4):
        make_unit(h, False)

    for h in range(4):
        make_unit(h, True)

    omT = V(R_C, [16, T], BF16)
    ae2 = Alloc(R_E, R_F)
    gub = [ae2([8, 512], BF16) for _ in range(2)]
    vln = [ae2([8, 512], BF16) for _ in range(2)]
    af2 = Alloc(R_F, R_END)
    gvf = [af2([512], F32) for _ in range(2)]
    omb = [af2([512], BF16) for _ in range(4)]
    WsT = af2([8, 128], BF16)
    wsf = af2([8, 128], F32)
    lnw_bc = af2([256], F32)
    lnb_bc = af2([256], F32)

    def stage2_pre():
        pump_n[0] = 1
        drain_bg()
        flush_deferred()
        fw.barrier()
        fw.dma("sp", "c8", lnw_bc, gmlp_ln_w.partition_broadcast(128), writes=["lnw_bc"])
        fw.dma("sp", "c9", lnb_bc, gmlp_ln_b.partition_broadcast(128), writes=["lnb_bc"])
        fw.dma("sp", "c10", wsf, w_spatial.rearrange("g t s -> t g s"), writes=["wsf"])
        for g in range(8):
            bank, bres = mm_bank()
            fw.op("pe", lambda e: e.transpose(out=bank[:, 0:128], in_=wsf[:, g, :], identity=identf),
                  reads=["wsf", "identf"], writes=[bres])
            fw.op("dve", lambda e: e.tensor_copy(out=WsT[:, g, :], in_=bank[:, 0:128]), reads=[bres], writes=["WsT"])
        fw.op("dve", lambda e: e.memset(WsT[64:128, :, 0:64], 0.0), reads=["WsT"], writes=["WsT"])

    hook_before(stage2_pre)

    def make_gmlp(j):
        par = j % 2

        def gu_compute(slot):
            for t in range(NT):
                bank, bres = mm_bank()
                mm_group(bank, [(xnT[:, kc, t * 128:(t + 1) * 128], ring[slot][:, kc, :]) for kc in range(KC)], bres,
                         reads=[("xnT", t), ("ring", slot)])
                fw.op("act", lambda e: e.activation(out=gub[par][:, t, :], in_=bank, func=AF.Gelu),
                      reads=[bres], writes=[("gub", par, t)])
                tick()

        def spatial_mm(t):
            k = t % 4
            sb = [KV[0], KV[1], SPB][t % 3]
            for g in range(2):
                gg = 2 * j + g
                fw.op("pe", lambda e: e.matmul(ps[sb][:, g * 256:(g + 1) * 256], lhsT=WsT[:, gg, :],
                                               rhs=vln[par][:, t, g * 256:(g + 1) * 256], start=True, stop=True),
                      reads=["WsT", ("vln", par, t)], writes=[("ps", sb)], inc=(g == 1))
            for g in range(2):
                gg = 2 * j + g
                fw.op("dve", lambda e: e.scalar_tensor_tensor(out=omb[k][:, g * 256:(g + 1) * 256],
                                                              in0=ps[sb][:, g * 256:(g + 1) * 256],
                                                              scalar=bsp[:, gg:gg + 1],
                                                              in1=gub[par][:, t, g * 256:(g + 1) * 256],
                                                              op0=ALU.add, op1=ALU.mult),
                      reads=[("ps", sb), "bsp", ("gub", par, t)], writes=[("omb", k)])

        def spatial_tr(t):
            k = t % 4
            tb_i = [MISC, OB][t % 2]
            tbf = ps[tb_i][:, 0:256].bitcast(BF16)
            for i in range(4):
                fw.op("pe", lambda e: e.transpose(out=tbf[:, i * 128:(i + 1) * 128], in_=omb[k][:, i * 128:(i + 1) * 128],
                                                  identity=identb),
                      reads=[("omb", k), "identb"], writes=[("ps", tb_i)], inc=(i == 3))
            copy_evac(omT[:, 4 * j:4 * j + 4, t * 128:(t + 1) * 128], tbf.rearrange("p (a b) -> p a b", a=4),
                      reads=[("ps", tb_i)], writes=[("omT", t)], eng=0)

        def gv_compute(slot):
            for t in range(NT):
                bank, bres = mm_bank()
                mm_group(bank, [(xnT[:, kc, t * 128:(t + 1) * 128], ring[slot][:, kc, :]) for kc in range(KC)], bres,
                         reads=[("xnT", t), ("ring", slot)])
                k = t % 2
                tick_deferred()
                fw.op("act", lambda e: e.activation(out=gvf[k], in_=bank, func=AF.Gelu), reads=[bres], writes=[("gvf", k)])
                for g in range(2):
                    sl = gvf[k][:, g * 256:(g + 1) * 256]
                    stv, sres_st = st_slot()
                    stv2, sres_st2 = st_slot()
                    fw.op("dve", lambda e: e.bn_stats(out=bn6[k][:, g * 6:(g + 1) * 6], in_=sl),
                          reads=[("gvf", k)], writes=[("bn6", k, g)])
                    fw.op("dve", lambda e: e.bn_aggr(out=stv[:, 0:2], in_=bn6[k][:, g * 6:(g + 1) * 6]),
                          reads=[("bn6", k, g)], writes=[sres_st])
                    rstd_op(stv[:, 1:2], stv2[:, 0:1], stv2[:, 1:2], [sres_st, sres_st2], 1.0)
                    fw.op("dve", lambda e: e.tensor_scalar(out=sl, in0=sl, scalar1=stv[:, 0:1], scalar2=stv2[:, 1:2],
                                                           op0=ALU.subtract, op1=ALU.mult),
                          reads=[("gvf", k), sres_st, sres_st2], writes=[("gvf", k)])
                    fw.op("dve", lambda e: e.tensor_tensor(out=sl, in0=sl, in1=lnw_bc, op=ALU.mult),
                          reads=[("gvf", k), "lnw_bc"], writes=[("gvf", k)])
                    fw.op("pool", lambda e: e.tensor_tensor(out=vln[par][:, t, g * 256:(g + 1) * 256], in0=sl, in1=lnb_bc, op=ALU.add),
                          reads=[("gvf", k), "lnb_bc"], writes=[("vln", par, t)])
                defer(lambda t=t: spatial_mm(t), 3)
                defer(lambda t=t: spatial_tr(t), 5)
                pump(pump_n[0])

        add_block([(0, w_in_v[:, :, C_GU + 512 * j:C_GU + 512 * j + 512], 512)], gu_compute)
        add_block([(0, w_in_v[:, :, C_GV + 512 * j:C_GV + 512 * j + 512], 512)], gv_compute)

    bn6 = [af2([12], F32) for _ in range(2)]
    for j in range(4):
        make_gmlp(j)

    mixT = V(R_E, [16, T], BF16)
    af3 = Alloc(R_F, R_END)
    gt = [af3([4, T], BF16) for _ in range(2)]
    m0 = af3([4, T], F32)
    tmp3 = [af3([512], F32) for _ in range(2)]

    def stage3_pre():
        flush_deferred()
        drain_bg()
        fw.barrier()
        if dbg:
            fw.dma("sp", "dbg", dbg_out["d_ogT"], ogT.rearrange("p a b -> p (a b)"), reads=[("ogT", t) for t in range(NT)])
            fw.dma("sp", "dbg", dbg_out["d_omT"], omT.rearrange("p a b -> p (a b)"), reads=[("omT", t) for t in range(NT)])

    hook_before(stage3_pre)

    def make_stage3(j):
        def fm_groups(slot, actT, act_name, evac):
            for ct in range(4):
                for tb in range(2):
                    bank, bres = mm_bank()
                    mm_group(bank, [(ring[slot][:, kc, ct * 128:(ct + 1) * 128], actT[:, kc, tb * 512:(tb + 1) * 512])
                                    for kc in range(KC)], bres,
                             reads=[("ring", slot)] + [(act_name, t) for t in range(4 * tb, 4 * tb + 4)])
                    evac(bank, bres, ct, tb)
                    tick()

        def gate_compute(n):
            def f(slot):
                def evac(bank, bres, ct, tb):
                    fw.op("act", lambda e: e.activation(out=gt[n][:, ct, tb * 512:(tb + 1) * 512], in_=bank, func=AF.Sigmoid,
                                                        bias=bgate[:, n, 4 * j + ct:4 * j + ct + 1]),
                          reads=[bres, "bgate"], writes=[("gt", n, ct, tb)])
                fm_groups(slot, xnT, "xnT", evac)
            return f

        def br0_compute(slot):
            def evac(bank, bres, ct, tb):
                fw.op("dve", lambda e: e.tensor_tensor(out=m0[:, ct, tb * 512:(tb + 1) * 512], in0=bank,
                                                       in1=gt[0][:, ct, tb * 512:(tb + 1) * 512], op=ALU.mult),
                      reads=[bres, ("gt", 0, ct, tb)], writes=[("m0", ct, tb)])
            fm_groups(slot, ogT, "ogT", evac)

        def br1_compute(slot):
            def evac(bank, bres, ct, tb):
                k = (ct * 2 + tb) % 2
                fw.op("dve", lambda e: e.tensor_tensor(out=tmp3[k], in0=bank, in1=gt[1][:, ct, tb * 512:(tb + 1) * 512],
                                                       op=ALU.mult),
                      reads=[bres, ("gt", 1, ct, tb)], writes=[("tmp3", k)])
                fw.op("dve", lambda e: e.tensor_tensor(out=mixT[:, 4 * j + ct, tb * 512:(tb + 1) * 512], in0=tmp3[k],
                                                       in1=m0[:, ct, tb * 512:(tb + 1) * 512], op=ALU.add),
                      reads=[("tmp3", k), ("m0", ct, tb)], writes=[("mixT", tb)])
            fm_groups(slot, omT, "omT", evac)

        add_block([(0, w_in_v[:, :, C_G0 + 512 * j:C_G0 + 512 * j + 512], 512)], gate_compute(0))
        add_block([(0, w_br_v[0][:, :, 512 * j:512 * j + 512], 512)], br0_compute)
        add_block([(0, w_in_v[:, :, C_G1 + 512 * j:C_G1 + 512 * j + 512], 512)], gate_compute(1))
        add_block([(0, w_br_v[1][:, :, 512 * j:512 * j + 512], 512)], br1_compute)

    for j in range(4):
        make_stage3(j)

    hbuf = V(R_A, [NT, D], F32)
    hnT = V(R_C, [16, T], BF16)

    def stage4_pre():
        fw.barrier()
        xt[0], xt[1] = xt5[0], xt5[1]
        if dbg:
            fw.dma("sp", "dbg", dbg_out["d_mixT"], mixT.rearrange("p a b -> p (a b)"), reads=[("mixT", tb) for tb in range(2)])
        for t in range(NT):
            fw.dma("sp", f"hx{t}", hbuf[:, t, :], x_ext[t * 128:(t + 1) * 128, :],
                   writes=[("h", t)])

    hook_before(stage4_pre)

    def make_wout(j):
        def compute(slot):
            for t in range(NT):
                bank, bres = mm_bank()
                mm_group(bank, [(mixT[:, kc, t * 128:(t + 1) * 128], ring[slot][:, kc, :]) for kc in range(KC)], bres,
                         reads=[("mixT", t // 4), ("ring", slot)])
                hs = hbuf[:, t, 512 * j:512 * j + 512]
                fw.op("dve", lambda e: e.tensor_tensor(out=hs, in0=hs, in1=bank, op=ALU.add),
                      reads=[bres, ("h", t)], writes=[("h", t)])
                if j == 3:
                    defer_norm_tile(t, None, wcol_mlp, "wcol_mlp", hnT, "hnT", src_sb=lambda tt: hbuf[:, tt, :],
                                    src_res=lambda tt: ("h", tt))
                tick()
        add_block([(0, w_out_v[:, :, 512 * j:512 * j + 512], 512)], compute)

    for j in range(4):
        make_wout(j)

    aT = V(R_E, [16, T], BF16)
    af5 = Alloc(R_F, R_END)
    xt5 = [af5([D], F32) for _ in range(2)]
    rtmp = [af5([512], F32) for _ in range(2)]
    wbc = af5([D], F32)
    junk6 = af5([D], BF16)

    def stage5_pre():
        flush_deferred()
        fw.dma("sp", "c11", wbc, norm_final_w.partition_broadcast(128), writes=["wbc"])
        if dbg:
            fw.dma("sp", "dbg", dbg_out["d_hnT"], hnT.rearrange("p a b -> p (a b)"), reads=[("hnT", t) for t in range(NT)])
            fw.dma("sp", "dbg", dbg_out["d_h1"], hbuf.rearrange("p a b -> p (a b)"), reads=[("h", t) for t in range(NT)])

    hook_before(stage5_pre)

    def final_tile(t):
        stv, sres_st = st_slot()
        fw.op("dve", lambda e: e.tensor_reduce(out=stv[:, 0:1], in_=ssq[:, 4 * t:4 * t + 4], axis=mybir.AxisListType.X, op=ALU.add),
              reads=[("ssq", t, j) for j in range(4)], writes=[sres_st])
        rstd_op(stv[:, 0:1], stv[:, 1:2], stv[:, 2:3], [sres_st], 1.0 / D)
        fw.op("dve", lambda e: e.scalar_tensor_tensor(out=hbuf[:, t, 0:1024], in0=hbuf[:, t, 0:1024], scalar=stv[:, 2:3],
                                                      in1=wbc[:, 0:1024], op0=ALU.mult, op1=ALU.mult),
              reads=[("h", t), sres_st, "wbc"], writes=[("hA", t)])
        fw.op("pool", lambda e: e.tensor_scalar(out=hbuf[:, t, 1024:2048], in0=hbuf[:, t, 1024:2048], scalar1=stv[:, 2:3],
                                                scalar2=1.0, op0=ALU.mult, op1=ALU.mult),
              reads=[("h", t), sres_st], writes=[("hB", t)])
        fw.op("pool", lambda e: e.tensor_tensor(out=hbuf[:, t, 1024:2048], in0=hbuf[:, t, 1024:2048], in1=wbc[:, 1024:2048],
                                                op=ALU.mult),
              reads=[("hB", t), "wbc"], writes=[("hB", t)])
        fw.dma("sp", f"yo{t}", y[t * 128:(t + 1) * 128, :], hbuf[:, t, :], reads=[("h", t), ("hA", t), ("hB", t)], writes=[("y", t)])

    def make_ffn(i):
        def up_compute(u):
            def f(slot):
                for ct in range(4):
                    for tb in range(2):
                        bank, bres = mm_bank()
                        mm_group(bank, [(ring[slot][:, kc, ct * 128:(ct + 1) * 128], hnT[:, kc, tb * 512:(tb + 1) * 512])
                                        for kc in range(KC)], bres,
                                 reads=[("ring", slot)] + [("hnT", t) for t in range(4 * tb, 4 * tb + 4)])
                        k = (ct * 2 + tb) % 2
                        fw.op("act", lambda e: e.activation(out=rtmp[k], in_=bank, func=AF.Relu), reads=[bres], writes=[("rtmp", k)])
                        fw.op("dve", lambda e: e.tensor_tensor(out=aT[:, 4 * u + ct, tb * 512:(tb + 1) * 512], in0=rtmp[k],
                                                               in1=rtmp[k], op=ALU.mult),
                              reads=[("rtmp", k)], writes=[("aT", tb)])
                        tick()
            return f

        def down_compute(j):
            def f(slot):
                for t in range(NT):
                    bank, bres = mm_bank()
                    mm_group(bank, [(aT[:, kc, t * 128:(t + 1) * 128], ring[slot][:, kc, :]) for kc in range(KC)], bres,
                             reads=[("aT", t // 4), ("ring", slot)])
                    hs = hbuf[:, t, 512 * j:512 * j + 512]
                    fw.op("dve", lambda e: e.tensor_tensor(out=hs, in0=hs, in1=bank, op=ALU.add),
                          reads=[bres, ("h", t)], writes=[("h", t)])
                    if i == 3:
                        fw.op("act", lambda e: e.activation(out=junk6[:, 0:512], in_=hs, func=AF.Square,
                                                            accum_out=ssq[:, 4 * t + j:4 * t + j + 1]),
                              reads=[("h", t)], writes=["junk6", ("ssq", t, j)])
                    if i == 3 and j == 3:
                        defer(lambda t=t: final_tile(t), 1)
                    tick()
            return f

        for u in range(4):
            c0 = 2048 * i + 512 * u
            add_block([(0, w_up_v[:, :, c0:c0 + 512], 512)], up_compute(u))
        for j in range(4):
            add_block([(0, w_dn_v[i][:, :, 512 * j:512 * j + 512], 512)], down_compute(j))

    for i in range(4):
        make_ffn(i)

    n = len(blocks)
    if max_blocks is not None:
        n = min(n, max_blocks)
    issue_load(0)
    for i in range(n):
        if i in stage_hooks:
            stage_hooks[i]()
        if i + 1 < n:
            issue_load(i + 1)
        blocks[i][1](i % 2)

    drain_bg()
    flush_deferred()
    fw.wait_all_dma("sp", "yo")
    fw.wait_all_dma("sp", "dbg")
    return nc


_CACHE = {}


def _prep_inputs(inputs):
    x = np.ascontiguousarray(inputs["x"], dtype=np.float32)
    B, Sq, _ = x.shape
    per_seq = Sq // T
    shared = {
        "norm_mix_w": inputs["norm_mix_w"].reshape(D),
        "w_in": inputs["w_in"].reshape(D, DIN),
        "w_alpha_up": inputs["w_alpha_up"].reshape(16, 1024),
        "b_alpha": inputs["b_alpha"].reshape(1, 1024),
        "gla_norm_w": inputs["gla_norm_w"].reshape(512),
        "gmlp_ln_w": inputs["gmlp_ln_w"].reshape(256),
        "gmlp_ln_b": inputs["gmlp_ln_b"].reshape(256),
        "w_spatial": inputs["w_spatial"].reshape(8, 128, 128),
        "b_spatial": inputs["b_spatial"].reshape(8, 128),
        "b_gate": inputs["b_gate"].reshape(2, D),
        "w_branch": inputs["w_branch"].reshape(2, D, D),
        "w_out": inputs["w_out"].reshape(D, D),
        "norm_mlp_w": inputs["norm_mlp_w"].reshape(D),
        "w_ff_up": inputs["w_ff_up"].reshape(D, DFF),
        "w_ff_down": inputs["w_ff_down"].reshape(DFF, D),
        "norm_final_w": inputs["norm_final_w"].reshape(D),
    }
    shared = {k: np.ascontiguousarray(v, dtype=np.float32) for k, v in shared.items()}
    in_maps = []
    for c in range(8):
        b, j = c // per_seq, c % per_seq
        m = dict(shared)
        m["x_ext"] = np.ascontiguousarray(x[b, j * T:(j + 1) * T])
        cm = np.zeros((128, 4), np.float32)
        cm[:, :j] = 1.0
        m["cmask"] = cm
        in_maps.append(m)
    return in_maps, B, Sq


def kernel(**inputs):
    in_maps, B, Sq = _prep_inputs(inputs)
    if "nc" not in _CACHE:
        _CACHE["nc"] = build_program()
    res = run_bass_kernel_spmd(_CACHE["nc"], in_maps, core_ids=list(range(8)))
    per_seq = Sq // T
    out = np.empty((B, Sq, D), np.float32)
    for c in range(8):
        b, j = c // per_seq, c % per_seq
        out[b, j * T:(j + 1) * T] = res.results[c]["y"]
    return out
```
